# Optimizing a Trainium2 kernel written in Bass

```python
import math
import jax, jax.numpy as jnp
from jax import lax
import numpy as np

D_MODEL = 1024
BATCH = 32
SEQ = 256
DEPTH = 4
DEC_BATCH = 2
DEC_SEQ = 2048
PAST_LEN = 256

GRID_W = 64
N_MIXERS = 3
N_CONV_LAYERS = (DEPTH + 2) // 3
N_NA_LAYERS = (DEPTH + 1) // 3
N_SSD_LAYERS = DEPTH // 3
RMS_EPS = 1e-6
LN_EPS = 1e-5
CONF_WIDTH = D_MODEL
CONF_K = 31
NA_HEADS = 16
NA_HEAD_DIM = D_MODEL // NA_HEADS
NA_WIDTH = NA_HEADS * NA_HEAD_DIM
NA_KH_MAX = 8
NA_KW = 16
NA_QB = 16
NA_KB = NA_QB + NA_KW
ATTN_BLOCK = 128
NEG_INF = -1e30
SSD_INNER = 2 * D_MODEL
SSD_HEAD_DIM = 64
SSD_HEADS = SSD_INNER // SSD_HEAD_DIM
SSD_STATE = 128
SSD_GROUPS = 4
SSD_GN = SSD_GROUPS * SSD_STATE
SSD_CONV = 7
SSD_CHUNK = 128

kernel_name = "hybrid_diffusion_trunk_step"

F32 = jnp.float32


def rms_norm(x, w):
    xf = x.astype(F32)
    y = xf * lax.rsqrt(jnp.mean(xf * xf, axis=-1, keepdims=True) + RMS_EPS)
    return (y * w.astype(F32)).astype(x.dtype)


def layer_norm(x, w, b):
    xf = x.astype(F32)
    mu = jnp.mean(xf, axis=-1, keepdims=True)
    xc = xf - mu
    var = jnp.mean(xc * xc, axis=-1, keepdims=True)
    return (xc * lax.rsqrt(var + LN_EPS) * w.astype(F32) + b.astype(F32)).astype(x.dtype)


def depthwise_conv(x, w, b):
    k = w.shape[0]
    pad = k // 2
    y = lax.conv_general_dilated(x, w[:, None, :].astype(x.dtype), window_strides=(1,),
                                 padding=[(pad, k - 1 - pad)],
                                 dimension_numbers=('NWC', 'WIO', 'NWC'),
                                 feature_group_count=x.shape[-1])
    return y + b


def modulated_norm(x, cond, ada_w, ada_b, norm_w):
    mod = jax.nn.silu(cond) @ ada_w + ada_b
    shift, scale, gate = jnp.split(mod[:, None, :], 3, axis=-1)
    h = rms_norm(x, norm_w) * (1 + scale) + shift
    return h, gate


def conformer_conv_mixer(h, w_in, b_in, w_dw, b_dw, ln_w, ln_b, w_out, b_out):
    v, g, z = jnp.split(h @ w_in + b_in, 3, axis=-1)
    u = v * jax.nn.sigmoid(g)
    u = depthwise_conv(u, w_dw, b_dw)
    u = jax.nn.silu(layer_norm(u, ln_w, ln_b))
    return (u * jax.nn.silu(z)) @ w_out + b_out


def na_project(h, w_in):
    B, L, _ = h.shape
    q, k, v, z = jnp.split(h @ w_in, 4, axis=-1)
    shp = (B, L, NA_HEADS, NA_HEAD_DIM)
    return q.reshape(shp), k.reshape(shp), v.reshape(shp), z


def context_attention(q, k, v):
    B, S, H, Dh = q.shape
    scale = Dh ** -0.5
    qb = q.reshape(B, S // ATTN_BLOCK, ATTN_BLOCK, H, Dh).transpose(1, 0, 2, 3, 4)

    def block(qi):
        s = jnp.einsum('bqhd,bkhd->bhqk', qi, k).astype(F32) * scale
        p = jax.nn.softmax(s, axis=-1).astype(v.dtype)
        return jnp.einsum('bhqk,bkhd->bqhd', p, v)

    o = lax.map(block, qb)
    return o.transpose(1, 0, 2, 3, 4).reshape(B, S, H, Dh)


def na_block_layout(rows):
    kh = min(NA_KH_MAX, rows)
    r = np.arange(rows)
    row_start = np.clip(r - kh // 2, 0, rows - kh)
    c = np.arange(GRID_W)
    col_start = np.clip(c - NA_KW // 2, 0, GRID_W - NA_KW)
    n_cb = GRID_W // NA_QB
    cb0 = np.arange(n_cb) * NA_QB
    kcol0 = np.clip(cb0 - NA_KW // 2, 0, GRID_W - NA_KB)
    key_row = row_start[:, None] + np.arange(kh)[None, :]
    key_col = kcol0[:, None] + np.arange(NA_KB)[None, :]
    q_col = cb0[:, None] + np.arange(NA_QB)[None, :]
    kidx = key_row[:, None, :, None] * GRID_W + key_col[None, :, None, :]
    kidx = kidx.reshape(rows, n_cb, kh * NA_KB)
    dr = key_row - r[:, None]
    dc = key_col[:, None, :] - q_col[:, :, None]
    dc_idx = np.clip(dc, -(NA_KW - 1), NA_KW - 1) + NA_KW - 1
    bidx = (dr[:, None, None, :, None] + NA_KH_MAX - 1) * (2 * NA_KW - 1) + dc_idx[None, :, :, None, :]
    bidx = bidx.reshape(rows, n_cb, NA_QB, kh * NA_KB)
    qcs = col_start[q_col][..., None]
    valid = (key_col[:, None, :] >= qcs) & (key_col[:, None, :] < qcs + NA_KW)
    valid = np.broadcast_to(valid[:, :, None, :], (n_cb, NA_QB, kh, NA_KB)).reshape(n_cb, NA_QB, kh * NA_KB)
    return kidx.astype(np.int32), bidx.astype(np.int32), valid


def neighbourhood_attention(q, k, v, k_ctx, v_ctx, rpb):
    B, T, H, Dh = q.shape
    rows = T // GRID_W
    n_cb = GRID_W // NA_QB
    kidx, bidx, valid = na_block_layout(rows)
    mask = jnp.asarray(valid)
    scale = Dh ** -0.5
    rpb_flat = rpb.reshape(H, -1)
    q_rows = q.reshape(B, rows, n_cb, NA_QB, H, Dh).transpose(1, 0, 2, 3, 4, 5)

    def row_block(args):
        qr, kidx_r, bidx_r = args
        kr = k[:, kidx_r]
        vr = v[:, kidx_r]
        s_loc = jnp.einsum('bnqhd,bnkhd->bhnqk', qr, kr).astype(F32) * scale
        s_loc = s_loc + rpb_flat[:, bidx_r].astype(F32)[None]
        s_loc = jnp.where(mask[None, None], s_loc, NEG_INF)
        s_ctx = jnp.einsum('bnqhd,bchd->bhnqc', qr, k_ctx).astype(F32) * scale
        n_loc = s_loc.shape[-1]
        p = jax.nn.softmax(jnp.concatenate([s_loc, s_ctx], axis=-1), axis=-1).astype(v.dtype)
        return (jnp.einsum('bhnqk,bnkhd->bnqhd', p[..., :n_loc], vr)
                + jnp.einsum('bhnqc,bchd->bnqhd', p[..., n_loc:], v_ctx))

    o = lax.map(row_block, (q_rows, jnp.asarray(kidx), jnp.asarray(bidx)))
    return o.transpose(1, 0, 2, 3, 4, 5).reshape(B, T, H, Dh)


def na_context_mixer(h, w_in, w_out):
    B, L, _ = h.shape
    q, k, v, z = na_project(h, w_in)
    o = context_attention(q, k, v)
    return (o.reshape(B, L, NA_WIDTH) * jax.nn.silu(z)) @ w_out, k, v


def na_latent_mixer(h, k_ctx, v_ctx, w_in, rpb, w_out):
    B, T, _ = h.shape
    q, k, v, z = na_project(h, w_in)
    o = neighbourhood_attention(q, k, v, k_ctx, v_ctx, rpb)
    return (o.reshape(B, T, NA_WIDTH) * jax.nn.silu(z)) @ w_out


def ssd_chunked(x, dt, A, Bm, Cm, h0):
    Bsz, L, H, P = x.shape
    N = Bm.shape[-1]
    Q = SSD_CHUNK
    nc = L // Q
    a_cum = jnp.cumsum((dt * A).reshape(Bsz, nc, Q, H), axis=2)
    xdt = (x.astype(F32) * dt[..., None]).reshape(Bsz, nc, Q, H, P)
    Bc = Bm.astype(F32).reshape(Bsz, nc, Q, H, N)
    Cc = Cm.astype(F32).reshape(Bsz, nc, Q, H, N)
    tri = np.tril(np.ones((Q, Q), dtype=bool))[None, None, :, :, None]
    seg = a_cum[:, :, :, None, :] - a_cum[:, :, None, :, :]
    decay = jnp.exp(jnp.where(tri, seg, -jnp.inf))
    scores = jnp.einsum('bcihn,bcjhn->bcijh', Cc, Bc) * decay
    y_diag = jnp.einsum('bcijh,bcjhp->bcihp', scores, xdt)
    decay_to_end = jnp.exp(a_cum[:, :, -1:, :] - a_cum)
    states = jnp.einsum('bcjhn,bcjh,bcjhp->bchpn', Bc, decay_to_end, xdt)
    chunk_decay = jnp.exp(a_cum[:, :, -1, :])

    def step(hs, inp):
        st, dec = inp
        return hs * dec[:, :, None, None] + st, hs

    h_final, h_in = lax.scan(step, h0.astype(F32),
                             (states.transpose(1, 0, 2, 3, 4), chunk_decay.transpose(1, 0, 2)))
    h_in = h_in.transpose(1, 0, 2, 3, 4)
    y_off = jnp.einsum('bcihn,bchpn,bcih->bcihp', Cc, h_in, jnp.exp(a_cum))
    return (y_diag + y_off).reshape(Bsz, L, H, P), h_final


def ssd_mixer(h, h0, w_in, w_conv, b_conv, a_log, dt_bias, d_skip, norm_w, w_out):
    B, L, _ = h.shape
    H, P, N, G = SSD_HEADS, SSD_HEAD_DIM, SSD_STATE, SSD_GROUPS
    z, xbc, dt_raw = jnp.split(h @ w_in, [SSD_INNER, 2 * SSD_INNER + 2 * SSD_GN], axis=-1)
    xbc = jax.nn.silu(depthwise_conv(xbc, w_conv, b_conv))
    x, Bm, Cm = jnp.split(xbc, [SSD_INNER, SSD_INNER + SSD_GN], axis=-1)
    x = x.reshape(B, L, H, P)
    Bm = jnp.repeat(Bm.reshape(B, L, G, N), H // G, axis=2)
    Cm = jnp.repeat(Cm.reshape(B, L, G, N), H // G, axis=2)
    dt = jax.nn.softplus(dt_raw.reshape(B, L, 2, H).astype(F32) + dt_bias.astype(F32))
    A = -jnp.exp(a_log.astype(F32))
    flip = lambda t: jnp.flip(t, axis=1)
    y_f, s_f = ssd_chunked(x, dt[:, :, 0], A[0], Bm, Cm, h0[:, 0])
    y_b, s_b = ssd_chunked(flip(x), flip(dt[:, :, 1]), A[1], flip(Bm), flip(Cm), h0[:, 1])
    y = y_f + flip(y_b) + x.astype(F32) * d_skip.astype(F32)[:, None]
    y = y.reshape(B, L, SSD_INNER).astype(h.dtype)
    y = rms_norm(y * jax.nn.silu(z), norm_w)
    return y @ w_out, jnp.stack([s_f, s_b], axis=1).astype(h.dtype)


def setup_inputs(seed: int = 0) -> dict:
    key = jax.random.key(seed)
    ks = iter(jax.random.split(key, 48))

    def nrm(shape, s=1.0):
        return jax.random.normal(next(ks), shape, F32) * s

    D = D_MODEL
    ssd_in_cols = 2 * SSD_INNER + 2 * SSD_GN + 2 * SSD_HEADS
    dt0 = jnp.exp(jax.random.uniform(next(ks), (N_SSD_LAYERS, 2, SSD_HEADS), F32,
                                     minval=math.log(1e-3), maxval=math.log(1e-1)))
    dt_bias = dt0 + jnp.log(-jnp.expm1(-dt0))
    a_log = jnp.log(jax.random.uniform(next(ks), (N_SSD_LAYERS, 2, SSD_HEADS), F32, minval=1.0, maxval=16.0))
    return {
        'x_prompt': nrm((BATCH, SEQ, D)),
        'x_sample': nrm((DEC_BATCH, DEC_SEQ, D)),
        'cache_k': nrm((DEC_BATCH, N_NA_LAYERS, PAST_LEN, NA_HEADS, NA_HEAD_DIM)),
        'cache_v': nrm((DEC_BATCH, N_NA_LAYERS, PAST_LEN, NA_HEADS, NA_HEAD_DIM)),
        'state_ssm': nrm((DEC_BATCH, N_SSD_LAYERS, 2, SSD_HEADS, SSD_HEAD_DIM, SSD_STATE), 0.5),
        'c': nrm((DEC_BATCH, D)),
        'c_ctx': nrm((D,)),
        'ada_w': nrm((DEPTH, D, 3 * D), D ** -0.5),
        'ada_b': nrm((DEPTH, 3 * D), 0.1),
        'norm_w': 1.0 + nrm((DEPTH, D), 0.05),
        'final_norm_w': 1.0 + nrm((D,), 0.05),
        'conv_w_in': nrm((N_CONV_LAYERS, D, 3 * CONF_WIDTH), D ** -0.5),
        'conv_b_in': nrm((N_CONV_LAYERS, 3 * CONF_WIDTH), 0.02),
        'conv_w_dw': nrm((N_CONV_LAYERS, CONF_K, CONF_WIDTH), CONF_K ** -0.5),
        'conv_b_dw': nrm((N_CONV_LAYERS, CONF_WIDTH), 0.02),
        'conv_ln_w': 1.0 + nrm((N_CONV_LAYERS, CONF_WIDTH), 0.05),
        'conv_ln_b': nrm((N_CONV_LAYERS, CONF_WIDTH), 0.02),
        'conv_w_out': nrm((N_CONV_LAYERS, CONF_WIDTH, D), CONF_WIDTH ** -0.5),
        'conv_b_out': nrm((N_CONV_LAYERS, D), 0.02),
        'na_w_in': nrm((N_NA_LAYERS, D, 4 * NA_WIDTH), D ** -0.5),
        'na_rpb': nrm((N_NA_LAYERS, NA_HEADS, 2 * NA_KH_MAX - 1, 2 * NA_KW - 1), 0.1),
        'na_w_out': nrm((N_NA_LAYERS, NA_WIDTH, D), NA_WIDTH ** -0.5),
        'ssd_w_in': nrm((N_SSD_LAYERS, D, ssd_in_cols), D ** -0.5),
        'ssd_w_conv': nrm((N_SSD_LAYERS, SSD_CONV, SSD_INNER + 2 * SSD_GN), SSD_CONV ** -0.5),
        'ssd_b_conv': nrm((N_SSD_LAYERS, SSD_INNER + 2 * SSD_GN), 0.02),
        'ssd_a_log': a_log,
        'ssd_dt_bias': dt_bias,
        'ssd_d': 1.0 + nrm((N_SSD_LAYERS, SSD_HEADS), 0.1),
        'ssd_norm_w': 1.0 + nrm((N_SSD_LAYERS, SSD_INNER), 0.05),
        'ssd_w_out': nrm((N_SSD_LAYERS, SSD_INNER, D), SSD_INNER ** -0.5),
    }


def reference(x_prompt, x_sample, cache_k, cache_v, state_ssm, c, c_ctx,
              ada_w, ada_b, norm_w, final_norm_w,
              conv_w_in, conv_b_in, conv_w_dw, conv_b_dw, conv_ln_w, conv_ln_b, conv_w_out, conv_b_out,
              na_w_in, na_rpb, na_w_out,
              ssd_w_in, ssd_w_conv, ssd_b_conv, ssd_a_log, ssd_dt_bias, ssd_d, ssd_norm_w, ssd_w_out):
    ctx = x_prompt
    lat = x_sample
    cond_ctx = c_ctx[None, :]
    new_k, new_v, new_s = [], [], []
    for i in range(DEPTH):
        kind = i % N_MIXERS
        j = i // N_MIXERS
        h_ctx, g_ctx = modulated_norm(ctx, cond_ctx, ada_w[i], ada_b[i], norm_w[i])
        h_lat, g_lat = modulated_norm(lat, c, ada_w[i], ada_b[i], norm_w[i])
        if kind == 0:
            p = (conv_w_in[j], conv_b_in[j], conv_w_dw[j], conv_b_dw[j],
                 conv_ln_w[j], conv_ln_b[j], conv_w_out[j], conv_b_out[j])
            y_ctx = conformer_conv_mixer(h_ctx, *p)
            y_lat = conformer_conv_mixer(h_lat, *p)
        elif kind == 1:
            y_ctx, k_c, v_c = na_context_mixer(h_ctx, na_w_in[j], na_w_out[j])
            new_k.append(k_c)
            new_v.append(v_c)
            y_lat = na_latent_mixer(h_lat, cache_k[:, j], cache_v[:, j], na_w_in[j], na_rpb[j], na_w_out[j])
        else:
            p = (ssd_w_in[j], ssd_w_conv[j], ssd_b_conv[j], ssd_a_log[j], ssd_dt_bias[j],
                 ssd_d[j], ssd_norm_w[j], ssd_w_out[j])
            h0 = jnp.zeros((ctx.shape[0], 2, SSD_HEADS, SSD_HEAD_DIM, SSD_STATE), ctx.dtype)
            y_ctx, s_c = ssd_mixer(h_ctx, h0, *p)
            new_s.append(s_c)
            y_lat, _ = ssd_mixer(h_lat, state_ssm[:, j], *p)
        ctx = ctx + g_ctx * y_ctx
        lat = lat + g_lat * y_lat
    y_prompt = rms_norm(ctx, final_norm_w)
    y_sample = rms_norm(lat, final_norm_w)
    new_cache_k = jnp.stack(new_k, axis=1)
    new_cache_v = jnp.stack(new_v, axis=1)
    new_state_ssm = jnp.stack(new_s, axis=1)
    return (y_prompt, y_sample, new_cache_k, new_cache_v, new_state_ssm)
```

```python
import numpy as np
import concourse.bass as bass
import concourse.mybir as mybir
from concourse.bass_utils import run_bass_kernel_spmd
from contextlib import ExitStack

F32 = mybir.dt.float32
BF16 = mybir.dt.bfloat16
AF = mybir.ActivationFunctionType
ALU = mybir.AluOpType
AX = mybir.AxisListType

ENGS = ['pe', 'act', 'dve', 'pool', 'sp']
CAP = 30000
SAME_ENG_SYNC = True
NSLOT = 24


class Buf:
    __slots__ = ('name', 'lw', 'rd', 'dw', 'dr')

    def __init__(self, name):
        self.name = name
        self.lw = None
        self.rd = {}
        self.dw = {}
        self.dr = {}


class Prog:
    def __init__(self, nc, es):
        self.nc = nc
        self.es = es
        self.q = {e: [] for e in ENGS}
        self.seq = {e: 0 for e in ENGS}
        self.seen = {e: {} for e in ENGS}
        self.esems = {e: [] for e in ENGS}
        self.dslots = {e: [] for e in ENGS}
        self.dcount = {e: 0 for e in ENGS}
        self.ninst = 0

    def newsem(self, name):
        return self.es.enter_context(self.nc.semaphore(name))

    def esem(self, e, ep):
        while len(self.esems[e]) <= ep:
            self.esems[e].append(self.newsem("s_%s_%d" % (e, len(self.esems[e]))))
        return self.esems[e][ep]

    def _w_eng(self, e, waits, src, s):
        if src == e and (e == 'pe' or not SAME_ENG_SYNC):
            return
        ep = (s - 1) // CAP
        val = (s - 1) % CAP + 1
        key = ('e', src, ep)
        if self.seen[e].get(key, 0) >= val:
            return
        self.seen[e][key] = val
        waits[key] = (self.esem(src, ep), val)

    def _w_sem(self, e, waits, sem, val):
        key = ('d', sem.num)
        if self.seen[e].get(key, 0) >= val:
            return
        self.seen[e][key] = val
        waits[key] = (sem, val)

    def _w_evs(self, e, waits, evs):
        for sem, val in evs.values():
            self._w_sem(e, waits, sem, val)

    def op(self, e, fname, *args, r=(), w=(), ww=(), inc=True, **kw):
        fn = (fname, args, kw)
        assert inc or e == 'pe'
        weak = set(id(b) for b in ww)
        w = list(w) + list(ww)
        waits = {}
        for b in r:
            if b.lw:
                self._w_eng(e, waits, *b.lw)
            self._w_evs(e, waits, b.dw)
        for b in w:
            if b.lw and not (id(b) in weak and b.lw[0] == e):
                self._w_eng(e, waits, *b.lw)
            for src, s in b.rd.items():
                self._w_eng(e, waits, src, s)
            self._w_evs(e, waits, b.dw)
            self._w_evs(e, waits, b.dr)
        if inc:
            self.seq[e] += 1
            s = self.seq[e]
        else:
            s = self.seq[e] + 1
        self.esem(e, (s - 1) // CAP)
        self.q[e].append((list(waits.values()), fn, None, s if inc else -1))
        for b in r:
            if b.rd.get(e, 0) < s:
                b.rd[e] = s
        for b in w:
            b.lw = (e, s)
            b.rd = {}
            b.dw = {}
            b.dr = {}
        self.ninst += 1

    def dma(self, e, fname, *args, r=(), w=(), **kw):
        fn = (fname, args, kw)
        if e == 'auto':
            e = r[0].lw[0] if (r and r[0].lw) else 'pool'
            if e == 'pe':
                e = 'pool'
            if e == 'dve':
                e = 'sp'
        waits = {}
        for b in r:
            if b.lw:
                self._w_eng(e, waits, *b.lw)
            self._w_evs(e, waits, b.dw)
        for b in w:
            if b.lw:
                self._w_eng(e, waits, *b.lw)
            for src, s in b.rd.items():
                self._w_eng(e, waits, src, s)
            self._w_evs(e, waits, b.dr)
        i = self.dcount[e]
        self.dcount[e] += 1
        slot = i % NSLOT
        while len(self.dslots[e]) <= slot:
            self.dslots[e].append(self.newsem("d_%s_%d" % (e, len(self.dslots[e]))))
        sem = self.dslots[e][slot]
        val = 16 * (i // NSLOT + 1)
        if val > 16:
            self._w_sem(e, waits, sem, val - 16)
        self.q[e].append((list(waits.values()), fn, sem, 16))
        for b in r:
            b.dr[sem.num] = (sem, val)
        for b in w:
            b.dw[sem.num] = (sem, val)
        self.ninst += 1

    def barrier(self):
        for e in ENGS:
            waits = {}
            for src in ENGS:
                if self.seq[src] > 0 and not (src == e and e == 'pe'):
                    ep = (self.seq[src] - 1) // CAP
                    val = (self.seq[src] - 1) % CAP + 1
                    key = ('e', src, ep)
                    if self.seen[e].get(key, 0) < val:
                        self.seen[e][key] = val
                        waits[key] = (self.esem(src, ep), val)
            for q in ENGS:
                n = self.dcount[q]
                for slot, sem in enumerate(self.dslots[q]):
                    cnt = (n - slot + NSLOT - 1) // NSLOT
                    if cnt > 0:
                        self._w_sem(e, waits, sem, 16 * cnt)
            self.q[e].append((list(waits.values()), None, None, 0))

    def finish(self, e='sp'):
        waits = {}
        for q in ENGS:
            n = self.dcount[q]
            for slot, sem in enumerate(self.dslots[q]):
                cnt = (n - slot + NSLOT - 1) // NSLOT
                if cnt > 0:
                    self._w_sem(e, waits, sem, 16 * cnt)
        self.q[e].append((list(waits.values()), None, None, 0))

    def emit(self):
        nc = self.nc
        engmap = {'pe': 'tensor', 'act': 'scalar', 'dve': 'vector', 'pool': 'gpsimd', 'sp': 'sync'}
        with nc.Block() as block:
            for e in ENGS:
                items = self.q[e]
                if not items:
                    continue

                def body(eng, items=items, e=e):
                    for waits, fn, dsem, s in items:
                        for sem, val in waits:
                            eng.wait_ge(sem, val)
                        if fn is None:
                            continue
                        ins = getattr(eng, fn[0])(*fn[1], **fn[2])
                        if dsem is not None:
                            ins.then_inc(dsem, 16)
                        elif s > 0:
                            ins.then_inc(self.esems[e][(s - 1) // CAP], 1)
                getattr(block, engmap[e])(body)


D = 1024
NCH = 8
T = 3072
TT = 256
NTILE = T // TT
NHALF = NTILE // 2
SEQS = [(0, 256), (256, 256), (512, 256), (768, 256), (1024, 2048)]
RMS_EPS = 1e-6
LN_EPS = 1e-5

VROW = {}
_r = 0
def _vr(name, n):
    global _r
    VROW[name] = _r
    _r += n
_vr('c_ctx', 1); _vr('c', 1)
for _l in range(4):
    _vr('ada_b%d' % _l, 3); _vr('norm_w%d' % _l, 1)
_vr('final_norm_w', 1)
for _j in range(2):
    _vr('conv_b_in%d' % _j, 3); _vr('conv_w_dw%d' % _j, 31); _vr('conv_b_dw%d' % _j, 1)
    _vr('conv_ln_w%d' % _j, 1); _vr('conv_ln_b%d' % _j, 1); _vr('conv_b_out%d' % _j, 1)
_vr('ssd_w_conv', 21); _vr('ssd_b_conv', 3); _vr('ssd_norm_w', 2)
NVEC = _r
assert NVEC <= 128


class KB:
    def __init__(self, nc, es):
        self.nc = nc
        self.es = es
        self.P = Prog(nc, es)
        self.psn = 0
        self.rot = list(range(8))
        self.pan = 0
        self.uid = 0

    def sb(self, name, shape, dt):
        t = self.es.enter_context(self.nc.sbuf_tensor(name, list(shape), dt))
        return t, Buf(name)

    def pool(self, name, shape, dt, n):
        return [self.sb("%s%d" % (name, i), shape, dt) for i in range(n)]

    def init_psum(self):
        self.psb = []
        for i in range(8):
            t = self.es.enter_context(self.nc.psum_tensor("ps%d" % i, [128, 512], F32))
            self.psb.append((t, Buf("ps%d" % i)))

    def ps(self):
        r = self.psb[self.rot[self.psn % len(self.rot)]]
        self.psn += 1
        return r

    def set_acc(self, on):
        self.rot = [4, 5, 6, 7] if on else list(range(8))

    def psacc(self):
        r = self.psb[self.pan % 4]
        self.pan += 1
        return r

    def din(self, name, shape):
        return self.nc.dram_tensor(name, list(shape), F32, kind="ExternalInput").ap()

    def dout(self, name, shape):
        return self.nc.dram_tensor(name, list(shape), F32, kind="ExternalOutput").ap()

    def dscr(self, name, shape, dt):
        return self.nc.dram_tensor(name, list(shape), dt, kind="Internal").ap(), Buf(name)


class Rot:
    def __init__(self, items):
        self.items = items
        self.i = 0

    def next(self):
        r = self.items[self.i % len(self.items)]
        self.i += 1
        return r


def build(nlayers=4, dbg=False):
    nc = bass.Bass("TRN2", target_bir_lowering=False)
    with ExitStack() as es:
        K = KB(nc, es)
        P = K.P
        xp = K.din("xp", [1024, D])
        xs = K.din("xs", [2048, D])
        vtab = K.din("vtab", [128, D])
        cst = K.din("cst", [128, 768])
        ada_w = K.din("ada_w", [4, D, 3 * D])
        conv_w_in = K.din("conv_w_in", [2, D, 3 * D])
        conv_w_out = K.din("conv_w_out", [2, D, D])
        na_w_in = K.din("na_w_in", [D, 4 * D])
        na_w_out = K.din("na_w_out", [D, D])
        ck = K.din("ck", [256, D])
        cv = K.din("cv", [256, D])
        btab = K.din("btab", [5, 16, 5, 128, 128])
        ssd_w_in = K.din("ssd_w_in", [D, 5184])
        ssd_w_out = K.din("ssd_w_out", [2 * D, D])
        ssc = K.din("ssc", [128, 160])
        st_in = K.din("st_in", [2 * 2048, 128])
        y_ctx = K.dout("y_ctx", [1024, D])
        y_lat = K.dout("y_lat", [2048, D])
        nk = K.dout("nk", [1024, D])
        nv = K.dout("nv", [1024, D])
        ns = K.dout("ns", [4 * 2 * 32 * 64, 128])
        U_d, U_b = K.dscr("U_d", [D, T], BF16)
        SZ_d, SZ_b = K.dscr("SZ_d", [D, T], BF16)
        C_d, C_b = K.dscr("C_d", [D, T], F32)
        Q_d, Q_b = K.dscr("Q_d", [D, T], BF16)
        K_d, K_b = K.dscr("K_d", [D, T + 256], BF16)
        V_d, V_b = K.dscr("V_d", [T + 256, D], BF16)
        XBC_d, XBC_b = K.dscr("XBC_d", [3 * D, T], BF16)
        XT_d, XT_b = K.dscr("XT_d", [T, 2 * D], BF16)
        BT_d, BT_b = K.dscr("BT_d", [T, 512], BF16)
        BC_d, BC_b = K.dscr("BC_d", [D, T], BF16)
        ZT_d, ZT_b = K.dscr("ZT_d", [T, 2 * D], BF16)
        DT_d, DT_b = K.dscr("DT_d", [T, 128], F32)
        YF_d, YF_b = K.dscr("YF_d", [T, 2 * D], F32)
        GT_d, GT_b = K.dscr("GT_d", [T, 2 * D], F32)

        K.init_psum()
        X, X_b = K.sb("X", [128, NCH, T], F32)
        Xb = [Buf("X_t%d" % i) for i in range(NTILE)]
        arena, _ = K.sb("arena", [128, 24576], BF16)
        H = arena[:, 0:12288].rearrange("p (c t) -> p c t", c=NCH)
        Hb = [Buf("H_t%d" % i) for i in range(NHALF)]
        cstt, cst_b = K.sb("cst_sb", [128, 768], F32)
        Mle, Mge, Mgt, Mlt = cstt[:, 256:384], cstt[:, 384:512], cstt[:, 512:640], cstt[:, 640:768]
        ssct, ssc_b = K.sb("ssc_sb", [128, 160], F32)
        ident = cstt[:, 0:128]
        ones = cstt[:, 128:256]
        identb, identb_b = K.sb("identb", [128, 128], BF16)
        PV, PV_b = K.sb("PV", [128, NCH, 128], F32)
        MOD, MOD_b = K.sb("MOD", [128, 4, 24, 2], F32)
        AMOD, AMOD_b = K.sb("AMOD", [128, 4, NCH, 2], F32)
        wmem = arena[:, 12288:20480]
        wb0, wb1 = Buf("wb0"), Buf("wb1")
        wrot = Rot([(wmem[:, 0:4096].rearrange("p (k n) -> p k n", k=NCH), wb0), (wmem[:, 4096:8192].rearrange("p (k n) -> p k n", k=NCH), wb1)])
        diag, diag_b = arena[:, 20480:20480 + 31 * 128], Buf("diag")
        onesb, onesb_b = K.sb("onesb", [128, 128], BF16)
        bq = Rot(K.pool("bq", [128, TT], BF16, 4))
        ptp = Rot(K.pool("ptp", [128, 128], BF16, 16))
        t512 = Rot(K.pool("t512", [128, TT], F32, 8))
        stat = Rot(K.pool("stat", [128, TT], F32, 3))
        b512 = Rot(K.pool("b512", [128, TT], BF16, 8))
        big = Rot(K.pool("big", [128, NCH, TT], F32, 3))
        bigb = Rot(K.pool("bigb", [128, NCH, TT], BF16, 2))

        def tile_cond(tt):
            return 0 if tt * TT < 1024 else 1

        P.dma('sp', 'dma_start', out=cstt[:], in_=cst[:, :], w=[cst_b])
        vt, vt_b = big.next()
        vt2 = vt[:].rearrange("p c t -> p (c t)")[:, 0:D]
        P.dma('sp', 'dma_start', out=vt2, in_=vtab[:, :], w=[vt_b])
        P.op('dve', 'tensor_copy', out=identb[:], in_=ident, r=[cst_b], w=[identb_b])
        P.op('dve', 'tensor_copy', out=onesb[:], in_=ones, r=[cst_b], w=[onesb_b])
        for c in range(NCH):
            pt, pb = K.ps()
            P.op('pe', 'matmul', pt[:, 0:128], lhsT=vt2[:, c * 128:(c + 1) * 128], rhs=ident,
                                                      start=True, stop=True, r=[vt_b, cst_b], w=[pb])
            P.op('dve', 'tensor_copy', out=PV[:, c, :], in_=pt[:, 0:128], r=[pb], w=[PV_b])

        def pv(name, off=0):
            r = VROW[name] + off
            return PV[:, :, r]

        for tk in range(T // 128):
            src = xp[tk * 128:(tk + 1) * 128, :] if tk < 8 else xs[(tk - 8) * 128:(tk - 7) * 128, :]
            lt, lb = big.next()
            lt2 = lt[:].rearrange("p c t -> p (c t)")[:, 0:D]
            P.dma('sp', 'dma_start', out=lt2, in_=src, w=[lb])
            for half in range(2):
                pt, pb = K.ps()
                for q in range(4):
                    c = half * 4 + q
                    P.op('pe', 'transpose', pt[:, q * 128:(q + 1) * 128],
                                                                               lt2[:, c * 128:(c + 1) * 128], ident,
                         r=[lb, cst_b], w=[pb])
                dst = X[:, half * 4:half * 4 + 4, tk * 128:(tk + 1) * 128]
                srcp = pt[:].rearrange("p (q t) -> p q t", t=128)
                eng = 'act' if half == 0 else 'dve'
                if eng == 'act':
                    P.op('act', 'activation', out=dst, in_=srcp, func=AF.Copy,
                         r=[pb], w=[Xb[tk * 128 // TT]])
                else:
                    P.op('dve', 'tensor_copy', out=dst, in_=srcp,
                         r=[pb], w=[Xb[tk * 128 // TT]])

        scb, scb_b = K.sb("scb", [128, NCH, 2], BF16)
        P.op('act', 'activation', out=scb[:], in_=PV[:, :, 0:2], func=AF.Silu, r=[PV_b], w=[scb_b])
        for l in range(nlayers):
            for blk in range(6):
                wt, wb = wrot.next()
                P.dma('pool', 'dma_start',
                    out=wt[:], in_=ada_w[l, :, blk * 512:(blk + 1) * 512].rearrange("(k p) n -> p k n", p=128),
                    w=[wb])
                pt, pb = K.ps()
                for m in range(4):
                    for k in range(NCH):
                        P.op('pe', 'matmul',
                            pt[:, m * 2:m * 2 + 2], lhsT=wt[:, k, m * 128:(m + 1) * 128], rhs=scb[:, k, :],
                            start=(k == 0), stop=(k == NCH - 1), r=[wb, scb_b], w=[pb], inc=(k == NCH - 1))
                for m in range(4):
                    ch = blk * 4 + m
                    bias = PV[:, ch % 8, VROW['ada_b%d' % l] + ch // 8:VROW['ada_b%d' % l] + ch // 8 + 1]
                    P.op('dve', 'tensor_scalar',
                        out=MOD[:, l, ch, :], in0=pt[:, m * 2:m * 2 + 2], scalar1=bias, scalar2=None, op0=ALU.add,
                        r=[pb, PV_b], w=[MOD_b])
            nw = pv('norm_w%d' % l)
            for j in range(2):
                P.op('dve', 'scalar_tensor_tensor',
                    out=AMOD[:, l, :, j], in0=MOD[:, l, 8:16, j], scalar=1.0, in1=nw, op0=ALU.add, op1=ALU.mult,
                    r=[MOD_b, PV_b], w=[AMOD_b])

        def colsum_rstd(src3, src_b, eps, out_t, out_b, scale=1.0 / D):
            n = src3.shape[2]
            sq, sq_b = big.next()
            P.op('act', 'activation', out=sq[:, :, 0:n], in_=src3, func=AF.Square, r=[src_b], w=[sq_b])
            pt, pb = K.ps()
            for k in range(NCH):
                P.op('pe', 'matmul', pt[:, 0:n], lhsT=ones, rhs=sq[:, k, 0:n], start=(k == 0),
                                                   stop=(k == NCH - 1), r=[sq_b, cst_b], w=[pb], inc=(k == NCH - 1))
            P.op('act', 'activation', out=out_t[:, 0:n], in_=pt[:, 0:n], func=AF.Sqrt, scale=scale, bias=eps,
                 r=[pb], w=[out_b])
            P.op('dve', 'reciprocal', out=out_t[:, 0:n], in_=out_t[:, 0:n], r=[out_b], w=[out_b])

        def modnorm(l, half):
            for tt in range(half * NHALF, (half + 1) * NHALF):
                j = tile_cond(tt)
                sl = slice(tt * TT, (tt + 1) * TT)
                rs, rs_b = stat.next()
                colsum_rstd(X[:, :, sl], Xb[tt], RMS_EPS, rs, rs_b)
                for k in range(NCH):
                    tmp, tmp_b = t512.next()
                    P.op('dve', 'scalar_tensor_tensor',
                        out=tmp[:], in0=X[:, k, sl], scalar=AMOD[:, l, k, j:j + 1], in1=rs[:], op0=ALU.mult,
                        op1=ALU.mult, r=[Xb[tt], AMOD_b, rs_b], w=[tmp_b])
                    P.op('act', 'activation',
                        out=H[:, k, (tt - half * NHALF) * TT:(tt - half * NHALF + 1) * TT], in_=tmp[:], func=AF.Identity,
                        bias=MOD[:, l, k, j:j + 1], scale=1.0, r=[tmp_b, MOD_b], ww=[Hb[tt - half * NHALF]])

        def load_w(wsrc, cols):
            wt, wb = wrot.next()
            off = 0
            for c0, n in cols:
                P.dma('pool', 'dma_start',
                    out=wt[:, :, off:off + n], in_=wsrc[:, c0:c0 + n].rearrange("(k p) n -> p k n", p=128), w=[wb])
                off += n
            return wt, wb

        def residual_update(l, pt, pb, m, tt, bias_ap, n0=0, n=256):
            j = tile_cond(tt)
            sl = slice(tt * TT + n0, tt * TT + n0 + n)
            tmp, tmp_b = t512.next()
            if bias_ap is not None:
                P.op('dve', 'tensor_scalar', out=tmp[:, 0:n], in0=pt[:, 0:n], scalar1=bias_ap,
                                                      scalar2=MOD[:, l, 16 + m, j:j + 1], op0=ALU.add, op1=ALU.mult,
                     r=[pb, PV_b, MOD_b], w=[tmp_b])
            else:
                P.op('dve', 'tensor_scalar', out=tmp[:, 0:n], in0=pt[:, 0:n],
                                                      scalar1=MOD[:, l, 16 + m, j:j + 1], scalar2=None, op0=ALU.mult,
                     r=[pb, MOD_b], w=[tmp_b])
            P.op('pool', 'tensor_tensor', out=X[:, m, sl], in0=X[:, m, sl], in1=tmp[:, 0:n], op=ALU.add,
                 r=[tmp_b], ww=[Xb[tt]])

        def conv_layer(l, j):
            win = conv_w_in[j]
            r_bin = VROW['conv_b_in%d' % j]
            r_wdw = VROW['conv_w_dw%d' % j]
            if dbg == 3:
                P.op('dve', 'tensor_copy', out=X[:, 0, 0:48], in_=MOD[:, 0, :, :].rearrange("p a b -> p (a b)"), r=[MOD_b], w=[Xb[0]])
                P.op('dve', 'tensor_copy', out=X[:, 0, 48:64], in_=AMOD[:, 0, :, :].rearrange("p a b -> p (a b)"), r=[AMOD_b], w=[Xb[0]])
                rs, rs_b = stat.next()
                colsum_rstd(X[:, :, 256:512], Xb[1], RMS_EPS, rs, rs_b)
                P.op('dve', 'tensor_copy', out=X[:, 1, 0:256], in_=rs[:], r=[rs_b], w=[Xb[0]])
                P.op('dve', 'tensor_copy', out=X[:, 2, 0:128], in_=PV[:, 0, :], r=[PV_b], w=[Xb[0]])
                return
            if dbg == 2:
                modnorm(l, 0)
                for tt in range(NHALF):
                    for k in range(NCH):
                        P.op('dve', 'tensor_copy', out=X[:, k, tt * TT:(tt + 1) * TT], in_=H[:, k, tt * TT:(tt + 1) * TT],
                             r=[Hb[tt]], w=[Xb[tt]])
                return
            for half, m in [(hf, mm) for hf in range(2) for mm in range(NCH)]:
                if m == 0:
                    modnorm(l, half)
                wt, wb = load_w(win, [(m * 128, 128), (D + m * 128, 128), (2 * D + m * 128, 128)])
                bv = PV[:, m, r_bin:r_bin + 1]
                bg = PV[:, m, r_bin + 1:r_bin + 2]
                bz = PV[:, m, r_bin + 2:r_bin + 3]
                for tp in range(NHALF // 2):
                    hl = slice(tp * 512, (tp + 1) * 512)
                    hbs = [Hb[2 * tp], Hb[2 * tp + 1]]
                    pss = [K.ps() for _ in range(3)]
                    for q in range(3):
                        for k in range(NCH):
                            P.op('pe', 'matmul', pss[q][0][:], lhsT=wt[:, k, q * 128:(q + 1) * 128], rhs=H[:, k, hl],
                                 start=(k == 0), stop=(k == NCH - 1), r=[wb] + hbs, w=[pss[q][1]], inc=(k == NCH - 1))
                    for sub in range(2):
                        tt = half * NHALF + 2 * tp + sub
                        sl = slice(tt * TT, (tt + 1) * TT)
                        cs = slice(sub * TT, (sub + 1) * TT)
                        sg, sg_b = t512.next()
                        P.op('act', 'activation', out=sg[:], in_=pss[1][0][:, cs], func=AF.Sigmoid, bias=bg, scale=1.0,
                             r=[pss[1][1], PV_b], w=[sg_b])
                        ut, ut_b = b512.next()
                        P.op('dve', 'scalar_tensor_tensor', out=ut[:], in0=pss[0][0][:, cs], scalar=bv, in1=sg[:], op0=ALU.add,
                             op1=ALU.mult, r=[pss[0][1], sg_b, PV_b], w=[ut_b])
                        zb_, zb_b_ = t512.next()
                        P.op('act', 'activation', out=zb_[:], in_=pss[2][0][:, cs], func=AF.Identity, bias=bz, scale=1.0,
                             r=[pss[2][1], PV_b], w=[zb_b_])
                        sz_, sz_b_ = t512.next()
                        P.op('act', 'activation', out=sz_[:], in_=zb_[:], func=AF.Sigmoid, r=[zb_b_], w=[sz_b_])
                        zt, zt_b = b512.next()
                        P.op('dve', 'tensor_tensor', out=zt[:], in0=zb_[:], in1=sz_[:], op=ALU.mult, r=[zb_b_, sz_b_], w=[zt_b])
                        P.dma('auto', 'dma_start', out=U_d[m * 128:(m + 1) * 128, sl], in_=ut[:], r=[ut_b], w=[U_b])
                        P.dma('auto', 'dma_start', out=SZ_d[m * 128:(m + 1) * 128, sl], in_=zt[:], r=[zt_b], w=[SZ_b])
            for m in range(NCH):
                dgv, dg_b = diag, diag_b
                for k in range(31):
                    P.op('dve', 'tensor_scalar',
                        out=dgv[:, k * 128:(k + 1) * 128], in0=ident, scalar1=PV[:, m, r_wdw + k:r_wdw + k + 1],
                        scalar2=None, op0=ALU.mult, r=[cst_b, PV_b], ww=[dg_b])
                bdw = PV[:, m, VROW['conv_b_dw%d' % j]:VROW['conv_b_dw%d' % j] + 1]
                for (s0, L) in SEQS:
                    for t0 in range(s0, s0 + L, TT):
                        n = min(TT, s0 + L - t0)
                        lo = max(t0 - 15, s0)
                        hi = min(t0 + n + 15, s0 + L)
                        uu, uu_b = bigb.next()
                        uv = uu[:].rearrange("p c t -> p (c t)")
                        if lo > t0 - 15 or hi < t0 + n + 15:
                            P.op('dve', 'memset', uv[:, 0:n + 30], 0.0, w=[uu_b])
                        P.dma('sp', 'dma_start',
                            out=uv[:, lo - (t0 - 15):hi - (t0 - 15)], in_=U_d[m * 128:(m + 1) * 128, lo:hi],
                            r=[U_b], w=[uu_b])
                        pt, pb = K.ps()
                        for k in range(31):
                            P.op('pe', 'matmul',
                                pt[:, 0:n], lhsT=dgv[:, k * 128:(k + 1) * 128], rhs=uv[:, k:k + n],
                                start=(k == 0), stop=(k == 30), r=[dg_b, uu_b], w=[pb], inc=(k == 30))
                        ct, ct_b = t512.next()
                        P.op('act', 'activation', out=ct[:, 0:n], in_=pt[:, 0:n],
                                                                              func=AF.Identity, bias=bdw, scale=1.0,
                             r=[pb, PV_b], w=[ct_b])
                        P.dma('auto', 'dma_start',
                            out=C_d[m * 128:(m + 1) * 128, t0:t0 + n], in_=ct[:, 0:n], r=[ct_b], w=[C_b])
            P.barrier()
            gts3 = Rot([(arena[:, q_ * 2048:(q_ + 1) * 2048].rearrange("p (k n) -> p k n", n=TT), Buf("gt3_%d" % q_)) for q_ in range(3)])
            wo_t = wmem.rearrange("p (k n) -> p k n", k=NCH)
            for hh in range(2):
                P.dma('pool', 'dma_start',
                    out=wo_t[:, :, hh * 512:(hh + 1) * 512],
                    in_=conv_w_out[j][:, hh * 512:(hh + 1) * 512].rearrange("(k p) n -> p k n", p=128), w=[wb0, wb1])
            lnw = VROW['conv_ln_w%d' % j]
            lnb = VROW['conv_ln_b%d' % j]
            bo = VROW['conv_b_out%d' % j]
            for tt in range(NTILE):
                sl = slice(tt * TT, (tt + 1) * TT)
                ct, ct_b = big.next()
                P.dma('sp', 'dma_start',
                    out=ct[:], in_=C_d[:, sl].rearrange("(k p) n -> p k n", p=128), r=[C_b], w=[ct_b])
                zt, zt_b = bigb.next()
                P.dma('sp', 'dma_start',
                    out=zt[:], in_=SZ_d[:, sl].rearrange("(k p) n -> p k n", p=128), r=[SZ_b], w=[zt_b])
                pm, pm_b = K.ps()
                for k in range(NCH):
                    P.op('pe', 'matmul', pm[:, 0:TT], lhsT=ones, rhs=ct[:, k, :], start=(k == 0),
                                                                     stop=(k == NCH - 1), r=[ct_b, cst_b], w=[pm_b], inc=(k == NCH - 1))
                mean, mean_b = stat.next()
                P.op('dve', 'tensor_scalar', out=mean[:], in0=pm[:, 0:TT], scalar1=1.0 / D,
                                                                        scalar2=None, op0=ALU.mult, r=[pm_b], w=[mean_b])
                P.op('dve', 'tensor_tensor',
                    out=ct[:], in0=ct[:], in1=mean[:].unsqueeze(1).to_broadcast([128, NCH, TT]), op=ALU.subtract,
                    r=[ct_b, mean_b], w=[ct_b])
                rs, rs_b = stat.next()
                colsum_rstd(ct[:], ct_b, LN_EPS, rs, rs_b)
                gt, gt_b = gts3.next()
                for k in range(NCH):
                    tmp, tmp_b = t512.next()
                    P.op('dve', 'scalar_tensor_tensor',
                        out=tmp[:], in0=ct[:, k, :], scalar=PV[:, k, lnw:lnw + 1], in1=rs[:], op0=ALU.mult, op1=ALU.mult,
                        r=[ct_b, rs_b, PV_b], w=[tmp_b])
                    P.op('act', 'activation', out=tmp[:], in_=tmp[:], func=AF.Silu,
                                                                      bias=PV[:, k, lnb:lnb + 1], scale=1.0,
                         r=[tmp_b, PV_b], w=[tmp_b])
                    P.op('dve', 'tensor_tensor',
                        out=gt[:, k, :], in0=tmp[:], in1=zt[:, k, :], op=ALU.mult, r=[tmp_b, zt_b], ww=[gt_b])
                for m in range(NCH):
                    pt, pb = K.ps()
                    for k in range(NCH):
                        P.op('pe', 'matmul',
                            pt[:, 0:TT], lhsT=wo_t[:, k, m * 128:(m + 1) * 128], rhs=gt[:, k, :], start=(k == 0),
                            stop=(k == NCH - 1), r=[wb0, wb1, gt_b], w=[pb], inc=(k == NCH - 1))
                    residual_update(l, pt, pb, m, tt, PV[:, m, bo:bo + 1])
            P.barrier()


        def rs_row(qr):
            return min(max(qr - 4, 0), 24)

        def na_blocks():
            blocks = []
            for sq in range(4):
                for hq in range(2):
                    blocks.append((sq * 256 + hq * 128, 128, [2 * sq, 2 * sq + 1], None, 0))
            for bi in range(16):
                r = 2 * bi
                lo = rs_row(r)
                hi = rs_row(r + 1) + 8
                tiles = list(range(lo // 2, (hi - 1) // 2 + 1))
                cls = {0: 0, 2: 1, 28: 3, 30: 4}.get(r, 2)
                blocks.append((1024 + r * 64, 128, [8 + t for t in tiles], cls, len(tiles)))
            return blocks

        def na_layer(l):
            kc_t, kc_b = bigb.next()
            for kt in range(2):
                lt, lb = big.next()
                lt2 = lt[:].rearrange("p c t -> p (c t)")[:, 0:D]
                P.dma('sp', 'dma_start', out=lt2, in_=ck[kt * 128:(kt + 1) * 128, :], w=[lb])
                for half in range(2):
                    pt, pb = K.ps()
                    for q in range(4):
                        c = half * 4 + q
                        P.op('pe', 'transpose', pt[:, q * 128:(q + 1) * 128], lt2[:, c * 128:(c + 1) * 128], ident,
                             r=[lb, cst_b], w=[pb])
                    P.op('dve', 'tensor_copy', out=kc_t[:, half * 4:half * 4 + 4, kt * 128:(kt + 1) * 128],
                         in_=pt[:].rearrange("p (q t) -> p q t", t=128), r=[pb], w=[kc_b])
            P.dma('auto', 'dma_start', out=K_d[:, T:T + 256].rearrange("(c p) n -> p c n", p=128), in_=kc_t[:],
                  r=[kc_b], w=[K_b])
            for kt in range(2):
                vb, vb_b = bigb.next()
                vb2 = vb[:].rearrange("p c t -> p (c t)")[:, 0:D]
                P.dma('pool', 'dma_start', out=vb2, in_=cv[kt * 128:(kt + 1) * 128, :], w=[vb_b])
                P.dma('auto', 'dma_start', out=V_d[T + kt * 128:T + (kt + 1) * 128, :], in_=vb2, r=[vb_b], w=[V_b])
            for half in range(2):
                modnorm(l, half)
                for sec, dstd, dst_b in ((0, Q_d, Q_b), (1, K_d, K_b), (3, SZ_d, SZ_b)):
                    for hb in range(2):
                        wt, wb = load_w(na_w_in, [(sec * D + hb * 512, 512)])
                        for tp in range(NHALF // 2):
                            hl = slice(tp * 512, (tp + 1) * 512)
                            hbs = [Hb[2 * tp], Hb[2 * tp + 1]]
                            for mm in range(4):
                                pt, pb = K.ps()
                                for k in range(NCH):
                                    P.op('pe', 'matmul', pt[:], lhsT=wt[:, k, mm * 128:(mm + 1) * 128], rhs=H[:, k, hl],
                                         start=(k == 0), stop=(k == NCH - 1), r=[wb] + hbs, w=[pb], inc=(k == NCH - 1))
                                ch = hb * 4 + mm
                                for sub in range(2):
                                    tt = half * NHALF + 2 * tp + sub
                                    sl = slice(tt * TT, (tt + 1) * TT)
                                    cs = slice(sub * TT, (sub + 1) * TT)
                                    ot, ot_b = b512.next()
                                    if sec == 3:
                                        P.op('act', 'activation', out=ot[:], in_=pt[:, cs], func=AF.Silu, r=[pb], w=[ot_b])
                                    elif mm % 2 == 0:
                                        P.op('act', 'activation', out=ot[:], in_=pt[:, cs], func=AF.Copy, r=[pb], w=[ot_b])
                                    else:
                                        P.op('dve', 'tensor_copy', out=ot[:], in_=pt[:, cs], r=[pb], w=[ot_b])
                                    P.dma('auto', 'dma_start', out=dstd[ch * 128:(ch + 1) * 128, sl], in_=ot[:], r=[ot_b], w=[dst_b])
                for sec in (1, 2):
                    if sec == 1 and half == 1:
                        continue
                    for hb in range(2):
                        wt, wb = load_w(na_w_in, [(sec * D + hb * 512, 512)])
                        for tk in range(half * 12, (half + 1) * 12):
                            if sec == 1 and tk >= 8:
                                continue
                            lk = tk - half * 12
                            pt, pb = K.ps()
                            for k in range(NCH):
                                P.op('pe', 'matmul', pt[:], lhsT=H[:, k, lk * 128:(lk + 1) * 128], rhs=wt[:, k, :],
                                     start=(k == 0), stop=(k == NCH - 1), r=[wb, Hb[lk * 128 // TT]], w=[pb], inc=(k == NCH - 1))
                            src_ap, src_b = pt[:], pb
                            if tk < 8:
                                ot, ot_b = big.next()
                                ot2 = ot[:].rearrange("p c t -> p (c t)")[:, 0:512]
                                P.op('act', 'activation', out=ot2, in_=pt[:], func=AF.Copy, r=[pb], w=[ot_b])
                                dst = nk if sec == 1 else nv
                                P.dma('auto', 'dma_start', out=dst[tk * 128:(tk + 1) * 128, hb * 512:(hb + 1) * 512], in_=ot2, r=[ot_b])
                                src_ap, src_b = ot2, ot_b
                            if sec == 2:
                                vb, vb_b = bigb.next()
                                vb2 = vb[:].rearrange("p c t -> p (c t)")[:, 0:512]
                                P.op('dve', 'tensor_copy', out=vb2, in_=src_ap, r=[src_b], w=[vb_b])
                                P.dma('auto', 'dma_start', out=V_d[tk * 128:(tk + 1) * 128, hb * 512:(hb + 1) * 512], in_=vb2,
                                      r=[vb_b], w=[V_b])
            wo_t = wmem.rearrange("p (k n) -> p k n", k=NCH)
            for hh in range(2):
                P.dma('pool', 'dma_start', out=wo_t[:, :, hh * 512:(hh + 1) * 512],
                      in_=na_w_out[:, hh * 512:(hh + 1) * 512].rearrange("(k p) n -> p k n", p=128), w=[wb0, wb1])
            P.barrier()
            K.set_acc(True)
            qT = arena[:, 0:T]
            kT = arena[:, T:2 * T + 256]
            vT = arena[:, 2 * T + 256:3 * T + 512].rearrange("p (t n) -> p t n", n=128)
            for c in range(NCH):
                qT_b, kT_b, vT_b = Buf("qT%d" % c), Buf("kT%d" % c), Buf("vT%d" % c)
                if c > 0:
                    P.barrier()
                P.dma('sp', 'dma_start', out=qT, in_=Q_d[c * 128:(c + 1) * 128, :], r=[Q_b], w=[qT_b])
                P.dma('sp', 'dma_start', out=kT, in_=K_d[c * 128:(c + 1) * 128, :], r=[K_b], w=[kT_b])
                P.dma('sp', 'dma_start', out=vT, in_=V_d[:, c * 128:(c + 1) * 128].rearrange("(t p) n -> p t n", p=128),
                      r=[V_b], w=[vT_b])
                for (q0, N, tiles, cls, nloc) in na_blocks():
                    szt, szt_b = bq.next()
                    P.dma('sp', 'dma_start', out=szt[:, 0:N], in_=SZ_d[c * 128:(c + 1) * 128, q0:q0 + N], r=[SZ_b], w=[szt_b])
                    gt, gt_b = bq.next()
                    alltiles = list(tiles) + ([24, 25] if cls is not None else [])
                    pts = {}
                    bts = {}
                    for hh in range(2):
                        R = slice(hh * 64, hh * 64 + 64)
                        hd = 2 * c + hh
                        if cls is not None:
                            bt, bt_b = big.next()
                            bt3 = bt[:].rearrange("p c t -> p (c t)")[:, 0:nloc * 128].rearrange("p (t q) -> p t q", q=128)
                            P.dma('sp', 'dma_start', out=bt3, in_=btab[cls, hd, 0:nloc].rearrange("t k q -> k t q"), w=[bt_b])
                        for ti, tile in enumerate(alltiles):
                            ps_, ps_b = K.ps()
                            P.op('pe', 'matmul', ps_[:, 0:N], lhsT=kT[R, tile * 128:(tile + 1) * 128], rhs=qT[R, q0:q0 + N],
                                 start=True, stop=True, r=[kT_b, qT_b], w=[ps_b])
                            pT, pT_b = ptp.next()
                            pts[(hh, ti)] = (pT, pT_b)
                            if cls is not None and ti < nloc:
                                ssb, ssb_b = t512.next()
                                P.op('dve', 'scalar_tensor_tensor', out=ssb[:, 0:N], in0=ps_[:, 0:N], scalar=0.125,
                                     in1=bt3[:, ti, :], op0=ALU.mult, op1=ALU.add, r=[ps_b, bt_b], w=[ssb_b])
                                P.op('act', 'activation', out=pT[:, 0:N], in_=ssb[:, 0:N], func=AF.Exp, r=[ssb_b], w=[pT_b])
                            else:
                                P.op('act', 'activation', out=pT[:, 0:N], in_=ps_[:, 0:N], func=AF.Exp, scale=0.125,
                                     r=[ps_b], w=[pT_b])
                    for hh in range(2):
                        R = slice(hh * 64, hh * 64 + 64)
                        po, po_b = K.psacc()
                        pd, pd_b = K.psacc()
                        for ti, tile in enumerate(alltiles):
                            pT, pT_b = pts[(hh, ti)]
                            first, last = (ti == 0), (ti == len(alltiles) - 1)
                            P.op('pe', 'matmul', po[:, 0:N], lhsT=vT[:, tile, :], rhs=pT[:, 0:N], start=first, stop=last,
                                 r=[vT_b, pT_b], w=[po_b], inc=last)
                        for ti, tile in enumerate(alltiles):
                            pT, pT_b = pts[(hh, ti)]
                            first, last = (ti == 0), (ti == len(alltiles) - 1)
                            P.op('pe', 'matmul', pd[:, 0:N], lhsT=onesb[:], rhs=pT[:, 0:N], start=first, stop=last,
                                 r=[onesb_b, pT_b], w=[pd_b], inc=last)
                        rd, rd_b = t512.next()
                        P.op('dve', 'reciprocal', out=rd[R, 0:N], in_=pd[R, 0:N], r=[pd_b], w=[rd_b])
                        oo, oo_b = t512.next()
                        P.op('dve', 'tensor_tensor', out=oo[R, 0:N], in0=po[R, 0:N], in1=rd[R, 0:N], op=ALU.mult,
                             r=[po_b, rd_b], w=[oo_b])
                        P.op('dve', 'tensor_tensor', out=gt[R, 0:N], in0=oo[R, 0:N], in1=szt[R, 0:N], op=ALU.mult,
                             r=[oo_b, szt_b], ww=[gt_b])
                    P.dma('pool', 'dma_start', out=U_d[c * 128:(c + 1) * 128, q0:q0 + N], in_=gt[:, 0:N], r=[gt_b], w=[U_b])
            K.set_acc(False)
            P.barrier()
            for tt in range(NTILE):
                sl = slice(tt * TT, (tt + 1) * TT)
                gt, gt_b = bigb.next()
                P.dma('sp', 'dma_start', out=gt[:], in_=U_d[:, sl].rearrange("(k p) n -> p k n", p=128), r=[U_b], w=[gt_b])
                for m in range(NCH):
                    pt, pb = K.ps()
                    for k in range(NCH):
                        P.op('pe', 'matmul', pt[:, 0:TT], lhsT=wo_t[:, k, m * 128:(m + 1) * 128], rhs=gt[:, k, :],
                             start=(k == 0), stop=(k == NCH - 1), r=[wb0, wb1, gt_b], w=[pb], inc=(k == NCH - 1))
                    residual_update(l, pt, pb, m, tt, None)

        gTall_b = Buf('gTall')
        def ssd_layer(l):
            W = ssd_w_in
            r_wc = VROW['ssd_w_conv']
            r_bc = VROW['ssd_b_conv']
            P.dma('sp', 'dma_start', out=ssct[:], in_=ssc[:, :], w=[ssc_b])
            P.op('act', 'activation', out=ssct[:, 64:128], in_=ssct[:, 64:128], func=AF.Exp, r=[ssc_b], w=[ssc_b])
            P.op('dve', 'tensor_scalar', out=ssct[:, 64:128], in0=ssct[:, 64:128], scalar1=-1.0, scalar2=None, op0=ALU.mult,
                 r=[ssc_b], w=[ssc_b])
            dtb, Abc, Dbc = ssct[:, 0:64], ssct[:, 64:128], ssct[:, 128:160]
            for half in range(2):
                modnorm(l, half)
                for blk in range(4):
                    wt, wb = load_w(W, [(blk * 512, 512)])
                    for lk in range(12):
                        tk = half * 12 + lk
                        pt, pb = K.ps()
                        for k in range(NCH):
                            P.op('pe', 'matmul', pt[:], lhsT=H[:, k, lk * 128:(lk + 1) * 128], rhs=wt[:, k, :],
                                 start=(k == 0), stop=(k == NCH - 1), r=[wb, Hb[lk * 128 // TT]], w=[pb], inc=(k == NCH - 1))
                        zb, zb_b = bigb.next()
                        zb2 = zb[:].rearrange("p c t -> p (c t)")[:, 0:512]
                        P.op('act', 'activation', out=zb2, in_=pt[:], func=AF.Silu, r=[pb], w=[zb_b])
                        P.dma('auto', 'dma_start', out=ZT_d[tk * 128:(tk + 1) * 128, blk * 512:(blk + 1) * 512], in_=zb2,
                              r=[zb_b], w=[ZT_b])
                for blk in range(6):
                    wt, wb = load_w(W, [(2048 + blk * 512, 512)])
                    for tp in range(NHALF // 2):
                        hl = slice(tp * 512, (tp + 1) * 512)
                        hbs = [Hb[2 * tp], Hb[2 * tp + 1]]
                        for mm in range(4):
                            pt, pb = K.ps()
                            for k in range(NCH):
                                P.op('pe', 'matmul', pt[:], lhsT=wt[:, k, mm * 128:(mm + 1) * 128], rhs=H[:, k, hl],
                                     start=(k == 0), stop=(k == NCH - 1), r=[wb] + hbs, w=[pb], inc=(k == NCH - 1))
                            ch = blk * 4 + mm
                            for sub in range(2):
                                tt = half * NHALF + 2 * tp + sub
                                sl = slice(tt * TT, (tt + 1) * TT)
                                cs = slice(sub * TT, (sub + 1) * TT)
                                ot, ot_b = b512.next()
                                if mm % 2 == 0:
                                    P.op('act', 'activation', out=ot[:], in_=pt[:, cs], func=AF.Copy, r=[pb], w=[ot_b])
                                else:
                                    P.op('dve', 'tensor_copy', out=ot[:], in_=pt[:, cs], r=[pb], w=[ot_b])
                                P.dma('auto', 'dma_start', out=XBC_d[ch * 128:(ch + 1) * 128, sl], in_=ot[:], r=[ot_b], w=[XBC_b])
                wt, wb = load_w(W, [(5120, 64)])
                for lk in range(12):
                    tk = half * 12 + lk
                    pt, pb = K.ps()
                    for k in range(NCH):
                        P.op('pe', 'matmul', pt[:, 0:64], lhsT=H[:, k, lk * 128:(lk + 1) * 128], rhs=wt[:, k, 0:64],
                             start=(k == 0), stop=(k == NCH - 1), r=[wb, Hb[lk * 128 // TT]], w=[pb], inc=(k == NCH - 1))
                    dta, dta_b = t512.next()
                    P.op('dve', 'tensor_tensor', out=dta[:, 0:64], in0=pt[:, 0:64], in1=dtb, op=ALU.add, r=[pb, ssc_b], w=[dta_b])
                    P.op('act', 'activation', out=dta[:, 0:64], in_=dta[:, 0:64], func=AF.Exp, r=[dta_b], w=[dta_b])
                    P.op('act', 'activation', out=dta[:, 0:64], in_=dta[:, 0:64], func=AF.Ln, bias=1.0, scale=1.0,
                         r=[dta_b], w=[dta_b])
                    P.op('dve', 'tensor_tensor', out=dta[:, 64:128], in0=dta[:, 0:64], in1=Abc, op=ALU.mult,
                         r=[dta_b, ssc_b], w=[dta_b])
                    P.dma('auto', 'dma_start', out=DT_d[tk * 128:(tk + 1) * 128, :], in_=dta[:, 0:128], r=[dta_b], w=[DT_b])
            for m in range(24):
                for k in range(7):
                    P.op('dve', 'tensor_scalar', out=diag[:, k * 128:(k + 1) * 128], in0=ident,
                         scalar1=PV[:, m % 8, r_wc + 3 * k + m // 8:r_wc + 3 * k + m // 8 + 1], scalar2=None, op0=ALU.mult,
                         r=[cst_b, PV_b], ww=[diag_b])
                bcv = PV[:, m % 8, r_bc + m // 8:r_bc + m // 8 + 1]
                P.op('dve', 'tensor_scalar', out=diag[:, 7 * 128:8 * 128], in0=ident, scalar1=bcv, scalar2=None, op0=ALU.mult,
                     r=[cst_b, PV_b], ww=[diag_b])
                for (s0, L) in SEQS:
                    for t0 in range(s0, s0 + L, TT):
                        n = TT
                        lo = max(t0 - 3, s0)
                        hi = min(t0 + n + 3, s0 + L)
                        uu, uu_b = bigb.next()
                        uv = uu[:].rearrange("p c t -> p (c t)")
                        if lo > t0 - 3 or hi < t0 + n + 3:
                            P.op('dve', 'memset', uv[:, 0:n + 6], 0.0, w=[uu_b])
                        P.dma('sp', 'dma_start', out=uv[:, lo - (t0 - 3):hi - (t0 - 3)], in_=XBC_d[m * 128:(m + 1) * 128, lo:hi],
                              r=[XBC_b], w=[uu_b])
                        if m < 20:
                            for sub in range(n // 128):
                                pt, pb = K.ps()
                                for k in range(7):
                                    P.op('pe', 'matmul', pt[:, 0:128], lhsT=uv[:, k + sub * 128:k + sub * 128 + 128],
                                         rhs=diag[:, k * 128:(k + 1) * 128], start=(k == 0), stop=False, r=[uu_b, diag_b], w=[pb], inc=False)
                                P.op('pe', 'matmul', pt[:, 0:128], lhsT=onesb[:], rhs=diag[:, 7 * 128:8 * 128], start=False, stop=True,
                                     r=[onesb_b, diag_b], w=[pb])
                                ot, ot_b = b512.next()
                                P.op('act', 'activation', out=ot[:, 0:128], in_=pt[:, 0:128], func=AF.Silu, r=[pb], w=[ot_b])
                                tok = t0 + sub * 128
                                if m < 16:
                                    P.dma('auto', 'dma_start', out=XT_d[tok:tok + 128, m * 128:(m + 1) * 128], in_=ot[:, 0:128],
                                          r=[ot_b], w=[XT_b])
                                else:
                                    P.dma('auto', 'dma_start', out=BT_d[tok:tok + 128, (m - 16) * 128:(m - 15) * 128], in_=ot[:, 0:128],
                                          r=[ot_b], w=[BT_b])
                        if m >= 16:
                            pt, pb = K.ps()
                            for k in range(7):
                                P.op('pe', 'matmul', pt[:, 0:n], lhsT=diag[:, k * 128:(k + 1) * 128], rhs=uv[:, k:k + n],
                                     start=(k == 0), stop=(k == 6), r=[uu_b, diag_b], w=[pb], inc=(k == 6))
                            ot, ot_b = b512.next()
                            P.op('act', 'activation', out=ot[:, 0:n], in_=pt[:, 0:n], func=AF.Silu, bias=bcv, scale=1.0,
                                 r=[pb, PV_b], w=[ot_b])
                            P.dma('auto', 'dma_start', out=BC_d[(m - 16) * 128:(m - 15) * 128, t0:t0 + n], in_=ot[:, 0:n],
                                  r=[ot_b], w=[BC_b])
            P.barrier()
            hst = arena[:, 0:4096].bitcast(F32)
            hb16 = arena[:, 4096:6144]
            xdt = arena[:, 6144:8192]
            xdte = arena[:, 8192:10240]
            btms = Rot([(arena[:, 10240:10752], Buf("btm0")), (arena[:, 20736:21248], Buf("btm1"))])
            bcTs = Rot([(arena[:, 10752:11776].rearrange("p (c n) -> p c n", n=128), Buf("bcT0")),
                        (arena[:, 21248:22272].rearrange("p (c n) -> p c n", n=128), Buf("bcT1"))])
            sms = Rot([(arena[:, 15360:15616].bitcast(F32), Buf("sm0")), (arena[:, 22272:22528].bitcast(F32), Buf("sm1"))])
            Gm = arena[:, 13312:14336].bitcast(F32).rearrange("p (g i) -> p g i", i=128)
            Mts = Rot([(arena[:, 11776 + 0:11776 + 512].rearrange("p (h i) -> p h i", i=128), Buf("Mt0"))] +
                      [(arena[:, 15616 + q * 512:15616 + (q + 1) * 512].rearrange("p (h i) -> p h i", i=128), Buf("Mt%d" % (q + 1))) for q in range(2)])
            Rfs = Rot([(arena[:, 12288:13312].bitcast(F32).rearrange("p (h i) -> p h i", i=128), Buf("Rf0"))] +
                      [(arena[:, 16640 + q * 1024:16640 + (q + 1) * 1024].bitcast(F32).rearrange("p (h i) -> p h i", i=128), Buf("Rf%d" % (q + 1))) for q in range(2)])
            Lfs = Rot([(arena[:, 14336:15360].bitcast(F32).rearrange("p (h i) -> p h i", i=128), Buf("Lf0"))] +
                      [(arena[:, 18688 + q * 1024:18688 + (q + 1) * 1024].bitcast(F32).rearrange("p (h i) -> p h i", i=128), Buf("Lf%d" % (q + 1))) for q in range(2)])
            xdt_b, xdte_b, Gm_b = [Buf("ssd%d" % i) for i in range(3)]
            hst_bs = [Buf("hst%d" % g) for g in range(4)]
            hb16_bs = [Buf("hb16_%d" % g) for g in range(4)]
            K.set_acc(True)

            pend_yf = []

            def flush_yf():
                while pend_yf:
                    t0_, ys2_, ysb_b_ = pend_yf.pop(0)
                    P.dma('sp', 'dma_start', out=YF_d[t0_:t0_ + 128, :], in_=ys2_, r=ysb_b_, w=[YF_b])

            def chunk_gen(t0, d, final):
                lhs_seg = Mgt if d == 0 else Mlt
                rhs_msk = Mle if d == 0 else Mge
                xt, xt_b = bigb.next()
                xt2 = xt[:].rearrange("p c t -> p (c t)")
                btm, btm_b = btms.next()
                bcT, bcT_b = bcTs.next()
                sm, sm_b = sms.next()
                P.dma('sp', 'dma_start', out=xt2, in_=XT_d[t0:t0 + 128, :], r=[XT_b], w=[xt_b])
                P.dma('sp', 'dma_start', out=btm, in_=BT_d[t0:t0 + 128, :], r=[BT_b], w=[btm_b])
                P.dma('sp', 'dma_start', out=bcT, in_=BC_d[:, t0:t0 + 128].rearrange("(c p) n -> p c n", p=128), r=[BC_b], w=[bcT_b])
                dta, dta_b = t512.next()
                P.dma('sp', 'dma_start', out=dta[:, 0:128], in_=DT_d[t0:t0 + 128, :], r=[DT_b], w=[dta_b])
                flush_yf()
                dt_d = dta[:, d * 32:(d + 1) * 32]
                a_d = dta[:, 64 + d * 32:64 + (d + 1) * 32]
                pt, pb = K.ps()
                P.op('pe', 'matmul', pt[:, 0:32], lhsT=ones, rhs=a_d, start=True, stop=True, r=[cst_b, dta_b], w=[pb])
                P.op('pe', 'matmul', pt[:, 32:64], lhsT=rhs_msk, rhs=a_d, start=True, stop=True, r=[cst_b, dta_b], w=[pb])
                P.op('pe', 'matmul', pt[:, 64:96], lhsT=lhs_seg, rhs=a_d, start=True, stop=True, r=[cst_b, dta_b], w=[pb])
                P.op('act', 'activation', out=sm[:, 0:96], in_=pt[:, 0:96], func=AF.Exp, r=[pb], w=[sm_b])
                P.op('dve', 'tensor_tensor', out=sm[:, 96:128], in0=sm[:, 64:96], in1=dt_d, op=ALU.mult, r=[sm_b, dta_b], w=[sm_b])
                x3 = xt2.rearrange("p (h q) -> p h q", q=64)
                P.op('dve', 'tensor_tensor', out=xdt.rearrange("p (h q) -> p h q", q=64), in0=x3,
                     in1=dt_d.unsqueeze(2).to_broadcast([128, 32, 64]), op=ALU.mult, r=[xt_b, dta_b], w=[xdt_b])
                pg, pg_b = K.ps()
                for g in range(4):
                    P.op('pe', 'matmul', pg[:, g * 128:(g + 1) * 128], lhsT=bcT[:, g, :], rhs=bcT[:, 4 + g, :], start=True, stop=True,
                         r=[bcT_b], w=[pg_b])
                P.op('dve', 'tensor_tensor', out=Gm, in0=pg[:].rearrange("p (g i) -> p g i", i=128),
                     in1=rhs_msk.unsqueeze(1).to_broadcast([128, 4, 128]), op=ALU.mult, r=[pg_b, cst_b], w=[Gm_b])
                yield
                ysb, ysb_b = big.next()
                ys2 = ysb[:].rearrange("p c t -> p (c t)")
                ysb_bs = [Buf("ysg%d" % g_) for g_ in range(4)]
                ysb_all = [ysb_b] + ysb_bs
                def seg_part(g, hq):
                    h0 = g * 8 + hq * 4
                    Rf, Rf_b = Rfs.next()
                    Lf, Lf_b = Lfs.next()
                    Mt, Mt_b = Mts.next()
                    P.op('pool', 'tensor_tensor', out=Rf, in0=a_d[:, h0:h0 + 4].unsqueeze(2).to_broadcast([128, 4, 128]),
                         in1=rhs_msk.unsqueeze(1).to_broadcast([128, 4, 128]), op=ALU.mult, r=[dta_b, cst_b], w=[Rf_b])
                    psg, psg_b = K.ps()
                    P.op('pe', 'matmul', psg[:], lhsT=lhs_seg, rhs=Rf.rearrange("p h i -> p (h i)"), start=True, stop=True,
                         r=[cst_b, Rf_b], w=[psg_b])
                    P.op('act', 'activation', out=Lf, in_=psg[:].rearrange("p (h i) -> p h i", i=128), func=AF.Exp,
                         r=[psg_b], w=[Lf_b])
                    P.op('dve', 'tensor_tensor', out=Mt, in0=Lf, in1=Gm[:, g, :].unsqueeze(1).to_broadcast([128, 4, 128]),
                         op=ALU.mult, r=[Lf_b, Gm_b], w=[Mt_b])
                    return Mt, Mt_b

                order = [(g, hq) for g in range(4) for hq in range(2)]
                pend = seg_part(*order[0])
                yd = yd_b = None
                for idx, (g, hq) in enumerate(order):
                    Mt, Mt_b = pend
                    if idx + 1 < len(order):
                        pend = seg_part(*order[idx + 1])
                    if hq == 0:
                        yd, yd_b = K.psacc()
                    h0 = g * 8 + hq * 4
                    for hh in range(4):
                        h = h0 + hh
                        P.op('pe', 'matmul', yd[:, (h % 8) * 64:(h % 8 + 1) * 64], lhsT=Mt[:, hh, :], rhs=xdt[:, h * 64:(h + 1) * 64],
                             start=True, stop=True, r=[Mt_b, xdt_b], w=[yd_b])
                    if hq == 0:
                        continue
                    yo, yo_b = K.psacc()
                    P.op('pe', 'matmul', yo[:], lhsT=bcT[:, 4 + g, :], rhs=hb16[:, g * 512:(g + 1) * 512], start=True, stop=True,
                         r=[bcT_b, hb16_bs[g]], w=[yo_b])
                    ysg = ys2[:, g * 512:(g + 1) * 512]
                    P.op('dve', 'tensor_tensor', out=ysg.rearrange("p (h q) -> p h q", q=64),
                         in0=yo[:].rearrange("p (h q) -> p h q", q=64),
                         in1=sm[:, 32 + g * 8:32 + (g + 1) * 8].unsqueeze(2).to_broadcast([128, 8, 64]), op=ALU.mult,
                         r=[yo_b, sm_b], w=[ysb_bs[g]], ww=[ysb_b])
                    P.op('dve', 'tensor_tensor', out=ysg, in0=ysg, in1=yd[:], op=ALU.add, r=[yd_b], w=[ysb_bs[g]], ww=[ysb_b])
                yield
                P.op('pool', 'tensor_tensor', out=xdte.rearrange("p (h q) -> p h q", q=64), in0=x3,
                     in1=sm[:, 96:128].unsqueeze(2).to_broadcast([128, 32, 64]), op=ALU.mult, r=[xt_b, sm_b], w=[xdte_b])
                for g in range(4):
                    pst, pst_b = K.psacc()
                    P.op('pe', 'matmul', pst[:], lhsT=btm[:, g * 128:(g + 1) * 128], rhs=xdte[:, g * 512:(g + 1) * 512], start=True,
                         stop=True, r=[btm_b, xdte_b], w=[pst_b])
                    hg = hst[:, g * 512:(g + 1) * 512]
                    P.op('pool', 'tensor_tensor', out=hg.rearrange("p (h q) -> p h q", q=64), in0=hg.rearrange("p (h q) -> p h q", q=64),
                         in1=sm[:, g * 8:(g + 1) * 8].unsqueeze(2).to_broadcast([128, 8, 64]), op=ALU.mult, r=[sm_b], w=[hst_bs[g]])
                    P.op('dve', 'tensor_tensor', out=hg, in0=hg, in1=pst[:], op=ALU.add, r=[pst_b], w=[hst_bs[g]])
                    P.op('act', 'activation', out=hb16[:, g * 512:(g + 1) * 512], in_=hg, func=AF.Copy, r=[hst_bs[g]], w=[hb16_bs[g]])
                if not final:
                    pend_yf.append((t0, ys2, ysb_all))
                    return
                yf, yf_b = big.next()
                yf2 = yf[:].rearrange("p c t -> p (c t)")
                P.dma('sp', 'dma_start', out=yf2, in_=YF_d[t0:t0 + 128, :], r=[YF_b], w=[yf_b])
                zt, zt_b = big.next()
                zt2 = zt[:].rearrange("p c t -> p (c t)").bitcast(BF16)[:, 0:2048]
                P.dma('sp', 'dma_start', out=zt2, in_=ZT_d[t0:t0 + 128, :], r=[ZT_b], w=[zt_b])
                P.op('pool', 'tensor_tensor', out=ys2, in0=ys2, in1=yf2, op=ALU.add, r=[yf_b], w=ysb_all)
                P.op('dve', 'tensor_tensor', out=yf2.rearrange("p (h q) -> p h q", q=64), in0=x3,
                     in1=Dbc.unsqueeze(2).to_broadcast([128, 32, 64]), op=ALU.mult, r=[xt_b, ssc_b], w=[yf_b])
                P.op('dve', 'tensor_tensor', out=ys2, in0=ys2, in1=yf2, op=ALU.add, r=[yf_b], w=ysb_all)
                P.op('dve', 'tensor_tensor', out=ys2, in0=ys2, in1=zt2, op=ALU.mult, r=[zt_b], w=ysb_all)
                ssq, ssq_b = stat.next()
                P.op('act', 'activation', out=yf2, in_=ys2, func=AF.Square, accum_out=ssq[:, 0:1], r=ysb_all, w=[yf_b, ssq_b])
                P.op('act', 'activation', out=ssq[:, 1:2], in_=ssq[:, 0:1], func=AF.Sqrt, scale=1.0 / 2048, bias=RMS_EPS,
                     r=[ssq_b], w=[ssq_b])
                P.op('dve', 'reciprocal', out=ssq[:, 2:3], in_=ssq[:, 1:2], r=[ssq_b], w=[ssq_b])
                P.op('act', 'activation', out=ys2, in_=ys2, func=AF.Copy, scale=ssq[:, 2:3], r=[ssq_b], w=ysb_all)
                P.dma('auto', 'dma_start', out=GT_d[t0:t0 + 128, :], in_=ys2, r=ysb_all, w=[GT_b])

            def sweep(items):
                gens = [chunk_gen(*a) for a in items]
                next(gens[0])
                for i, g in enumerate(gens):
                    next(g)
                    if i + 1 < len(gens):
                        next(gens[i + 1])
                    for _ in g:
                        pass

            def init_state(seq_is_lat, d):
                if not seq_is_lat:
                    P.op('pool', 'memset', hst, 0.0, w=hst_bs)
                    P.op('dve', 'memset', hb16, 0.0, w=hb16_bs)
                    return
                for q4 in range(4):
                    lt, lb = big.next()
                    lt3 = lt[:].rearrange("p c t -> p (c t)")[:, 0:512].rearrange("p (a n) -> p a n", n=128)
                    r0 = d * 2048 + q4 * 512
                    P.dma('sp', 'dma_start', out=lt3, in_=st_in[r0:r0 + 512, :].rearrange("(a p) n -> p a n", p=128), w=[lb])
                    pt, pb = K.ps()
                    for a in range(4):
                        P.op('pe', 'transpose', pt[:, a * 128:(a + 1) * 128], lt3[:, a, :], ident, r=[lb, cst_b], w=[pb])
                    P.op('dve', 'tensor_copy', out=hst[:, q4 * 512:(q4 + 1) * 512], in_=pt[:], r=[pb], w=[hst_bs[q4]])
                    P.op('act', 'activation', out=hb16[:, q4 * 512:(q4 + 1) * 512], in_=hst[:, q4 * 512:(q4 + 1) * 512], func=AF.Copy,
                         r=[hst_bs[q4]], w=[hb16_bs[q4]])

            def out_state(si, d):
                for q4 in range(4):
                    pt, pb = K.ps()
                    for a in range(4):
                        c0 = q4 * 512 + a * 128
                        P.op('pe', 'transpose', pt[:, a * 128:(a + 1) * 128], hst[:, c0:c0 + 128], ident, r=[hst_bs[q4], cst_b], w=[pb])
                    ot, ot_b = big.next()
                    ot3 = ot[:].rearrange("p c t -> p (c t)")[:, 0:512]
                    P.op('act', 'activation', out=ot3, in_=pt[:], func=AF.Copy, r=[pb], w=[ot_b])
                    r0 = (si * 2 + d) * 2048 + q4 * 512
                    P.dma('auto', 'dma_start', out=ns[r0:r0 + 512, :].rearrange("(a p) n -> p a n", p=128),
                          in_=ot3.rearrange("p (a n) -> p a n", n=128), r=[ot_b])

            for si, (s0, L) in enumerate(SEQS):
                lat = (si == 4)
                nchk = L // 128
                init_state(lat, 0)
                sweep([(s0 + c * 128, 0, False) for c in range(nchk)])
                flush_yf()
                if not lat:
                    out_state(si, 0)
                init_state(lat, 1)
                sweep([(s0 + c * 128, 1, True) for c in reversed(range(nchk))])
                if not lat:
                    out_state(si, 1)
            P.barrier()
            K.set_acc(False)
            wo_t = arena[:, 0:16384].rearrange("p (k n) -> p k n", n=D)
            wo_b = Buf("ssd_wo")
            for k4 in range(4):
                P.dma('pool', 'dma_start', out=wo_t[:, k4 * 4:(k4 + 1) * 4, :],
                      in_=ssd_w_out[k4 * 512:(k4 + 1) * 512, :].rearrange("(k p) n -> p k n", p=128), w=[wo_b])
            gTs = Rot([(arena[:, 16384 + q_ * 4096:16384 + (q_ + 1) * 4096].rearrange("p (k n) -> p k n", n=TT), Buf("gT%d" % q_)) for q_ in range(2)])
            r_nw = VROW['ssd_norm_w']
            for tt in range(NTILE):
                gT, gTall_b = gTs.next()
                if tt > 0:
                    pass
                for sub in range(TT // 128):
                    tok = tt * TT + sub * 128
                    gl, gl_b = big.next()
                    gl2 = gl[:].rearrange("p c t -> p (c t)")
                    P.dma('sp', 'dma_start', out=gl2, in_=GT_d[tok:tok + 128, :], r=[GT_b], w=[gl_b])
                    for q4 in range(4):
                        pt, pb = K.ps()
                        for a in range(4):
                            k = q4 * 4 + a
                            P.op('pe', 'transpose', pt[:, a * 128:(a + 1) * 128], gl2[:, k * 128:(k + 1) * 128], ident,
                                 r=[gl_b, cst_b], w=[pb])
                        for a in range(4):
                            k = q4 * 4 + a
                            nwv = PV[:, k % 8, r_nw + k // 8:r_nw + k // 8 + 1]
                            P.op('act', 'activation', out=gT[:, k, sub * 128:(sub + 1) * 128], in_=pt[:, a * 128:(a + 1) * 128],
                                 func=AF.Copy, scale=nwv, r=[pb, PV_b], ww=[gTall_b])
                for m in range(NCH):
                    pt, pb = K.ps()
                    for k in range(16):
                        P.op('pe', 'matmul', pt[:, 0:TT], lhsT=wo_t[:, k, m * 128:(m + 1) * 128], rhs=gT[:, k, :],
                             start=(k == 0), stop=(k == 15), r=[wo_b, gTall_b], w=[pb], inc=(k == 15))
                    residual_update(l, pt, pb, m, tt, None)
            P.barrier()

        for l in range(nlayers):
            kind = l % 3
            if kind == 0:
                conv_layer(l, l // 3)
            elif kind == 1:
                na_layer(l)
            else:
                ssd_layer(l)

        fnw = VROW['final_norm_w']
        for tt in range(NTILE):
            sl = slice(tt * TT, (tt + 1) * TT)
            rs, rs_b = stat.next()
            if dbg:
                P.op('dve', 'memset', rs[:], 1.0, w=[rs_b])
            else:
                colsum_rstd(X[:, :, sl], Xb[tt], RMS_EPS, rs, rs_b)
            yt, yt_b = big.next()
            for k in range(NCH):
                if dbg:
                    P.op('dve', 'tensor_copy', out=yt[:, k, :], in_=X[:, k, sl],
                         r=[Xb[tt]], w=[yt_b])
                else:
                    P.op('dve', 'scalar_tensor_tensor',
                        out=yt[:, k, :], in0=X[:, k, sl], scalar=PV[:, k, fnw:fnw + 1], in1=rs[:], op0=ALU.mult,
                        op1=ALU.mult, r=[Xb[tt], rs_b, PV_b], ww=[yt_b])
            for q in range(TT // 128):
                ot, ot_b = big.next()
                ot2 = ot[:].rearrange("p c t -> p (c t)")[:, 0:D]
                for half in range(2):
                    pt, pb = K.ps()
                    for c4 in range(4):
                        c = half * 4 + c4
                        P.op('pe', 'transpose',
                            pt[:, c4 * 128:(c4 + 1) * 128], yt[:, c, q * 128:(q + 1) * 128], ident,
                            r=[yt_b, cst_b], w=[pb])
                    if half == 0:
                        P.op('act', 'activation', out=ot2[:, 0:512], in_=pt[:], func=AF.Copy,
                             r=[pb], w=[ot_b])
                    else:
                        P.op('dve', 'tensor_copy', out=ot2[:, 512:1024], in_=pt[:],
                             r=[pb], w=[ot_b])
                tok = tt * TT + q * 128
                dst = y_ctx[tok:tok + 128, :] if tok < 1024 else y_lat[tok - 1024:tok - 1024 + 128, :]
                P.dma('auto', 'dma_start', out=dst, in_=ot2, r=[ot_b])

        P.finish()
        P.emit()
        print("instructions:", P.ninst, {e: P.seq[e] for e in ENGS}, {e: P.dcount[e] for e in ENGS})
    return nc


def make_btab(rpb):
    ext = np.concatenate([rpb.reshape(16, -1), np.full((16, 1), -30000.0, np.float32)], axis=1)
    rs_row = lambda qr: min(max(qr - 4, 0), 24)
    idx = np.full((5, 5, 128, 128), 465, np.int64)
    kc = np.arange(64)[:, None]
    qc = np.arange(64)[None, :]
    cs = np.clip(qc - 8, 0, 48)
    colok = (kc >= cs) & (kc < cs + 16)
    dcol = np.clip(kc - qc, -15, 15) + 15
    for cls, r in enumerate((0, 2, 4, 28, 30)):
        ft = rs_row(r) // 2
        for jt in range(5):
            for a in range(2):
                for b in range(2):
                    kr = 2 * (ft + jt) + a
                    qr = r + b
                    if kr > 31 or not (rs_row(qr) <= kr < rs_row(qr) + 8):
                        continue
                    blk = np.where(colok, (kr - qr + 7) * 31 + dcol, 465)
                    idx[cls, jt, a * 64:(a + 1) * 64, b * 64:(b + 1) * 64] = blk
    return np.ascontiguousarray(ext[:, idx].transpose(1, 0, 2, 3, 4))


def make_in_maps(inp):
    g = lambda k: np.ascontiguousarray(np.asarray(inp[k], dtype=np.float32))
    vi = np.arange(128)[:, None]
    ii = np.arange(128)[None, :]
    cst = np.concatenate([np.eye(128, dtype=np.float32), np.ones((128, 128), np.float32), (vi <= ii).astype(np.float32),
                          (vi >= ii).astype(np.float32), (vi > ii).astype(np.float32), (vi < ii).astype(np.float32)], axis=1)
    ssc = np.concatenate([g('ssd_dt_bias')[0].reshape(-1), g('ssd_a_log')[0].reshape(-1), g('ssd_d')[0].reshape(-1)])
    ssc = np.ascontiguousarray(np.broadcast_to(ssc[None, :], (128, 160)))
    btab = make_btab(g('na_rpb')[0])
    maps = []
    for core in range(8):
        s = core // 4
        vt = np.zeros((128, D), np.float32)

        def put(name, arr):
            a = np.asarray(arr, np.float32).reshape(-1, D)
            vt[VROW[name]:VROW[name] + a.shape[0]] = a
        put('c_ctx', g('c_ctx'))
        put('c', g('c')[s])
        for l in range(4):
            put('ada_b%d' % l, g('ada_b')[l])
            put('norm_w%d' % l, g('norm_w')[l])
        put('final_norm_w', g('final_norm_w'))
        for j in range(2):
            put('conv_b_in%d' % j, g('conv_b_in')[j])
            put('conv_w_dw%d' % j, g('conv_w_dw')[j])
            put('conv_b_dw%d' % j, g('conv_b_dw')[j])
            put('conv_ln_w%d' % j, g('conv_ln_w')[j])
            put('conv_ln_b%d' % j, g('conv_ln_b')[j])
            put('conv_b_out%d' % j, g('conv_b_out')[j])
        put('ssd_w_conv', g('ssd_w_conv')[0])
        put('ssd_b_conv', g('ssd_b_conv')[0])
        put('ssd_norm_w', g('ssd_norm_w')[0])
        m = {
            "xp": g('x_prompt')[core * 4:(core + 1) * 4].reshape(1024, D),
            "xs": g('x_sample')[s],
            "vtab": vt,
            "cst": cst,
            "ada_w": g('ada_w'),
            "conv_w_in": g('conv_w_in'),
            "conv_w_out": g('conv_w_out'),
            "na_w_in": g('na_w_in')[0],
            "na_w_out": g('na_w_out')[0],
            "ck": g('cache_k')[s, 0].reshape(256, D),
            "cv": g('cache_v')[s, 0].reshape(256, D),
            "btab": btab,
            "ssd_w_in": g('ssd_w_in')[0],
            "ssd_w_out": g('ssd_w_out')[0],
            "ssc": ssc,
            "st_in": g('state_ssm')[s, 0].reshape(2 * 2048, 128),
        }
        maps.append(m)
    return maps


def run(inp, nlayers=4, dbg=False):
    nc = build(nlayers, dbg)
    maps = make_in_maps(inp)
    res = run_bass_kernel_spmd(nc, maps, core_ids=list(range(8)))
    return res.results


def kernel(**inp):
    rs = run(inp)
    y_prompt = np.concatenate([r["y_ctx"].reshape(4, 256, D) for r in rs], axis=0)
    y_sample = np.stack([rs[0]["y_lat"], rs[4]["y_lat"]], axis=0)
    new_k = np.concatenate([r["nk"].reshape(4, 1, 256, 16, 64) for r in rs], axis=0)
    new_v = np.concatenate([r["nv"].reshape(4, 1, 256, 16, 64) for r in rs], axis=0)
    new_s = np.concatenate([r["ns"].reshape(4, 1, 2, 32, 64, 128) for r in rs], axis=0)
    return (y_prompt, y_sample, new_k, new_v, new_s)
```

```python
import numpy as np
import concourse.bass as bass
import concourse.mybir as mybir
from concourse.bass_utils import run_bass_kernel_spmd
from contextlib import ExitStack

F32 = mybir.dt.float32
BF16 = mybir.dt.bfloat16
AF = mybir.ActivationFunctionType
ALU = mybir.AluOpType
AX = mybir.AxisListType

ENGS = ['pe', 'act', 'dve', 'pool', 'sp']
CAP = 30000
SAME_ENG_SYNC = True
NSLOT = 24


class Buf:
    __slots__ = ('name', 'lw', 'rd', 'dw', 'dr')

    def __init__(self, name):
        self.name = name
        self.lw = None
        self.rd = {}
        self.dw = {}
        self.dr = {}


class Prog:
    def __init__(self, nc, es):
        self.nc = nc
        self.es = es
        self.q = {e: [] for e in ENGS}
        self.seq = {e: 0 for e in ENGS}
        self.seen = {e: {} for e in ENGS}
        self.esems = {e: [] for e in ENGS}
        self.dslots = {e: [] for e in ENGS}
        self.dcount = {e: 0 for e in ENGS}
        self.ninst = 0

    def newsem(self, name):
        return self.es.enter_context(self.nc.semaphore(name))

    def esem(self, e, ep):
        while len(self.esems[e]) <= ep:
            self.esems[e].append(self.newsem("s_%s_%d" % (e, len(self.esems[e]))))
        return self.esems[e][ep]

    def _w_eng(self, e, waits, src, s):
        if src == e and (e == 'pe' or not SAME_ENG_SYNC):
            return
        ep = (s - 1) // CAP
        val = (s - 1) % CAP + 1
        key = ('e', src, ep)
        if self.seen[e].get(key, 0) >= val:
            return
        self.seen[e][key] = val
        waits[key] = (self.esem(src, ep), val)

    def _w_sem(self, e, waits, sem, val):
        key = ('d', sem.num)
        if self.seen[e].get(key, 0) >= val:
            return
        self.seen[e][key] = val
        waits[key] = (sem, val)

    def _w_evs(self, e, waits, evs):
        for sem, val in evs.values():
            self._w_sem(e, waits, sem, val)

    def op(self, e, fname, *args, r=(), w=(), ww=(), inc=True, **kw):
        fn = (fname, args, kw)
        assert inc or e == 'pe'
        weak = set(id(b) for b in ww)
        w = list(w) + list(ww)
        waits = {}
        for b in r:
            if b.lw:
                self._w_eng(e, waits, *b.lw)
            self._w_evs(e, waits, b.dw)
        for b in w:
            if b.lw and not (id(b) in weak and b.lw[0] == e):
                self._w_eng(e, waits, *b.lw)
            for src, s in b.rd.items():
                self._w_eng(e, waits, src, s)
            self._w_evs(e, waits, b.dw)
            self._w_evs(e, waits, b.dr)
        if inc:
            self.seq[e] += 1
            s = self.seq[e]
        else:
            s = self.seq[e] + 1
        self.esem(e, (s - 1) // CAP)
        self.q[e].append((list(waits.values()), fn, None, s if inc else -1))
        for b in r:
            if b.rd.get(e, 0) < s:
                b.rd[e] = s
        for b in w:
            b.lw = (e, s)
            b.rd = {}
            b.dw = {}
            b.dr = {}
        self.ninst += 1

    def dma(self, e, fname, *args, r=(), w=(), **kw):
        fn = (fname, args, kw)
        if e == 'auto':
            e = r[0].lw[0] if (r and r[0].lw) else 'pool'
            if e == 'pe':
                e = 'pool'
            if e == 'dve':
                e = 'sp'
        waits = {}
        for b in r:
            if b.lw:
                self._w_eng(e, waits, *b.lw)
            self._w_evs(e, waits, b.dw)
        for b in w:
            if b.lw:
                self._w_eng(e, waits, *b.lw)
            for src, s in b.rd.items():
                self._w_eng(e, waits, src, s)
            self._w_evs(e, waits, b.dr)
        i = self.dcount[e]
        self.dcount[e] += 1
        slot = i % NSLOT
        while len(self.dslots[e]) <= slot:
            self.dslots[e].append(self.newsem("d_%s_%d" % (e, len(self.dslots[e]))))
        sem = self.dslots[e][slot]
        val = 16 * (i // NSLOT + 1)
        if val > 16:
            self._w_sem(e, waits, sem, val - 16)
        self.q[e].append((list(waits.values()), fn, sem, 16))
        for b in r:
            b.dr[sem.num] = (sem, val)
        for b in w:
            b.dw[sem.num] = (sem, val)
        self.ninst += 1

    def barrier(self):
        for e in ENGS:
            waits = {}
            for src in ENGS:
                if self.seq[src] > 0 and not (src == e and e == 'pe'):
                    ep = (self.seq[src] - 1) // CAP
                    val = (self.seq[src] - 1) % CAP + 1
                    key = ('e', src, ep)
                    if self.seen[e].get(key, 0) < val:
                        self.seen[e][key] = val
                        waits[key] = (self.esem(src, ep), val)
            for q in ENGS:
                n = self.dcount[q]
                for slot, sem in enumerate(self.dslots[q]):
                    cnt = (n - slot + NSLOT - 1) // NSLOT
                    if cnt > 0:
                        self._w_sem(e, waits, sem, 16 * cnt)
            self.q[e].append((list(waits.values()), None, None, 0))

    def finish(self, e='sp'):
        waits = {}
        for q in ENGS:
            n = self.dcount[q]
            for slot, sem in enumerate(self.dslots[q]):
                cnt = (n - slot + NSLOT - 1) // NSLOT
                if cnt > 0:
                    self._w_sem(e, waits, sem, 16 * cnt)
        self.q[e].append((list(waits.values()), None, None, 0))

    def emit(self):
        nc = self.nc
        engmap = {'pe': 'tensor', 'act': 'scalar', 'dve': 'vector', 'pool': 'gpsimd', 'sp': 'sync'}
        with nc.Block() as block:
            for e in ENGS:
                items = self.q[e]
                if not items:
                    continue

                def body(eng, items=items, e=e):
                    for waits, fn, dsem, s in items:
                        for sem, val in waits:
                            eng.wait_ge(sem, val)
                        if fn is None:
                            continue
                        ins = getattr(eng, fn[0])(*fn[1], **fn[2])
                        if dsem is not None:
                            ins.then_inc(dsem, 16)
                        elif s > 0:
                            ins.then_inc(self.esems[e][(s - 1) // CAP], 1)
                getattr(block, engmap[e])(body)


D = 1024
NCH = 8
T = 3072
TT = 256
NTILE = T // TT
NHALF = NTILE // 2
SEQS = [(0, 256), (256, 256), (512, 256), (768, 256), (1024, 2048)]
RMS_EPS = 1e-6
LN_EPS = 1e-5

VROW = {}
_r = 0
def _vr(name, n):
    global _r
    VROW[name] = _r
    _r += n
_vr('c_ctx', 1); _vr('c', 1)
for _l in range(4):
    _vr('ada_b%d' % _l, 3); _vr('norm_w%d' % _l, 1)
_vr('final_norm_w', 1)
for _j in range(2):
    _vr('conv_b_in%d' % _j, 3); _vr('conv_w_dw%d' % _j, 31); _vr('conv_b_dw%d' % _j, 1)
    _vr('conv_ln_w%d' % _j, 1); _vr('conv_ln_b%d' % _j, 1); _vr('conv_b_out%d' % _j, 1)
_vr('ssd_w_conv', 21); _vr('ssd_b_conv', 3); _vr('ssd_norm_w', 2)
NVEC = _r
assert NVEC <= 128


class KB:
    def __init__(self, nc, es):
        self.nc = nc
        self.es = es
        self.P = Prog(nc, es)
        self.psn = 0
        self.rot = list(range(8))
        self.pan = 0
        self.uid = 0

    def sb(self, name, shape, dt):
        t = self.es.enter_context(self.nc.sbuf_tensor(name, list(shape), dt))
        return t, Buf(name)

    def pool(self, name, shape, dt, n):
        return [self.sb("%s%d" % (name, i), shape, dt) for i in range(n)]

    def init_psum(self):
        self.psb = []
        for i in range(8):
            t = self.es.enter_context(self.nc.psum_tensor("ps%d" % i, [128, 512], F32))
            self.psb.append((t, Buf("ps%d" % i)))

    def ps(self):
        r = self.psb[self.rot[self.psn % len(self.rot)]]
        self.psn += 1
        return r

    def set_acc(self, on):
        self.rot = [4, 5, 6, 7] if on else list(range(8))

    def psacc(self):
        r = self.psb[self.pan % 4]
        self.pan += 1
        return r

    def din(self, name, shape):
        return self.nc.dram_tensor(name, list(shape), F32, kind="ExternalInput").ap()

    def dout(self, name, shape):
        return self.nc.dram_tensor(name, list(shape), F32, kind="ExternalOutput").ap()

    def dscr(self, name, shape, dt):
        return self.nc.dram_tensor(name, list(shape), dt, kind="Internal").ap(), Buf(name)


class Rot:
    def __init__(self, items):
        self.items = items
        self.i = 0

    def next(self):
        r = self.items[self.i % len(self.items)]
        self.i += 1
        return r


def build(nlayers=4, dbg=False):
    nc = bass.Bass("TRN2", target_bir_lowering=False)
    with ExitStack() as es:
        K = KB(nc, es)
        P = K.P
        xp = K.din("xp", [1024, D])
        xs = K.din("xs", [2048, D])
        vtab = K.din("vtab", [128, D])
        cst = K.din("cst", [128, 768])
        ada_w = K.din("ada_w", [4, D, 3 * D])
        conv_w_in = K.din("conv_w_in", [2, D, 3 * D])
        conv_w_out = K.din("conv_w_out", [2, D, D])
        na_w_in = K.din("na_w_in", [D, 4 * D])
        na_w_out = K.din("na_w_out", [D, D])
        ck = K.din("ck", [256, D])
        cv = K.din("cv", [256, D])
        btab = K.din("btab", [5, 16, 5, 128, 128])
        ssd_w_in = K.din("ssd_w_in", [D, 5184])
        ssd_w_out = K.din("ssd_w_out", [2 * D, D])
        ssc = K.din("ssc", [128, 160])
        st_in = K.din("st_in", [2 * 2048, 128])
        y_ctx = K.dout("y_ctx", [1024, D])
        y_lat = K.dout("y_lat", [2048, D])
        nk = K.dout("nk", [1024, D])
        nv = K.dout("nv", [1024, D])
        ns = K.dout("ns", [4 * 2 * 32 * 64, 128])
        U_d, U_b = K.dscr("U_d", [D, T], BF16)
        SZ_d, SZ_b = K.dscr("SZ_d", [D, T], BF16)
        C_d, C_b = K.dscr("C_d", [D, T], F32)
        Q_d, Q_b = K.dscr("Q_d", [D, T], BF16)
        K_d, K_b = K.dscr("K_d", [D, T + 256], BF16)
        V_d, V_b = K.dscr("V_d", [T + 256, D], BF16)
        XBC_d, XBC_b = K.dscr("XBC_d", [3 * D, T], BF16)
        XT_d, XT_b = K.dscr("XT_d", [T, 2 * D], BF16)
        BT_d, BT_b = K.dscr("BT_d", [T, 512], BF16)
        BC_d, BC_b = K.dscr("BC_d", [D, T], BF16)
        ZT_d, ZT_b = K.dscr("ZT_d", [T, 2 * D], BF16)
        DT_d, DT_b = K.dscr("DT_d", [T, 128], F32)
        YF_d, YF_b = K.dscr("YF_d", [T, 2 * D], F32)
        GT_d, GT_b = K.dscr("GT_d", [T, 2 * D], F32)

        K.init_psum()
        X, X_b = K.sb("X", [128, NCH, T], F32)
        Xb = [Buf("X_t%d" % i) for i in range(NTILE)]
        arena, _ = K.sb("arena", [128, 24576], BF16)
        H = arena[:, 0:12288].rearrange("p (c t) -> p c t", c=NCH)
        Hb = [Buf("H_t%d" % i) for i in range(NHALF)]
        cstt, cst_b = K.sb("cst_sb", [128, 768], F32)
        Mle, Mge, Mgt, Mlt = cstt[:, 256:384], cstt[:, 384:512], cstt[:, 512:640], cstt[:, 640:768]
        ssct, ssc_b = K.sb("ssc_sb", [128, 160], F32)
        ident = cstt[:, 0:128]
        ones = cstt[:, 128:256]
        identb, identb_b = K.sb("identb", [128, 128], BF16)
        PV, PV_b = K.sb("PV", [128, NCH, 128], F32)
        MOD, MOD_b = K.sb("MOD", [128, 4, 24, 2], F32)
        AMOD, AMOD_b = K.sb("AMOD", [128, 4, NCH, 2], F32)
        wmem = arena[:, 12288:20480]
        wb0, wb1 = Buf("wb0"), Buf("wb1")
        wrot = Rot([(wmem[:, 0:4096].rearrange("p (k n) -> p k n", k=NCH), wb0), (wmem[:, 4096:8192].rearrange("p (k n) -> p k n", k=NCH), wb1)])
        diag, diag_b = arena[:, 20480:20480 + 31 * 128], Buf("diag")
        onesb, onesb_b = K.sb("onesb", [128, 128], BF16)
        bq = Rot(K.pool("bq", [128, TT], BF16, 4))
        ptp = Rot(K.pool("ptp", [128, 128], BF16, 16))
        t512 = Rot(K.pool("t512", [128, TT], F32, 8))
        stat = Rot(K.pool("stat", [128, TT], F32, 3))
        b512 = Rot(K.pool("b512", [128, TT], BF16, 8))
        big = Rot(K.pool("big", [128, NCH, TT], F32, 3))
        bigb = Rot(K.pool("bigb", [128, NCH, TT], BF16, 2))

        def tile_cond(tt):
            return 0 if tt * TT < 1024 else 1

        P.dma('sp', 'dma_start', out=cstt[:], in_=cst[:, :], w=[cst_b])
        vt, vt_b = big.next()
        vt2 = vt[:].rearrange("p c t -> p (c t)")[:, 0:D]
        P.dma('sp', 'dma_start', out=vt2, in_=vtab[:, :], w=[vt_b])
        P.op('dve', 'tensor_copy', out=identb[:], in_=ident, r=[cst_b], w=[identb_b])
        P.op('dve', 'tensor_copy', out=onesb[:], in_=ones, r=[cst_b], w=[onesb_b])
        for c in range(NCH):
            pt, pb = K.ps()
            P.op('pe', 'matmul', pt[:, 0:128], lhsT=vt2[:, c * 128:(c + 1) * 128], rhs=ident,
                                                      start=True, stop=True, r=[vt_b, cst_b], w=[pb])
            P.op('dve', 'tensor_copy', out=PV[:, c, :], in_=pt[:, 0:128], r=[pb], w=[PV_b])

        def pv(name, off=0):
            r = VROW[name] + off
            return PV[:, :, r]

        for tk in range(T // 128):
            src = xp[tk * 128:(tk + 1) * 128, :] if tk < 8 else xs[(tk - 8) * 128:(tk - 7) * 128, :]
            lt, lb = big.next()
            lt2 = lt[:].rearrange("p c t -> p (c t)")[:, 0:D]
            P.dma('sp', 'dma_start', out=lt2, in_=src, w=[lb])
            for half in range(2):
                pt, pb = K.ps()
                for q in range(4):
                    c = half * 4 + q
                    P.op('pe', 'transpose', pt[:, q * 128:(q + 1) * 128],
                                                                               lt2[:, c * 128:(c + 1) * 128], ident,
                         r=[lb, cst_b], w=[pb])
                dst = X[:, half * 4:half * 4 + 4, tk * 128:(tk + 1) * 128]
                srcp = pt[:].rearrange("p (q t) -> p q t", t=128)
                eng = 'act' if half == 0 else 'dve'
                if eng == 'act':
                    P.op('act', 'activation', out=dst, in_=srcp, func=AF.Copy,
                         r=[pb], w=[Xb[tk * 128 // TT]])
                else:
                    P.op('dve', 'tensor_copy', out=dst, in_=srcp,
                         r=[pb], w=[Xb[tk * 128 // TT]])

        scb, scb_b = K.sb("scb", [128, NCH, 2], BF16)
        P.op('act', 'activation', out=scb[:], in_=PV[:, :, 0:2], func=AF.Silu, r=[PV_b], w=[scb_b])
        for l in range(nlayers):
            for blk in range(6):
                wt, wb = wrot.next()
                P.dma('pool', 'dma_start',
                    out=wt[:], in_=ada_w[l, :, blk * 512:(blk + 1) * 512].rearrange("(k p) n -> p k n", p=128),
                    w=[wb])
                pt, pb = K.ps()
                for m in range(4):
                    for k in range(NCH):
                        P.op('pe', 'matmul',
                            pt[:, m * 2:m * 2 + 2], lhsT=wt[:, k, m * 128:(m + 1) * 128], rhs=scb[:, k, :],
                            start=(k == 0), stop=(k == NCH - 1), r=[wb, scb_b], w=[pb], inc=(k == NCH - 1))
                for m in range(4):
                    ch = blk * 4 + m
                    bias = PV[:, ch % 8, VROW['ada_b%d' % l] + ch // 8:VROW['ada_b%d' % l] + ch // 8 + 1]
                    P.op('dve', 'tensor_scalar',
                        out=MOD[:, l, ch, :], in0=pt[:, m * 2:m * 2 + 2], scalar1=bias, scalar2=None, op0=ALU.add,
                        r=[pb, PV_b], w=[MOD_b])
            nw = pv('norm_w%d' % l)
            for j in range(2):
                P.op('dve', 'scalar_tensor_tensor',
                    out=AMOD[:, l, :, j], in0=MOD[:, l, 8:16, j], scalar=1.0, in1=nw, op0=ALU.add, op1=ALU.mult,
                    r=[MOD_b, PV_b], w=[AMOD_b])

        def colsum_rstd(src3, src_b, eps, out_t, out_b, scale=1.0 / D):
            n = src3.shape[2]
            sq, sq_b = big.next()
            P.op('act', 'activation', out=sq[:, :, 0:n], in_=src3, func=AF.Square, r=[src_b], w=[sq_b])
            pt, pb = K.ps()
            for k in range(NCH):
                P.op('pe', 'matmul', pt[:, 0:n], lhsT=ones, rhs=sq[:, k, 0:n], start=(k == 0),
                                                   stop=(k == NCH - 1), r=[sq_b, cst_b], w=[pb], inc=(k == NCH - 1))
            P.op('act', 'activation', out=out_t[:, 0:n], in_=pt[:, 0:n], func=AF.Sqrt, scale=scale, bias=eps,
                 r=[pb], w=[out_b])
            P.op('dve', 'reciprocal', out=out_t[:, 0:n], in_=out_t[:, 0:n], r=[out_b], w=[out_b])

        def modnorm(l, half):
            for tt in range(half * NHALF, (half + 1) * NHALF):
                j = tile_cond(tt)
                sl = slice(tt * TT, (tt + 1) * TT)
                rs, rs_b = stat.next()
                colsum_rstd(X[:, :, sl], Xb[tt], RMS_EPS, rs, rs_b)
                for k in range(NCH):
                    tmp, tmp_b = t512.next()
                    P.op('dve', 'scalar_tensor_tensor',
                        out=tmp[:], in0=X[:, k, sl], scalar=AMOD[:, l, k, j:j + 1], in1=rs[:], op0=ALU.mult,
                        op1=ALU.mult, r=[Xb[tt], AMOD_b, rs_b], w=[tmp_b])
                    P.op('act', 'activation',
                        out=H[:, k, (tt - half * NHALF) * TT:(tt - half * NHALF + 1) * TT], in_=tmp[:], func=AF.Identity,
                        bias=MOD[:, l, k, j:j + 1], scale=1.0, r=[tmp_b, MOD_b], ww=[Hb[tt - half * NHALF]])

        def load_w(wsrc, cols):
            wt, wb = wrot.next()
            off = 0
            for c0, n in cols:
                P.dma('pool', 'dma_start',
                    out=wt[:, :, off:off + n], in_=wsrc[:, c0:c0 + n].rearrange("(k p) n -> p k n", p=128), w=[wb])
                off += n
            return wt, wb

        def residual_update(l, pt, pb, m, tt, bias_ap, n0=0, n=256):
            j = tile_cond(tt)
            sl = slice(tt * TT + n0, tt * TT + n0 + n)
            tmp, tmp_b = t512.next()
            if bias_ap is not None:
                P.op('dve', 'tensor_scalar', out=tmp[:, 0:n], in0=pt[:, 0:n], scalar1=bias_ap,
                                                      scalar2=MOD[:, l, 16 + m, j:j + 1], op0=ALU.add, op1=ALU.mult,
                     r=[pb, PV_b, MOD_b], w=[tmp_b])
            else:
                P.op('dve', 'tensor_scalar', out=tmp[:, 0:n], in0=pt[:, 0:n],
                                                      scalar1=MOD[:, l, 16 + m, j:j + 1], scalar2=None, op0=ALU.mult,
                     r=[pb, MOD_b], w=[tmp_b])
            P.op('pool', 'tensor_tensor', out=X[:, m, sl], in0=X[:, m, sl], in1=tmp[:, 0:n], op=ALU.add,
                 r=[tmp_b], ww=[Xb[tt]])

        def conv_layer(l, j):
            win = conv_w_in[j]
            r_bin = VROW['conv_b_in%d' % j]
            r_wdw = VROW['conv_w_dw%d' % j]
            if dbg == 3:
                P.op('dve', 'tensor_copy', out=X[:, 0, 0:48], in_=MOD[:, 0, :, :].rearrange("p a b -> p (a b)"), r=[MOD_b], w=[Xb[0]])
                P.op('dve', 'tensor_copy', out=X[:, 0, 48:64], in_=AMOD[:, 0, :, :].rearrange("p a b -> p (a b)"), r=[AMOD_b], w=[Xb[0]])
                rs, rs_b = stat.next()
                colsum_rstd(X[:, :, 256:512], Xb[1], RMS_EPS, rs, rs_b)
                P.op('dve', 'tensor_copy', out=X[:, 1, 0:256], in_=rs[:], r=[rs_b], w=[Xb[0]])
                P.op('dve', 'tensor_copy', out=X[:, 2, 0:128], in_=PV[:, 0, :], r=[PV_b], w=[Xb[0]])
                return
            if dbg == 2:
                modnorm(l, 0)
                for tt in range(NHALF):
                    for k in range(NCH):
                        P.op('dve', 'tensor_copy', out=X[:, k, tt * TT:(tt + 1) * TT], in_=H[:, k, tt * TT:(tt + 1) * TT],
                             r=[Hb[tt]], w=[Xb[tt]])
                return
            for half, m in [(hf, mm) for hf in range(2) for mm in range(NCH)]:
                if m == 0:
                    modnorm(l, half)
                wt, wb = load_w(win, [(m * 128, 128), (D + m * 128, 128), (2 * D + m * 128, 128)])
                bv = PV[:, m, r_bin:r_bin + 1]
                bg = PV[:, m, r_bin + 1:r_bin + 2]
                bz = PV[:, m, r_bin + 2:r_bin + 3]
                for tp in range(NHALF // 2):
                    hl = slice(tp * 512, (tp + 1) * 512)
                    hbs = [Hb[2 * tp], Hb[2 * tp + 1]]
                    pss = [K.ps() for _ in range(3)]
                    for q in range(3):
                        for k in range(NCH):
                            P.op('pe', 'matmul', pss[q][0][:], lhsT=wt[:, k, q * 128:(q + 1) * 128], rhs=H[:, k, hl],
                                 start=(k == 0), stop=(k == NCH - 1), r=[wb] + hbs, w=[pss[q][1]], inc=(k == NCH - 1))
                    for sub in range(2):
                        tt = half * NHALF + 2 * tp + sub
                        sl = slice(tt * TT, (tt + 1) * TT)
                        cs = slice(sub * TT, (sub + 1) * TT)
                        sg, sg_b = t512.next()
                        P.op('act', 'activation', out=sg[:], in_=pss[1][0][:, cs], func=AF.Sigmoid, bias=bg, scale=1.0,
                             r=[pss[1][1], PV_b], w=[sg_b])
                        ut, ut_b = b512.next()
                        P.op('dve', 'scalar_tensor_tensor', out=ut[:], in0=pss[0][0][:, cs], scalar=bv, in1=sg[:], op0=ALU.add,
                             op1=ALU.mult, r=[pss[0][1], sg_b, PV_b], w=[ut_b])
                        zb_, zb_b_ = t512.next()
                        P.op('act', 'activation', out=zb_[:], in_=pss[2][0][:, cs], func=AF.Identity, bias=bz, scale=1.0,
                             r=[pss[2][1], PV_b], w=[zb_b_])
                        sz_, sz_b_ = t512.next()
                        P.op('act', 'activation', out=sz_[:], in_=zb_[:], func=AF.Sigmoid, r=[zb_b_], w=[sz_b_])
                        zt, zt_b = b512.next()
                        P.op('dve', 'tensor_tensor', out=zt[:], in0=zb_[:], in1=sz_[:], op=ALU.mult, r=[zb_b_, sz_b_], w=[zt_b])
                        P.dma('auto', 'dma_start', out=U_d[m * 128:(m + 1) * 128, sl], in_=ut[:], r=[ut_b], w=[U_b])
                        P.dma('auto', 'dma_start', out=SZ_d[m * 128:(m + 1) * 128, sl], in_=zt[:], r=[zt_b], w=[SZ_b])
            P.barrier()
            dg2 = (arena[:, 0:31 * 128], Buf("diag2_%d" % l))
            for m in range(NCH):
                dgv, dg_b = (diag, diag_b) if m % 2 == 0 else dg2
                for k in range(31):
                    P.op('dve', 'tensor_scalar',
                        out=dgv[:, k * 128:(k + 1) * 128], in0=ident, scalar1=PV[:, m, r_wdw + k:r_wdw + k + 1],
                        scalar2=None, op0=ALU.mult, r=[cst_b, PV_b], ww=[dg_b])
                bdw = PV[:, m, VROW['conv_b_dw%d' % j]:VROW['conv_b_dw%d' % j] + 1]
                for (s0, L) in SEQS:
                    for t0 in range(s0, s0 + L, TT):
                        n = min(TT, s0 + L - t0)
                        lo = max(t0 - 15, s0)
                        hi = min(t0 + n + 15, s0 + L)
                        uu, uu_b = bigb.next()
                        uv = uu[:].rearrange("p c t -> p (c t)")
                        if lo > t0 - 15 or hi < t0 + n + 15:
                            P.op('dve', 'memset', uv[:, 0:n + 30], 0.0, w=[uu_b])
                        P.dma('sp', 'dma_start',
                            out=uv[:, lo - (t0 - 15):hi - (t0 - 15)], in_=U_d[m * 128:(m + 1) * 128, lo:hi],
                            r=[U_b], w=[uu_b])
                        pt, pb = K.ps()
                        for k in range(31):
                            P.op('pe', 'matmul',
                                pt[:, 0:n], lhsT=dgv[:, k * 128:(k + 1) * 128], rhs=uv[:, k:k + n],
                                start=(k == 0), stop=(k == 30), r=[dg_b, uu_b], w=[pb], inc=(k == 30))
                        ct, ct_b = t512.next()
                        P.op('act', 'activation', out=ct[:, 0:n], in_=pt[:, 0:n],
                                                                              func=AF.Identity, bias=bdw, scale=1.0,
                             r=[pb, PV_b], w=[ct_b])
                        P.dma('auto', 'dma_start',
                            out=C_d[m * 128:(m + 1) * 128, t0:t0 + n], in_=ct[:, 0:n], r=[ct_b], w=[C_b])
            P.barrier()
            gts3 = Rot([(arena[:, q_ * 2048:(q_ + 1) * 2048].rearrange("p (k n) -> p k n", n=TT), Buf("gt3_%d" % q_)) for q_ in range(3)])
            wo_t = wmem.rearrange("p (k n) -> p k n", k=NCH)
            for hh in range(2):
                P.dma('pool', 'dma_start',
                    out=wo_t[:, :, hh * 512:(hh + 1) * 512],
                    in_=conv_w_out[j][:, hh * 512:(hh + 1) * 512].rearrange("(k p) n -> p k n", p=128), w=[wb0, wb1])
            lnw = VROW['conv_ln_w%d' % j]
            lnb = VROW['conv_ln_b%d' % j]
            bo = VROW['conv_b_out%d' % j]
            for tt in range(NTILE):
                sl = slice(tt * TT, (tt + 1) * TT)
                ct, ct_b = big.next()
                P.dma('sp', 'dma_start',
                    out=ct[:], in_=C_d[:, sl].rearrange("(k p) n -> p k n", p=128), r=[C_b], w=[ct_b])
                zt, zt_b = bigb.next()
                P.dma('sp', 'dma_start',
                    out=zt[:], in_=SZ_d[:, sl].rearrange("(k p) n -> p k n", p=128), r=[SZ_b], w=[zt_b])
                pm, pm_b = K.ps()
                for k in range(NCH):
                    P.op('pe', 'matmul', pm[:, 0:TT], lhsT=ones, rhs=ct[:, k, :], start=(k == 0),
                                                                     stop=(k == NCH - 1), r=[ct_b, cst_b], w=[pm_b], inc=(k == NCH - 1))
                mean, mean_b = stat.next()
                P.op('dve', 'tensor_scalar', out=mean[:], in0=pm[:, 0:TT], scalar1=1.0 / D,
                                                                        scalar2=None, op0=ALU.mult, r=[pm_b], w=[mean_b])
                P.op('dve', 'tensor_tensor',
                    out=ct[:], in0=ct[:], in1=mean[:].unsqueeze(1).to_broadcast([128, NCH, TT]), op=ALU.subtract,
                    r=[ct_b, mean_b], w=[ct_b])
                rs, rs_b = stat.next()
                colsum_rstd(ct[:], ct_b, LN_EPS, rs, rs_b)
                gt, gt_b = gts3.next()
                for k in range(NCH):
                    tmp, tmp_b = t512.next()
                    P.op('dve', 'scalar_tensor_tensor',
                        out=tmp[:], in0=ct[:, k, :], scalar=PV[:, k, lnw:lnw + 1], in1=rs[:], op0=ALU.mult, op1=ALU.mult,
                        r=[ct_b, rs_b, PV_b], w=[tmp_b])
                    P.op('act', 'activation', out=tmp[:], in_=tmp[:], func=AF.Silu,
                                                                      bias=PV[:, k, lnb:lnb + 1], scale=1.0,
                         r=[tmp_b, PV_b], w=[tmp_b])
                    P.op('dve', 'tensor_tensor',
                        out=gt[:, k, :], in0=tmp[:], in1=zt[:, k, :], op=ALU.mult, r=[tmp_b, zt_b], ww=[gt_b])
                for m in range(NCH):
                    pt, pb = K.ps()
                    for k in range(NCH):
                        P.op('pe', 'matmul',
                            pt[:, 0:TT], lhsT=wo_t[:, k, m * 128:(m + 1) * 128], rhs=gt[:, k, :], start=(k == 0),
                            stop=(k == NCH - 1), r=[wb0, wb1, gt_b], w=[pb], inc=(k == NCH - 1))
                    residual_update(l, pt, pb, m, tt, PV[:, m, bo:bo + 1])
            P.barrier()


        def rs_row(qr):
            return min(max(qr - 4, 0), 24)

        def na_blocks():
            blocks = []
            for sq in range(4):
                for hq in range(2):
                    blocks.append((sq * 256 + hq * 128, 128, [2 * sq, 2 * sq + 1], None, 0))
            for bi in range(16):
                r = 2 * bi
                lo = rs_row(r)
                hi = rs_row(r + 1) + 8
                tiles = list(range(lo // 2, (hi - 1) // 2 + 1))
                cls = {0: 0, 2: 1, 28: 3, 30: 4}.get(r, 2)
                blocks.append((1024 + r * 64, 128, [8 + t for t in tiles], cls, len(tiles)))
            return blocks

        def na_layer(l):
            kc_t, kc_b = bigb.next()
            for kt in range(2):
                lt, lb = big.next()
                lt2 = lt[:].rearrange("p c t -> p (c t)")[:, 0:D]
                P.dma('sp', 'dma_start', out=lt2, in_=ck[kt * 128:(kt + 1) * 128, :], w=[lb])
                for half in range(2):
                    pt, pb = K.ps()
                    for q in range(4):
                        c = half * 4 + q
                        P.op('pe', 'transpose', pt[:, q * 128:(q + 1) * 128], lt2[:, c * 128:(c + 1) * 128], ident,
                             r=[lb, cst_b], w=[pb])
                    P.op('dve', 'tensor_copy', out=kc_t[:, half * 4:half * 4 + 4, kt * 128:(kt + 1) * 128],
                         in_=pt[:].rearrange("p (q t) -> p q t", t=128), r=[pb], w=[kc_b])
            P.dma('auto', 'dma_start', out=K_d[:, T:T + 256].rearrange("(c p) n -> p c n", p=128), in_=kc_t[:],
                  r=[kc_b], w=[K_b])
            for kt in range(2):
                vb, vb_b = bigb.next()
                vb2 = vb[:].rearrange("p c t -> p (c t)")[:, 0:D]
                P.dma('pool', 'dma_start', out=vb2, in_=cv[kt * 128:(kt + 1) * 128, :], w=[vb_b])
                P.dma('auto', 'dma_start', out=V_d[T + kt * 128:T + (kt + 1) * 128, :], in_=vb2, r=[vb_b], w=[V_b])
            for half in range(2):
                modnorm(l, half)
                for sec, dstd, dst_b in ((0, Q_d, Q_b), (1, K_d, K_b), (3, SZ_d, SZ_b)):
                    for hb in range(2):
                        wt, wb = load_w(na_w_in, [(sec * D + hb * 512, 512)])
                        for tp in range(NHALF // 2):
                            hl = slice(tp * 512, (tp + 1) * 512)
                            hbs = [Hb[2 * tp], Hb[2 * tp + 1]]
                            for mm in range(4):
                                pt, pb = K.ps()
                                for k in range(NCH):
                                    P.op('pe', 'matmul', pt[:], lhsT=wt[:, k, mm * 128:(mm + 1) * 128], rhs=H[:, k, hl],
                                         start=(k == 0), stop=(k == NCH - 1), r=[wb] + hbs, w=[pb], inc=(k == NCH - 1))
                                ch = hb * 4 + mm
                                for sub in range(2):
                                    tt = half * NHALF + 2 * tp + sub
                                    sl = slice(tt * TT, (tt + 1) * TT)
                                    cs = slice(sub * TT, (sub + 1) * TT)
                                    ot, ot_b = b512.next()
                                    if sec == 3:
                                        P.op('act', 'activation', out=ot[:], in_=pt[:, cs], func=AF.Silu, r=[pb], w=[ot_b])
                                    elif mm % 2 == 0:
                                        P.op('act', 'activation', out=ot[:], in_=pt[:, cs], func=AF.Copy, r=[pb], w=[ot_b])
                                    else:
                                        P.op('dve', 'tensor_copy', out=ot[:], in_=pt[:, cs], r=[pb], w=[ot_b])
                                    P.dma('auto', 'dma_start', out=dstd[ch * 128:(ch + 1) * 128, sl], in_=ot[:], r=[ot_b], w=[dst_b])
                for sec in (1, 2):
                    if sec == 1 and half == 1:
                        continue
                    for hb in range(2):
                        wt, wb = load_w(na_w_in, [(sec * D + hb * 512, 512)])
                        for tk in range(half * 12, (half + 1) * 12):
                            if sec == 1 and tk >= 8:
                                continue
                            lk = tk - half * 12
                            pt, pb = K.ps()
                            for k in range(NCH):
                                P.op('pe', 'matmul', pt[:], lhsT=H[:, k, lk * 128:(lk + 1) * 128], rhs=wt[:, k, :],
                                     start=(k == 0), stop=(k == NCH - 1), r=[wb, Hb[lk * 128 // TT]], w=[pb], inc=(k == NCH - 1))
                            src_ap, src_b = pt[:], pb
                            if tk < 8:
                                ot, ot_b = big.next()
                                ot2 = ot[:].rearrange("p c t -> p (c t)")[:, 0:512]
                                P.op('act', 'activation', out=ot2, in_=pt[:], func=AF.Copy, r=[pb], w=[ot_b])
                                dst = nk if sec == 1 else nv
                                P.dma('auto', 'dma_start', out=dst[tk * 128:(tk + 1) * 128, hb * 512:(hb + 1) * 512], in_=ot2, r=[ot_b])
                                src_ap, src_b = ot2, ot_b
                            if sec == 2:
                                vb, vb_b = bigb.next()
                                vb2 = vb[:].rearrange("p c t -> p (c t)")[:, 0:512]
                                P.op('dve', 'tensor_copy', out=vb2, in_=src_ap, r=[src_b], w=[vb_b])
                                P.dma('auto', 'dma_start', out=V_d[tk * 128:(tk + 1) * 128, hb * 512:(hb + 1) * 512], in_=vb2,
                                      r=[vb_b], w=[V_b])
            wo_t = wmem.rearrange("p (k n) -> p k n", k=NCH)
            for hh in range(2):
                P.dma('pool', 'dma_start', out=wo_t[:, :, hh * 512:(hh + 1) * 512],
                      in_=na_w_out[:, hh * 512:(hh + 1) * 512].rearrange("(k p) n -> p k n", p=128), w=[wb0, wb1])
            P.barrier()
            K.set_acc(True)
            qT = arena[:, 0:T]
            kT = arena[:, T:2 * T + 256]
            vT = arena[:, 2 * T + 256:3 * T + 512].rearrange("p (t n) -> p t n", n=128)
            for c in range(NCH):
                qT_b, kT_b, vT_b = Buf("qT%d" % c), Buf("kT%d" % c), Buf("vT%d" % c)
                if c > 0:
                    P.barrier()
                P.dma('sp', 'dma_start', out=qT, in_=Q_d[c * 128:(c + 1) * 128, :], r=[Q_b], w=[qT_b])
                P.dma('sp', 'dma_start', out=kT, in_=K_d[c * 128:(c + 1) * 128, :], r=[K_b], w=[kT_b])
                P.dma('sp', 'dma_start', out=vT, in_=V_d[:, c * 128:(c + 1) * 128].rearrange("(t p) n -> p t n", p=128),
                      r=[V_b], w=[vT_b])
                for (q0, N, tiles, cls, nloc) in na_blocks():
                    szt, szt_b = bq.next()
                    P.dma('sp', 'dma_start', out=szt[:, 0:N], in_=SZ_d[c * 128:(c + 1) * 128, q0:q0 + N], r=[SZ_b], w=[szt_b])
                    gt, gt_b = bq.next()
                    alltiles = list(tiles) + ([24, 25] if cls is not None else [])
                    pts = {}
                    bts = {}
                    for hh in range(2):
                        R = slice(hh * 64, hh * 64 + 64)
                        hd = 2 * c + hh
                        if cls is not None:
                            bt, bt_b = big.next()
                            bt3 = bt[:].rearrange("p c t -> p (c t)")[:, 0:nloc * 128].rearrange("p (t q) -> p t q", q=128)
                            P.dma('sp', 'dma_start', out=bt3, in_=btab[cls, hd, 0:nloc].rearrange("t k q -> k t q"), w=[bt_b])
                        for ti, tile in enumerate(alltiles):
                            ps_, ps_b = K.ps()
                            P.op('pe', 'matmul', ps_[:, 0:N], lhsT=kT[R, tile * 128:(tile + 1) * 128], rhs=qT[R, q0:q0 + N],
                                 start=True, stop=True, r=[kT_b, qT_b], w=[ps_b])
                            pT, pT_b = ptp.next()
                            pts[(hh, ti)] = (pT, pT_b)
                            if cls is not None and ti < nloc:
                                ssb, ssb_b = t512.next()
                                P.op('dve', 'scalar_tensor_tensor', out=ssb[:, 0:N], in0=ps_[:, 0:N], scalar=0.125,
                                     in1=bt3[:, ti, :], op0=ALU.mult, op1=ALU.add, r=[ps_b, bt_b], w=[ssb_b])
                                P.op('act', 'activation', out=pT[:, 0:N], in_=ssb[:, 0:N], func=AF.Exp, r=[ssb_b], w=[pT_b])
                            else:
                                P.op('act', 'activation', out=pT[:, 0:N], in_=ps_[:, 0:N], func=AF.Exp, scale=0.125,
                                     r=[ps_b], w=[pT_b])
                    for hh in range(2):
                        R = slice(hh * 64, hh * 64 + 64)
                        po, po_b = K.psacc()
                        pd, pd_b = K.psacc()
                        for ti, tile in enumerate(alltiles):
                            pT, pT_b = pts[(hh, ti)]
                            first, last = (ti == 0), (ti == len(alltiles) - 1)
                            P.op('pe', 'matmul', po[:, 0:N], lhsT=vT[:, tile, :], rhs=pT[:, 0:N], start=first, stop=last,
                                 r=[vT_b, pT_b], w=[po_b], inc=last)
                        for ti, tile in enumerate(alltiles):
                            pT, pT_b = pts[(hh, ti)]
                            first, last = (ti == 0), (ti == len(alltiles) - 1)
                            P.op('pe', 'matmul', pd[:, 0:N], lhsT=onesb[:], rhs=pT[:, 0:N], start=first, stop=last,
                                 r=[onesb_b, pT_b], w=[pd_b], inc=last)
                        rd, rd_b = t512.next()
                        P.op('dve', 'reciprocal', out=rd[R, 0:N], in_=pd[R, 0:N], r=[pd_b], w=[rd_b])
                        oo, oo_b = t512.next()
                        P.op('dve', 'tensor_tensor', out=oo[R, 0:N], in0=po[R, 0:N], in1=rd[R, 0:N], op=ALU.mult,
                             r=[po_b, rd_b], w=[oo_b])
                        P.op('dve', 'tensor_tensor', out=gt[R, 0:N], in0=oo[R, 0:N], in1=szt[R, 0:N], op=ALU.mult,
                             r=[oo_b, szt_b], ww=[gt_b])
                    P.dma('pool', 'dma_start', out=U_d[c * 128:(c + 1) * 128, q0:q0 + N], in_=gt[:, 0:N], r=[gt_b], w=[U_b])
            K.set_acc(False)
            P.barrier()
            for tt in range(NTILE):
                sl = slice(tt * TT, (tt + 1) * TT)
                gt, gt_b = bigb.next()
                P.dma('sp', 'dma_start', out=gt[:], in_=U_d[:, sl].rearrange("(k p) n -> p k n", p=128), r=[U_b], w=[gt_b])
                for m in range(NCH):
                    pt, pb = K.ps()
                    for k in range(NCH):
                        P.op('pe', 'matmul', pt[:, 0:TT], lhsT=wo_t[:, k, m * 128:(m + 1) * 128], rhs=gt[:, k, :],
                             start=(k == 0), stop=(k == NCH - 1), r=[wb0, wb1, gt_b], w=[pb], inc=(k == NCH - 1))
                    residual_update(l, pt, pb, m, tt, None)

        gTall_b = Buf('gTall')
        def ssd_layer(l):
            W = ssd_w_in
            r_wc = VROW['ssd_w_conv']
            r_bc = VROW['ssd_b_conv']
            P.dma('sp', 'dma_start', out=ssct[:], in_=ssc[:, :], w=[ssc_b])
            P.op('act', 'activation', out=ssct[:, 64:128], in_=ssct[:, 64:128], func=AF.Exp, r=[ssc_b], w=[ssc_b])
            P.op('dve', 'tensor_scalar', out=ssct[:, 64:128], in0=ssct[:, 64:128], scalar1=-1.0, scalar2=None, op0=ALU.mult,
                 r=[ssc_b], w=[ssc_b])
            dtb, Abc, Dbc = ssct[:, 0:64], ssct[:, 64:128], ssct[:, 128:160]
            for half in range(2):
                modnorm(l, half)
                for blk in range(4):
                    wt, wb = load_w(W, [(blk * 512, 512)])
                    for lk in range(12):
                        tk = half * 12 + lk
                        pt, pb = K.ps()
                        for k in range(NCH):
                            P.op('pe', 'matmul', pt[:], lhsT=H[:, k, lk * 128:(lk + 1) * 128], rhs=wt[:, k, :],
                                 start=(k == 0), stop=(k == NCH - 1), r=[wb, Hb[lk * 128 // TT]], w=[pb], inc=(k == NCH - 1))
                        zb, zb_b = bigb.next()
                        zb2 = zb[:].rearrange("p c t -> p (c t)")[:, 0:512]
                        P.op('act', 'activation', out=zb2, in_=pt[:], func=AF.Silu, r=[pb], w=[zb_b])
                        P.dma('auto', 'dma_start', out=ZT_d[tk * 128:(tk + 1) * 128, blk * 512:(blk + 1) * 512], in_=zb2,
                              r=[zb_b], w=[ZT_b])
                for blk in range(6):
                    wt, wb = load_w(W, [(2048 + blk * 512, 512)])
                    for tp in range(NHALF // 2):
                        hl = slice(tp * 512, (tp + 1) * 512)
                        hbs = [Hb[2 * tp], Hb[2 * tp + 1]]
                        for mm in range(4):
                            pt, pb = K.ps()
                            for k in range(NCH):
                                P.op('pe', 'matmul', pt[:], lhsT=wt[:, k, mm * 128:(mm + 1) * 128], rhs=H[:, k, hl],
                                     start=(k == 0), stop=(k == NCH - 1), r=[wb] + hbs, w=[pb], inc=(k == NCH - 1))
                            ch = blk * 4 + mm
                            for sub in range(2):
                                tt = half * NHALF + 2 * tp + sub
                                sl = slice(tt * TT, (tt + 1) * TT)
                                cs = slice(sub * TT, (sub + 1) * TT)
                                ot, ot_b = b512.next()
                                if mm % 2 == 0:
                                    P.op('act', 'activation', out=ot[:], in_=pt[:, cs], func=AF.Copy, r=[pb], w=[ot_b])
                                else:
                                    P.op('dve', 'tensor_copy', out=ot[:], in_=pt[:, cs], r=[pb], w=[ot_b])
                                P.dma('auto', 'dma_start', out=XBC_d[ch * 128:(ch + 1) * 128, sl], in_=ot[:], r=[ot_b], w=[XBC_b])
                wt, wb = load_w(W, [(5120, 64)])
                for lk in range(12):
                    tk = half * 12 + lk
                    pt, pb = K.ps()
                    for k in range(NCH):
                        P.op('pe', 'matmul', pt[:, 0:64], lhsT=H[:, k, lk * 128:(lk + 1) * 128], rhs=wt[:, k, 0:64],
                             start=(k == 0), stop=(k == NCH - 1), r=[wb, Hb[lk * 128 // TT]], w=[pb], inc=(k == NCH - 1))
                    dta, dta_b = t512.next()
                    P.op('dve', 'tensor_tensor', out=dta[:, 0:64], in0=pt[:, 0:64], in1=dtb, op=ALU.add, r=[pb, ssc_b], w=[dta_b])
                    P.op('act', 'activation', out=dta[:, 0:64], in_=dta[:, 0:64], func=AF.Exp, r=[dta_b], w=[dta_b])
                    P.op('act', 'activation', out=dta[:, 0:64], in_=dta[:, 0:64], func=AF.Ln, bias=1.0, scale=1.0,
                         r=[dta_b], w=[dta_b])
                    P.op('dve', 'tensor_tensor', out=dta[:, 64:128], in0=dta[:, 0:64], in1=Abc, op=ALU.mult,
                         r=[dta_b, ssc_b], w=[dta_b])
                    P.dma('auto', 'dma_start', out=DT_d[tk * 128:(tk + 1) * 128, :], in_=dta[:, 0:128], r=[dta_b], w=[DT_b])
            for m in range(24):
                for k in range(7):
                    P.op('dve', 'tensor_scalar', out=diag[:, k * 128:(k + 1) * 128], in0=ident,
                         scalar1=PV[:, m % 8, r_wc + 3 * k + m // 8:r_wc + 3 * k + m // 8 + 1], scalar2=None, op0=ALU.mult,
                         r=[cst_b, PV_b], ww=[diag_b])
                bcv = PV[:, m % 8, r_bc + m // 8:r_bc + m // 8 + 1]
                P.op('dve', 'tensor_scalar', out=diag[:, 7 * 128:8 * 128], in0=ident, scalar1=bcv, scalar2=None, op0=ALU.mult,
                     r=[cst_b, PV_b], ww=[diag_b])
                for (s0, L) in SEQS:
                    for t0 in range(s0, s0 + L, TT):
                        n = TT
                        lo = max(t0 - 3, s0)
                        hi = min(t0 + n + 3, s0 + L)
                        uu, uu_b = bigb.next()
                        uv = uu[:].rearrange("p c t -> p (c t)")
                        if lo > t0 - 3 or hi < t0 + n + 3:
                            P.op('dve', 'memset', uv[:, 0:n + 6], 0.0, w=[uu_b])
                        P.dma('sp', 'dma_start', out=uv[:, lo - (t0 - 3):hi - (t0 - 3)], in_=XBC_d[m * 128:(m + 1) * 128, lo:hi],
                              r=[XBC_b], w=[uu_b])
                        if m < 20:
                            for sub in range(n // 128):
                                pt, pb = K.ps()
                                for k in range(7):
                                    P.op('pe', 'matmul', pt[:, 0:128], lhsT=uv[:, k + sub * 128:k + sub * 128 + 128],
                                         rhs=diag[:, k * 128:(k + 1) * 128], start=(k == 0), stop=False, r=[uu_b, diag_b], w=[pb], inc=False)
                                P.op('pe', 'matmul', pt[:, 0:128], lhsT=onesb[:], rhs=diag[:, 7 * 128:8 * 128], start=False, stop=True,
                                     r=[onesb_b, diag_b], w=[pb])
                                ot, ot_b = b512.next()
                                P.op('act', 'activation', out=ot[:, 0:128], in_=pt[:, 0:128], func=AF.Silu, r=[pb], w=[ot_b])
                                tok = t0 + sub * 128
                                if m < 16:
                                    P.dma('auto', 'dma_start', out=XT_d[tok:tok + 128, m * 128:(m + 1) * 128], in_=ot[:, 0:128],
                                          r=[ot_b], w=[XT_b])
                                else:
                                    P.dma('auto', 'dma_start', out=BT_d[tok:tok + 128, (m - 16) * 128:(m - 15) * 128], in_=ot[:, 0:128],
                                          r=[ot_b], w=[BT_b])
                        if m >= 16:
                            pt, pb = K.ps()
                            for k in range(7):
                                P.op('pe', 'matmul', pt[:, 0:n], lhsT=diag[:, k * 128:(k + 1) * 128], rhs=uv[:, k:k + n],
                                     start=(k == 0), stop=(k == 6), r=[uu_b, diag_b], w=[pb], inc=(k == 6))
                            ot, ot_b = b512.next()
                            P.op('act', 'activation', out=ot[:, 0:n], in_=pt[:, 0:n], func=AF.Silu, bias=bcv, scale=1.0,
                                 r=[pb, PV_b], w=[ot_b])
                            P.dma('auto', 'dma_start', out=BC_d[(m - 16) * 128:(m - 15) * 128, t0:t0 + n], in_=ot[:, 0:n],
                                  r=[ot_b], w=[BC_b])
            P.barrier()
            hst = arena[:, 0:4096].bitcast(F32)
            hb16 = arena[:, 4096:6144]
            xdt = arena[:, 6144:8192]
            xdte = arena[:, 8192:10240]
            btms = Rot([(arena[:, 10240:10752], Buf("btm0")), (arena[:, 20736:21248], Buf("btm1"))])
            bcTs = Rot([(arena[:, 10752:11776].rearrange("p (c n) -> p c n", n=128), Buf("bcT0")),
                        (arena[:, 21248:22272].rearrange("p (c n) -> p c n", n=128), Buf("bcT1"))])
            sms = Rot([(arena[:, 15360:15616].bitcast(F32), Buf("sm0")), (arena[:, 22272:22528].bitcast(F32), Buf("sm1"))])
            Gm = arena[:, 13312:14336].bitcast(F32).rearrange("p (g i) -> p g i", i=128)
            Mts = Rot([(arena[:, 11776 + 0:11776 + 512].rearrange("p (h i) -> p h i", i=128), Buf("Mt0"))] +
                      [(arena[:, 15616 + q * 512:15616 + (q + 1) * 512].rearrange("p (h i) -> p h i", i=128), Buf("Mt%d" % (q + 1))) for q in range(2)])
            Rfs = Rot([(arena[:, 12288:13312].bitcast(F32).rearrange("p (h i) -> p h i", i=128), Buf("Rf0"))] +
                      [(arena[:, 16640 + q * 1024:16640 + (q + 1) * 1024].bitcast(F32).rearrange("p (h i) -> p h i", i=128), Buf("Rf%d" % (q + 1))) for q in range(2)])
            Lfs = Rot([(arena[:, 14336:15360].bitcast(F32).rearrange("p (h i) -> p h i", i=128), Buf("Lf0"))] +
                      [(arena[:, 18688 + q * 1024:18688 + (q + 1) * 1024].bitcast(F32).rearrange("p (h i) -> p h i", i=128), Buf("Lf%d" % (q + 1))) for q in range(2)])
            xdt_b, xdte_b, Gm_b = [Buf("ssd%d" % i) for i in range(3)]
            hst_bs = [Buf("hst%d" % g) for g in range(4)]
            hb16_bs = [Buf("hb16_%d" % g) for g in range(4)]
            K.set_acc(True)

            pend_yf = []

            def flush_yf():
                while pend_yf:
                    t0_, ys2_, ysb_b_ = pend_yf.pop(0)
                    P.dma('sp', 'dma_start', out=YF_d[t0_:t0_ + 128, :], in_=ys2_, r=ysb_b_, w=[YF_b])

            def chunk(t0, d, final):
                lhs_seg = Mgt if d == 0 else Mlt
                rhs_msk = Mle if d == 0 else Mge
                xt, xt_b = bigb.next()
                xt2 = xt[:].rearrange("p c t -> p (c t)")
                btm, btm_b = btms.next()
                bcT, bcT_b = bcTs.next()
                sm, sm_b = sms.next()
                P.dma('sp', 'dma_start', out=xt2, in_=XT_d[t0:t0 + 128, :], r=[XT_b], w=[xt_b])
                P.dma('sp', 'dma_start', out=btm, in_=BT_d[t0:t0 + 128, :], r=[BT_b], w=[btm_b])
                P.dma('sp', 'dma_start', out=bcT, in_=BC_d[:, t0:t0 + 128].rearrange("(c p) n -> p c n", p=128), r=[BC_b], w=[bcT_b])
                dta, dta_b = t512.next()
                P.dma('sp', 'dma_start', out=dta[:, 0:128], in_=DT_d[t0:t0 + 128, :], r=[DT_b], w=[dta_b])
                flush_yf()
                dt_d = dta[:, d * 32:(d + 1) * 32]
                a_d = dta[:, 64 + d * 32:64 + (d + 1) * 32]
                pt, pb = K.ps()
                P.op('pe', 'matmul', pt[:, 0:32], lhsT=ones, rhs=a_d, start=True, stop=True, r=[cst_b, dta_b], w=[pb])
                P.op('pe', 'matmul', pt[:, 32:64], lhsT=rhs_msk, rhs=a_d, start=True, stop=True, r=[cst_b, dta_b], w=[pb])
                P.op('pe', 'matmul', pt[:, 64:96], lhsT=lhs_seg, rhs=a_d, start=True, stop=True, r=[cst_b, dta_b], w=[pb])
                P.op('act', 'activation', out=sm[:, 0:96], in_=pt[:, 0:96], func=AF.Exp, r=[pb], w=[sm_b])
                P.op('dve', 'tensor_tensor', out=sm[:, 96:128], in0=sm[:, 64:96], in1=dt_d, op=ALU.mult, r=[sm_b, dta_b], w=[sm_b])
                x3 = xt2.rearrange("p (h q) -> p h q", q=64)
                P.op('dve', 'tensor_tensor', out=xdt.rearrange("p (h q) -> p h q", q=64), in0=x3,
                     in1=dt_d.unsqueeze(2).to_broadcast([128, 32, 64]), op=ALU.mult, r=[xt_b, dta_b], w=[xdt_b])
                pg, pg_b = K.ps()
                for g in range(4):
                    P.op('pe', 'matmul', pg[:, g * 128:(g + 1) * 128], lhsT=bcT[:, g, :], rhs=bcT[:, 4 + g, :], start=True, stop=True,
                         r=[bcT_b], w=[pg_b])
                P.op('dve', 'tensor_tensor', out=Gm, in0=pg[:].rearrange("p (g i) -> p g i", i=128),
                     in1=rhs_msk.unsqueeze(1).to_broadcast([128, 4, 128]), op=ALU.mult, r=[pg_b, cst_b], w=[Gm_b])
                ysb, ysb_b = big.next()
                ys2 = ysb[:].rearrange("p c t -> p (c t)")
                ysb_bs = [Buf("ysg%d" % g_) for g_ in range(4)]
                ysb_all = [ysb_b] + ysb_bs
                def seg_part(g, hq):
                    h0 = g * 8 + hq * 4
                    Rf, Rf_b = Rfs.next()
                    Lf, Lf_b = Lfs.next()
                    Mt, Mt_b = Mts.next()
                    P.op('pool', 'tensor_tensor', out=Rf, in0=a_d[:, h0:h0 + 4].unsqueeze(2).to_broadcast([128, 4, 128]),
                         in1=rhs_msk.unsqueeze(1).to_broadcast([128, 4, 128]), op=ALU.mult, r=[dta_b, cst_b], w=[Rf_b])
                    psg, psg_b = K.ps()
                    P.op('pe', 'matmul', psg[:], lhsT=lhs_seg, rhs=Rf.rearrange("p h i -> p (h i)"), start=True, stop=True,
                         r=[cst_b, Rf_b], w=[psg_b])
                    P.op('act', 'activation', out=Lf, in_=psg[:].rearrange("p (h i) -> p h i", i=128), func=AF.Exp,
                         r=[psg_b], w=[Lf_b])
                    P.op('dve', 'tensor_tensor', out=Mt, in0=Lf, in1=Gm[:, g, :].unsqueeze(1).to_broadcast([128, 4, 128]),
                         op=ALU.mult, r=[Lf_b, Gm_b], w=[Mt_b])
                    return Mt, Mt_b

                order = [(g, hq) for g in range(4) for hq in range(2)]
                pend = seg_part(*order[0])
                yd = yd_b = None
                for idx, (g, hq) in enumerate(order):
                    Mt, Mt_b = pend
                    if idx + 1 < len(order):
                        pend = seg_part(*order[idx + 1])
                    if hq == 0:
                        yd, yd_b = K.psacc()
                    h0 = g * 8 + hq * 4
                    for hh in range(4):
                        h = h0 + hh
                        P.op('pe', 'matmul', yd[:, (h % 8) * 64:(h % 8 + 1) * 64], lhsT=Mt[:, hh, :], rhs=xdt[:, h * 64:(h + 1) * 64],
                             start=True, stop=True, r=[Mt_b, xdt_b], w=[yd_b])
                    if hq == 0:
                        continue
                    yo, yo_b = K.psacc()
                    P.op('pe', 'matmul', yo[:], lhsT=bcT[:, 4 + g, :], rhs=hb16[:, g * 512:(g + 1) * 512], start=True, stop=True,
                         r=[bcT_b, hb16_bs[g]], w=[yo_b])
                    ysg = ys2[:, g * 512:(g + 1) * 512]
                    P.op('dve', 'tensor_tensor', out=ysg.rearrange("p (h q) -> p h q", q=64),
                         in0=yo[:].rearrange("p (h q) -> p h q", q=64),
                         in1=sm[:, 32 + g * 8:32 + (g + 1) * 8].unsqueeze(2).to_broadcast([128, 8, 64]), op=ALU.mult,
                         r=[yo_b, sm_b], w=[ysb_bs[g]], ww=[ysb_b])
                    P.op('dve', 'tensor_tensor', out=ysg, in0=ysg, in1=yd[:], op=ALU.add, r=[yd_b], w=[ysb_bs[g]], ww=[ysb_b])
                P.op('pool', 'tensor_tensor', out=xdte.rearrange("p (h q) -> p h q", q=64), in0=x3,
                     in1=sm[:, 96:128].unsqueeze(2).to_broadcast([128, 32, 64]), op=ALU.mult, r=[xt_b, sm_b], w=[xdte_b])
                for g in range(4):
                    pst, pst_b = K.psacc()
                    P.op('pe', 'matmul', pst[:], lhsT=btm[:, g * 128:(g + 1) * 128], rhs=xdte[:, g * 512:(g + 1) * 512], start=True,
                         stop=True, r=[btm_b, xdte_b], w=[pst_b])
                    hg = hst[:, g * 512:(g + 1) * 512]
                    P.op('pool', 'tensor_tensor', out=hg.rearrange("p (h q) -> p h q", q=64), in0=hg.rearrange("p (h q) -> p h q", q=64),
                         in1=sm[:, g * 8:(g + 1) * 8].unsqueeze(2).to_broadcast([128, 8, 64]), op=ALU.mult, r=[sm_b], w=[hst_bs[g]])
                    P.op('dve', 'tensor_tensor', out=hg, in0=hg, in1=pst[:], op=ALU.add, r=[pst_b], w=[hst_bs[g]])
                    P.op('act', 'activation', out=hb16[:, g * 512:(g + 1) * 512], in_=hg, func=AF.Copy, r=[hst_bs[g]], w=[hb16_bs[g]])
                if not final:
                    pend_yf.append((t0, ys2, ysb_all))
                    return
                yf, yf_b = big.next()
                yf2 = yf[:].rearrange("p c t -> p (c t)")
                P.dma('sp', 'dma_start', out=yf2, in_=YF_d[t0:t0 + 128, :], r=[YF_b], w=[yf_b])
                zt, zt_b = bigb.next()
                zt2 = zt[:].rearrange("p c t -> p (c t)")
                P.dma('sp', 'dma_start', out=zt2, in_=ZT_d[t0:t0 + 128, :], r=[ZT_b], w=[zt_b])
                P.op('pool', 'tensor_tensor', out=ys2, in0=ys2, in1=yf2, op=ALU.add, r=[yf_b], w=ysb_all)
                P.op('dve', 'tensor_tensor', out=yf2.rearrange("p (h q) -> p h q", q=64), in0=x3,
                     in1=Dbc.unsqueeze(2).to_broadcast([128, 32, 64]), op=ALU.mult, r=[xt_b, ssc_b], w=[yf_b])
                P.op('dve', 'tensor_tensor', out=ys2, in0=ys2, in1=yf2, op=ALU.add, r=[yf_b], w=ysb_all)
                P.op('dve', 'tensor_tensor', out=ys2, in0=ys2, in1=zt2, op=ALU.mult, r=[zt_b], w=ysb_all)
                ssq, ssq_b = stat.next()
                P.op('act', 'activation', out=yf2, in_=ys2, func=AF.Square, accum_out=ssq[:, 0:1], r=ysb_all, w=[yf_b, ssq_b])
                P.op('act', 'activation', out=ssq[:, 1:2], in_=ssq[:, 0:1], func=AF.Sqrt, scale=1.0 / 2048, bias=RMS_EPS,
                     r=[ssq_b], w=[ssq_b])
                P.op('dve', 'reciprocal', out=ssq[:, 2:3], in_=ssq[:, 1:2], r=[ssq_b], w=[ssq_b])
                P.op('act', 'activation', out=ys2, in_=ys2, func=AF.Copy, scale=ssq[:, 2:3], r=[ssq_b], w=ysb_all)
                P.dma('auto', 'dma_start', out=GT_d[t0:t0 + 128, :], in_=ys2, r=ysb_all, w=[GT_b])

            def init_state(seq_is_lat, d):
                if not seq_is_lat:
                    P.op('pool', 'memset', hst, 0.0, w=hst_bs)
                    P.op('dve', 'memset', hb16, 0.0, w=hb16_bs)
                    return
                for q4 in range(4):
                    lt, lb = big.next()
                    lt3 = lt[:].rearrange("p c t -> p (c t)")[:, 0:512].rearrange("p (a n) -> p a n", n=128)
                    r0 = d * 2048 + q4 * 512
                    P.dma('sp', 'dma_start', out=lt3, in_=st_in[r0:r0 + 512, :].rearrange("(a p) n -> p a n", p=128), w=[lb])
                    pt, pb = K.ps()
                    for a in range(4):
                        P.op('pe', 'transpose', pt[:, a * 128:(a + 1) * 128], lt3[:, a, :], ident, r=[lb, cst_b], w=[pb])
                    P.op('dve', 'tensor_copy', out=hst[:, q4 * 512:(q4 + 1) * 512], in_=pt[:], r=[pb], w=[hst_bs[q4]])
                    P.op('act', 'activation', out=hb16[:, q4 * 512:(q4 + 1) * 512], in_=hst[:, q4 * 512:(q4 + 1) * 512], func=AF.Copy,
                         r=[hst_bs[q4]], w=[hb16_bs[q4]])

            def out_state(si, d):
                for q4 in range(4):
                    pt, pb = K.ps()
                    for a in range(4):
                        c0 = q4 * 512 + a * 128
                        P.op('pe', 'transpose', pt[:, a * 128:(a + 1) * 128], hst[:, c0:c0 + 128], ident, r=[hst_bs[q4], cst_b], w=[pb])
                    ot, ot_b = big.next()
                    ot3 = ot[:].rearrange("p c t -> p (c t)")[:, 0:512]
                    P.op('act', 'activation', out=ot3, in_=pt[:], func=AF.Copy, r=[pb], w=[ot_b])
                    r0 = (si * 2 + d) * 2048 + q4 * 512
                    P.dma('auto', 'dma_start', out=ns[r0:r0 + 512, :].rearrange("(a p) n -> p a n", p=128),
                          in_=ot3.rearrange("p (a n) -> p a n", n=128), r=[ot_b])

            for si, (s0, L) in enumerate(SEQS):
                lat = (si == 4)
                nchk = L // 128
                init_state(lat, 0)
                for c in range(nchk):
                    chunk(s0 + c * 128, 0, False)
                flush_yf()
                if not lat:
                    out_state(si, 0)
                init_state(lat, 1)
                for c in reversed(range(nchk)):
                    chunk(s0 + c * 128, 1, True)
                if not lat:
                    out_state(si, 1)
            P.barrier()
            K.set_acc(False)
            wo_t = arena[:, 0:16384].rearrange("p (k n) -> p k n", n=D)
            wo_b = Buf("ssd_wo")
            for k4 in range(4):
                P.dma('pool', 'dma_start', out=wo_t[:, k4 * 4:(k4 + 1) * 4, :],
                      in_=ssd_w_out[k4 * 512:(k4 + 1) * 512, :].rearrange("(k p) n -> p k n", p=128), w=[wo_b])
            gTs = Rot([(arena[:, 16384 + q_ * 4096:16384 + (q_ + 1) * 4096].rearrange("p (k n) -> p k n", n=TT), Buf("gT%d" % q_)) for q_ in range(2)])
            r_nw = VROW['ssd_norm_w']
            for tt in range(NTILE):
                gT, gTall_b = gTs.next()
                if tt > 0:
                    pass
                for sub in range(TT // 128):
                    tok = tt * TT + sub * 128
                    gl, gl_b = big.next()
                    gl2 = gl[:].rearrange("p c t -> p (c t)")
                    P.dma('sp', 'dma_start', out=gl2, in_=GT_d[tok:tok + 128, :], r=[GT_b], w=[gl_b])
                    for q4 in range(4):
                        pt, pb = K.ps()
                        for a in range(4):
                            k = q4 * 4 + a
                            P.op('pe', 'transpose', pt[:, a * 128:(a + 1) * 128], gl2[:, k * 128:(k + 1) * 128], ident,
                                 r=[gl_b, cst_b], w=[pb])
                        for a in range(4):
                            k = q4 * 4 + a
                            nwv = PV[:, k % 8, r_nw + k // 8:r_nw + k // 8 + 1]
                            P.op('act', 'activation', out=gT[:, k, sub * 128:(sub + 1) * 128], in_=pt[:, a * 128:(a + 1) * 128],
                                 func=AF.Copy, scale=nwv, r=[pb, PV_b], ww=[gTall_b])
                for m in range(NCH):
                    pt, pb = K.ps()
                    for k in range(16):
                        P.op('pe', 'matmul', pt[:, 0:TT], lhsT=wo_t[:, k, m * 128:(m + 1) * 128], rhs=gT[:, k, :],
                             start=(k == 0), stop=(k == 15), r=[wo_b, gTall_b], w=[pb], inc=(k == 15))
                    residual_update(l, pt, pb, m, tt, None)
            P.barrier()

        for l in range(nlayers):
            kind = l % 3
            if kind == 0:
                conv_layer(l, l // 3)
            elif kind == 1:
                na_layer(l)
            else:
                ssd_layer(l)

        fnw = VROW['final_norm_w']
        for tt in range(NTILE):
            sl = slice(tt * TT, (tt + 1) * TT)
            rs, rs_b = stat.next()
            if dbg:
                P.op('dve', 'memset', rs[:], 1.0, w=[rs_b])
            else:
                colsum_rstd(X[:, :, sl], Xb[tt], RMS_EPS, rs, rs_b)
            yt, yt_b = big.next()
            for k in range(NCH):
                if dbg:
                    P.op('dve', 'tensor_copy', out=yt[:, k, :], in_=X[:, k, sl],
                         r=[Xb[tt]], w=[yt_b])
                else:
                    P.op('dve', 'scalar_tensor_tensor',
                        out=yt[:, k, :], in0=X[:, k, sl], scalar=PV[:, k, fnw:fnw + 1], in1=rs[:], op0=ALU.mult,
                        op1=ALU.mult, r=[Xb[tt], rs_b, PV_b], ww=[yt_b])
            for q in range(TT // 128):
                ot, ot_b = big.next()
                ot2 = ot[:].rearrange("p c t -> p (c t)")[:, 0:D]
                for half in range(2):
                    pt, pb = K.ps()
                    for c4 in range(4):
                        c = half * 4 + c4
                        P.op('pe', 'transpose',
                            pt[:, c4 * 128:(c4 + 1) * 128], yt[:, c, q * 128:(q + 1) * 128], ident,
                            r=[yt_b, cst_b], w=[pb])
                    if half == 0:
                        P.op('act', 'activation', out=ot2[:, 0:512], in_=pt[:], func=AF.Copy,
                             r=[pb], w=[ot_b])
                    else:
                        P.op('dve', 'tensor_copy', out=ot2[:, 512:1024], in_=pt[:],
                             r=[pb], w=[ot_b])
                tok = tt * TT + q * 128
                dst = y_ctx[tok:tok + 128, :] if tok < 1024 else y_lat[tok - 1024:tok - 1024 + 128, :]
                P.dma('auto', 'dma_start', out=dst, in_=ot2, r=[ot_b])

        P.finish()
        P.emit()
        print("instructions:", P.ninst, {e: P.seq[e] for e in ENGS}, {e: P.dcount[e] for e in ENGS})
    return nc


def make_btab(rpb):
    ext = np.concatenate([rpb.reshape(16, -1), np.full((16, 1), -30000.0, np.float32)], axis=1)
    rs_row = lambda qr: min(max(qr - 4, 0), 24)
    idx = np.full((5, 5, 128, 128), 465, np.int64)
    kc = np.arange(64)[:, None]
    qc = np.arange(64)[None, :]
    cs = np.clip(qc - 8, 0, 48)
    colok = (kc >= cs) & (kc < cs + 16)
    dcol = np.clip(kc - qc, -15, 15) + 15
    for cls, r in enumerate((0, 2, 4, 28, 30)):
        ft = rs_row(r) // 2
        for jt in range(5):
            for a in range(2):
                for b in range(2):
                    kr = 2 * (ft + jt) + a
                    qr = r + b
                    if kr > 31 or not (rs_row(qr) <= kr < rs_row(qr) + 8):
                        continue
                    blk = np.where(colok, (kr - qr + 7) * 31 + dcol, 465)
                    idx[cls, jt, a * 64:(a + 1) * 64, b * 64:(b + 1) * 64] = blk
    return np.ascontiguousarray(ext[:, idx].transpose(1, 0, 2, 3, 4))


def make_in_maps(inp):
    g = lambda k: np.ascontiguousarray(np.asarray(inp[k], dtype=np.float32))
    vi = np.arange(128)[:, None]
    ii = np.arange(128)[None, :]
    cst = np.concatenate([np.eye(128, dtype=np.float32), np.ones((128, 128), np.float32), (vi <= ii).astype(np.float32),
                          (vi >= ii).astype(np.float32), (vi > ii).astype(np.float32), (vi < ii).astype(np.float32)], axis=1)
    ssc = np.concatenate([g('ssd_dt_bias')[0].reshape(-1), g('ssd_a_log')[0].reshape(-1), g('ssd_d')[0].reshape(-1)])
    ssc = np.ascontiguousarray(np.broadcast_to(ssc[None, :], (128, 160)))
    btab = make_btab(g('na_rpb')[0])
    maps = []
    for core in range(8):
        s = core // 4
        vt = np.zeros((128, D), np.float32)

        def put(name, arr):
            a = np.asarray(arr, np.float32).reshape(-1, D)
            vt[VROW[name]:VROW[name] + a.shape[0]] = a
        put('c_ctx', g('c_ctx'))
        put('c', g('c')[s])
        for l in range(4):
            put('ada_b%d' % l, g('ada_b')[l])
            put('norm_w%d' % l, g('norm_w')[l])
        put('final_norm_w', g('final_norm_w'))
        for j in range(2):
            put('conv_b_in%d' % j, g('conv_b_in')[j])
            put('conv_w_dw%d' % j, g('conv_w_dw')[j])
            put('conv_b_dw%d' % j, g('conv_b_dw')[j])
            put('conv_ln_w%d' % j, g('conv_ln_w')[j])
            put('conv_ln_b%d' % j, g('conv_ln_b')[j])
            put('conv_b_out%d' % j, g('conv_b_out')[j])
        put('ssd_w_conv', g('ssd_w_conv')[0])
        put('ssd_b_conv', g('ssd_b_conv')[0])
        put('ssd_norm_w', g('ssd_norm_w')[0])
        m = {
            "xp": g('x_prompt')[core * 4:(core + 1) * 4].reshape(1024, D),
            "xs": g('x_sample')[s],
            "vtab": vt,
            "cst": cst,
            "ada_w": g('ada_w'),
            "conv_w_in": g('conv_w_in'),
            "conv_w_out": g('conv_w_out'),
            "na_w_in": g('na_w_in')[0],
            "na_w_out": g('na_w_out')[0],
            "ck": g('cache_k')[s, 0].reshape(256, D),
            "cv": g('cache_v')[s, 0].reshape(256, D),
            "btab": btab,
            "ssd_w_in": g('ssd_w_in')[0],
            "ssd_w_out": g('ssd_w_out')[0],
            "ssc": ssc,
            "st_in": g('state_ssm')[s, 0].reshape(2 * 2048, 128),
        }
        maps.append(m)
    return maps


def run(inp, nlayers=4, dbg=False):
    nc = build(nlayers, dbg)
    maps = make_in_maps(inp)
    res = run_bass_kernel_spmd(nc, maps, core_ids=list(range(8)))
    return res.results


def kernel(**inp):
    rs = run(inp)
    y_prompt = np.concatenate([r["y_ctx"].reshape(4, 256, D) for r in rs], axis=0)
    y_sample = np.stack([rs[0]["y_lat"], rs[4]["y_lat"]], axis=0)
    new_k = np.concatenate([r["nk"].reshape(4, 1, 256, 16, 64) for r in rs], axis=0)
    new_v = np.concatenate([r["nv"].reshape(4, 1, 256, 16, 64) for r in rs], axis=0)
    new_s = np.concatenate([r["ns"].reshape(4, 1, 2, 32, 64, 128) for r in rs], axis=0)
    return (y_prompt, y_sample, new_k, new_v, new_s)
```

```python
import numpy as np
import concourse.bass as bass
import concourse.mybir as mybir
from concourse.bass_utils import run_bass_kernel_spmd
from contextlib import ExitStack

F32 = mybir.dt.float32
BF16 = mybir.dt.bfloat16
AF = mybir.ActivationFunctionType
ALU = mybir.AluOpType
AX = mybir.AxisListType

ENGS = ['pe', 'act', 'dve', 'pool', 'sp']
CAP = 30000
SAME_ENG_SYNC = True
NSLOT = 32


class Buf:
    __slots__ = ('name', 'lw', 'rd', 'dw', 'dr')

    def __init__(self, name):
        self.name = name
        self.lw = None
        self.rd = {}
        self.dw = {}
        self.dr = {}


class Prog:
    def __init__(self, nc, es):
        self.nc = nc
        self.es = es
        self.q = {e: [] for e in ENGS}
        self.seq = {e: 0 for e in ENGS}
        self.seen = {e: {} for e in ENGS}
        self.esems = {e: [] for e in ENGS}
        self.dslots = {e: [] for e in ENGS}
        self.dcount = {e: 0 for e in ENGS}
        self.ninst = 0

    def newsem(self, name):
        return self.es.enter_context(self.nc.semaphore(name))

    def esem(self, e, ep):
        while len(self.esems[e]) <= ep:
            self.esems[e].append(self.newsem("s_%s_%d" % (e, len(self.esems[e]))))
        return self.esems[e][ep]

    def _w_eng(self, e, waits, src, s):
        if src == e and (e == 'pe' or not SAME_ENG_SYNC):
            return
        ep = (s - 1) // CAP
        val = (s - 1) % CAP + 1
        key = ('e', src, ep)
        if self.seen[e].get(key, 0) >= val:
            return
        self.seen[e][key] = val
        waits[key] = (self.esem(src, ep), val)

    def _w_sem(self, e, waits, sem, val):
        key = ('d', sem.num)
        if self.seen[e].get(key, 0) >= val:
            return
        self.seen[e][key] = val
        waits[key] = (sem, val)

    def _w_evs(self, e, waits, evs):
        for sem, val in evs.values():
            self._w_sem(e, waits, sem, val)

    def op(self, e, fname, *args, r=(), w=(), ww=(), inc=True, **kw):
        fn = (fname, args, kw)
        assert inc or e == 'pe'
        weak = set(id(b) for b in ww)
        w = list(w) + list(ww)
        waits = {}
        for b in r:
            if b.lw:
                self._w_eng(e, waits, *b.lw)
            self._w_evs(e, waits, b.dw)
        for b in w:
            if b.lw and not (id(b) in weak and b.lw[0] == e):
                self._w_eng(e, waits, *b.lw)
            for src, s in b.rd.items():
                self._w_eng(e, waits, src, s)
            self._w_evs(e, waits, b.dw)
            self._w_evs(e, waits, b.dr)
        if inc:
            self.seq[e] += 1
            s = self.seq[e]
        else:
            s = self.seq[e] + 1
        self.esem(e, (s - 1) // CAP)
        self.q[e].append((list(waits.values()), fn, None, s if inc else -1))
        for b in r:
            if b.rd.get(e, 0) < s:
                b.rd[e] = s
        for b in w:
            b.lw = (e, s)
            b.rd = {}
            b.dw = {}
            b.dr = {}
        self.ninst += 1

    def dma(self, e, fname, *args, r=(), w=(), **kw):
        fn = (fname, args, kw)
        if e == 'auto':
            e = r[0].lw[0] if (r and r[0].lw) else 'pool'
            if e == 'pe':
                e = 'pool'
            if e == 'dve':
                e = 'sp'
        waits = {}
        for b in r:
            if b.lw:
                self._w_eng(e, waits, *b.lw)
            self._w_evs(e, waits, b.dw)
        for b in w:
            if b.lw:
                self._w_eng(e, waits, *b.lw)
            for src, s in b.rd.items():
                self._w_eng(e, waits, src, s)
            self._w_evs(e, waits, b.dr)
        i = self.dcount[e]
        self.dcount[e] += 1
        slot = i % NSLOT
        while len(self.dslots[e]) <= slot:
            self.dslots[e].append(self.newsem("d_%s_%d" % (e, len(self.dslots[e]))))
        sem = self.dslots[e][slot]
        val = 16 * (i // NSLOT + 1)
        if val > 16:
            self._w_sem(e, waits, sem, val - 16)
        self.q[e].append((list(waits.values()), fn, sem, 16))
        for b in r:
            b.dr[sem.num] = (sem, val)
        for b in w:
            b.dw[sem.num] = (sem, val)
        self.ninst += 1

    def barrier(self):
        for e in ENGS:
            waits = {}
            for src in ENGS:
                if self.seq[src] > 0 and not (src == e and e == 'pe'):
                    ep = (self.seq[src] - 1) // CAP
                    val = (self.seq[src] - 1) % CAP + 1
                    key = ('e', src, ep)
                    if self.seen[e].get(key, 0) < val:
                        self.seen[e][key] = val
                        waits[key] = (self.esem(src, ep), val)
            for q in ENGS:
                n = self.dcount[q]
                for slot, sem in enumerate(self.dslots[q]):
                    cnt = (n - slot + NSLOT - 1) // NSLOT
                    if cnt > 0:
                        self._w_sem(e, waits, sem, 16 * cnt)
            self.q[e].append((list(waits.values()), None, None, 0))

    def finish(self, e='sp'):
        waits = {}
        for q in ENGS:
            n = self.dcount[q]
            for slot, sem in enumerate(self.dslots[q]):
                cnt = (n - slot + NSLOT - 1) // NSLOT
                if cnt > 0:
                    self._w_sem(e, waits, sem, 16 * cnt)
        self.q[e].append((list(waits.values()), None, None, 0))

    def emit(self):
        nc = self.nc
        engmap = {'pe': 'tensor', 'act': 'scalar', 'dve': 'vector', 'pool': 'gpsimd', 'sp': 'sync'}
        with nc.Block() as block:
            for e in ENGS:
                items = self.q[e]
                if not items:
                    continue

                def body(eng, items=items, e=e):
                    for waits, fn, dsem, s in items:
                        for sem, val in waits:
                            eng.wait_ge(sem, val)
                        if fn is None:
                            continue
                        ins = getattr(eng, fn[0])(*fn[1], **fn[2])
                        if dsem is not None:
                            ins.then_inc(dsem, 16)
                        elif s > 0:
                            ins.then_inc(self.esems[e][(s - 1) // CAP], 1)
                getattr(block, engmap[e])(body)


D = 1024
NCH = 8
T = 3072
TT = 256
NTILE = T // TT
NHALF = NTILE // 2
SEQS = [(0, 256), (256, 256), (512, 256), (768, 256), (1024, 2048)]
RMS_EPS = 1e-6
LN_EPS = 1e-5

VROW = {}
_r = 0
def _vr(name, n):
    global _r
    VROW[name] = _r
    _r += n
_vr('c_ctx', 1); _vr('c', 1)
for _l in range(4):
    _vr('ada_b%d' % _l, 3); _vr('norm_w%d' % _l, 1)
_vr('final_norm_w', 1)
for _j in range(2):
    _vr('conv_b_in%d' % _j, 3); _vr('conv_w_dw%d' % _j, 31); _vr('conv_b_dw%d' % _j, 1)
    _vr('conv_ln_w%d' % _j, 1); _vr('conv_ln_b%d' % _j, 1); _vr('conv_b_out%d' % _j, 1)
_vr('ssd_w_conv', 21); _vr('ssd_b_conv', 3); _vr('ssd_norm_w', 2)
NVEC = _r
assert NVEC <= 128


class KB:
    def __init__(self, nc, es):
        self.nc = nc
        self.es = es
        self.P = Prog(nc, es)
        self.psn = 0
        self.rot = list(range(8))
        self.pan = 0
        self.uid = 0

    def sb(self, name, shape, dt):
        t = self.es.enter_context(self.nc.sbuf_tensor(name, list(shape), dt))
        return t, Buf(name)

    def pool(self, name, shape, dt, n):
        return [self.sb("%s%d" % (name, i), shape, dt) for i in range(n)]

    def init_psum(self):
        self.psb = []
        for i in range(8):
            t = self.es.enter_context(self.nc.psum_tensor("ps%d" % i, [128, 512], F32))
            self.psb.append((t, Buf("ps%d" % i)))

    def ps(self):
        r = self.psb[self.rot[self.psn % len(self.rot)]]
        self.psn += 1
        return r

    def set_acc(self, on):
        self.rot = [4, 5, 6, 7] if on else list(range(8))

    def psacc(self):
        r = self.psb[self.pan % 4]
        self.pan += 1
        return r

    def din(self, name, shape):
        return self.nc.dram_tensor(name, list(shape), F32, kind="ExternalInput").ap()

    def dout(self, name, shape):
        return self.nc.dram_tensor(name, list(shape), F32, kind="ExternalOutput").ap()

    def dscr(self, name, shape, dt):
        return self.nc.dram_tensor(name, list(shape), dt, kind="Internal").ap(), Buf(name)


class Rot:
    def __init__(self, items):
        self.items = items
        self.i = 0

    def next(self):
        r = self.items[self.i % len(self.items)]
        self.i += 1
        return r


def build(nlayers=4, dbg=False):
    nc = bass.Bass("TRN2", target_bir_lowering=False)
    with ExitStack() as es:
        K = KB(nc, es)
        P = K.P
        xp = K.din("xp", [1024, D])
        xs = K.din("xs", [2048, D])
        vtab = K.din("vtab", [128, D])
        cst = K.din("cst", [128, 768])
        ada_w = K.din("ada_w", [4, D, 3 * D])
        conv_w_in = K.din("conv_w_in", [2, D, 3 * D])
        conv_w_out = K.din("conv_w_out", [2, D, D])
        na_w_in = K.din("na_w_in", [D, 4 * D])
        na_w_out = K.din("na_w_out", [D, D])
        ck = K.din("ck", [256, D])
        cv = K.din("cv", [256, D])
        btab = K.din("btab", [5, 16, 5, 128, 128])
        ssd_w_in = K.din("ssd_w_in", [D, 5184])
        ssd_w_out = K.din("ssd_w_out", [2 * D, D])
        ssc = K.din("ssc", [128, 160])
        st_in = K.din("st_in", [2 * 2048, 128])
        y_ctx = K.dout("y_ctx", [1024, D])
        y_lat = K.dout("y_lat", [2048, D])
        nk = K.dout("nk", [1024, D])
        nv = K.dout("nv", [1024, D])
        ns = K.dout("ns", [4 * 2 * 32 * 64, 128])
        U_d, U_b = K.dscr("U_d", [D, T], BF16)
        SZ_d, SZ_b = K.dscr("SZ_d", [D, T], BF16)
        C_d, C_b = K.dscr("C_d", [D, T], F32)
        Q_d, Q_b = K.dscr("Q_d", [D, T], BF16)
        K_d, K_b = K.dscr("K_d", [D, T + 256], BF16)
        V_d, V_b = K.dscr("V_d", [T + 256, D], BF16)
        XBC_d, XBC_b = K.dscr("XBC_d", [3 * D, T], BF16)
        XT_d, XT_b = K.dscr("XT_d", [T, 2 * D], BF16)
        BT_d, BT_b = K.dscr("BT_d", [T, 512], BF16)
        BC_d, BC_b = K.dscr("BC_d", [D, T], BF16)
        ZT_d, ZT_b = K.dscr("ZT_d", [T, 2 * D], BF16)
        DT_d, DT_b = K.dscr("DT_d", [T, 128], F32)
        YF_d, YF_b = K.dscr("YF_d", [T, 2 * D], F32)
        GT_d, GT_b = K.dscr("GT_d", [T, 2 * D], F32)

        K.init_psum()
        X, X_b = K.sb("X", [128, NCH, T], F32)
        Xb = [Buf("X_t%d" % i) for i in range(NTILE)]
        arena, _ = K.sb("arena", [128, 24576], BF16)
        H = arena[:, 0:12288].rearrange("p (c t) -> p c t", c=NCH)
        Hb = [Buf("H_t%d" % i) for i in range(NHALF)]
        cstt, cst_b = K.sb("cst_sb", [128, 768], F32)
        Mle, Mge, Mgt, Mlt = cstt[:, 256:384], cstt[:, 384:512], cstt[:, 512:640], cstt[:, 640:768]
        ssct, ssc_b = K.sb("ssc_sb", [128, 160], F32)
        ident = cstt[:, 0:128]
        ones = cstt[:, 128:256]
        identb, identb_b = K.sb("identb", [128, 128], BF16)
        PV, PV_b = K.sb("PV", [128, NCH, 128], F32)
        MOD, MOD_b = K.sb("MOD", [128, 4, 24, 2], F32)
        AMOD, AMOD_b = K.sb("AMOD", [128, 4, NCH, 2], F32)
        wmem = arena[:, 12288:20480]
        wb0, wb1 = Buf("wb0"), Buf("wb1")
        wrot = Rot([(wmem[:, 0:4096].rearrange("p (k n) -> p k n", k=NCH), wb0), (wmem[:, 4096:8192].rearrange("p (k n) -> p k n", k=NCH), wb1)])
        diag, diag_b = arena[:, 20480:20480 + 31 * 128], Buf("diag")
        onesb, onesb_b = K.sb("onesb", [128, 128], BF16)
        bq = Rot(K.pool("bq", [128, TT], BF16, 4))
        ptp = Rot(K.pool("ptp", [128, 128], BF16, 16))
        t512 = Rot(K.pool("t512", [128, TT], F32, 8))
        stat = Rot(K.pool("stat", [128, TT], F32, 3))
        b512 = Rot(K.pool("b512", [128, TT], BF16, 8))
        big = Rot(K.pool("big", [128, NCH, TT], F32, 3))
        bigb = Rot(K.pool("bigb", [128, NCH, TT], BF16, 2))

        def tile_cond(tt):
            return 0 if tt * TT < 1024 else 1

        P.dma('sp', 'dma_start', out=cstt[:], in_=cst[:, :], w=[cst_b])
        vt, vt_b = big.next()
        vt2 = vt[:].rearrange("p c t -> p (c t)")[:, 0:D]
        P.dma('sp', 'dma_start', out=vt2, in_=vtab[:, :], w=[vt_b])
        P.op('dve', 'tensor_copy', out=identb[:], in_=ident, r=[cst_b], w=[identb_b])
        P.op('dve', 'tensor_copy', out=onesb[:], in_=ones, r=[cst_b], w=[onesb_b])
        for c in range(NCH):
            pt, pb = K.ps()
            P.op('pe', 'matmul', pt[:, 0:128], lhsT=vt2[:, c * 128:(c + 1) * 128], rhs=ident,
                                                      start=True, stop=True, r=[vt_b, cst_b], w=[pb])
            P.op('dve', 'tensor_copy', out=PV[:, c, :], in_=pt[:, 0:128], r=[pb], w=[PV_b])

        def pv(name, off=0):
            r = VROW[name] + off
            return PV[:, :, r]

        for tk in range(T // 128):
            src = xp[tk * 128:(tk + 1) * 128, :] if tk < 8 else xs[(tk - 8) * 128:(tk - 7) * 128, :]
            lt, lb = big.next()
            lt2 = lt[:].rearrange("p c t -> p (c t)")[:, 0:D]
            P.dma('sp', 'dma_start', out=lt2, in_=src, w=[lb])
            for half in range(2):
                pt, pb = K.ps()
                for q in range(4):
                    c = half * 4 + q
                    P.op('pe', 'transpose', pt[:, q * 128:(q + 1) * 128],
                                                                               lt2[:, c * 128:(c + 1) * 128], ident,
                         r=[lb, cst_b], w=[pb])
                dst = X[:, half * 4:half * 4 + 4, tk * 128:(tk + 1) * 128]
                srcp = pt[:].rearrange("p (q t) -> p q t", t=128)
                eng = 'act' if half == 0 else 'dve'
                if eng == 'act':
                    P.op('act', 'activation', out=dst, in_=srcp, func=AF.Copy,
                         r=[pb], w=[Xb[tk * 128 // TT]])
                else:
                    P.op('dve', 'tensor_copy', out=dst, in_=srcp,
                         r=[pb], w=[Xb[tk * 128 // TT]])

        scb, scb_b = K.sb("scb", [128, NCH, 2], BF16)
        P.op('act', 'activation', out=scb[:], in_=PV[:, :, 0:2], func=AF.Silu, r=[PV_b], w=[scb_b])
        for l in range(nlayers):
            for blk in range(6):
                wt, wb = wrot.next()
                P.dma('pool', 'dma_start',
                    out=wt[:], in_=ada_w[l, :, blk * 512:(blk + 1) * 512].rearrange("(k p) n -> p k n", p=128),
                    w=[wb])
                pt, pb = K.ps()
                for m in range(4):
                    for k in range(NCH):
                        P.op('pe', 'matmul',
                            pt[:, m * 2:m * 2 + 2], lhsT=wt[:, k, m * 128:(m + 1) * 128], rhs=scb[:, k, :],
                            start=(k == 0), stop=(k == NCH - 1), r=[wb, scb_b], w=[pb], inc=(k == NCH - 1))
                for m in range(4):
                    ch = blk * 4 + m
                    bias = PV[:, ch % 8, VROW['ada_b%d' % l] + ch // 8:VROW['ada_b%d' % l] + ch // 8 + 1]
                    P.op('dve', 'tensor_scalar',
                        out=MOD[:, l, ch, :], in0=pt[:, m * 2:m * 2 + 2], scalar1=bias, scalar2=None, op0=ALU.add,
                        r=[pb, PV_b], w=[MOD_b])
            nw = pv('norm_w%d' % l)
            for j in range(2):
                P.op('dve', 'scalar_tensor_tensor',
                    out=AMOD[:, l, :, j], in0=MOD[:, l, 8:16, j], scalar=1.0, in1=nw, op0=ALU.add, op1=ALU.mult,
                    r=[MOD_b, PV_b], w=[AMOD_b])

        def colsum_rstd(src3, src_b, eps, out_t, out_b, scale=1.0 / D):
            n = src3.shape[2]
            sq, sq_b = big.next()
            P.op('act', 'activation', out=sq[:, :, 0:n], in_=src3, func=AF.Square, r=[src_b], w=[sq_b])
            pt, pb = K.ps()
            for k in range(NCH):
                P.op('pe', 'matmul', pt[:, 0:n], lhsT=ones, rhs=sq[:, k, 0:n], start=(k == 0),
                                                   stop=(k == NCH - 1), r=[sq_b, cst_b], w=[pb], inc=(k == NCH - 1))
            P.op('act', 'activation', out=out_t[:, 0:n], in_=pt[:, 0:n], func=AF.Sqrt, scale=scale, bias=eps,
                 r=[pb], w=[out_b])
            P.op('dve', 'reciprocal', out=out_t[:, 0:n], in_=out_t[:, 0:n], r=[out_b], w=[out_b])

        def modnorm(l, half):
            for tt in range(half * NHALF, (half + 1) * NHALF):
                j = tile_cond(tt)
                sl = slice(tt * TT, (tt + 1) * TT)
                rs, rs_b = stat.next()
                colsum_rstd(X[:, :, sl], Xb[tt], RMS_EPS, rs, rs_b)
                for k in range(NCH):
                    tmp, tmp_b = t512.next()
                    P.op('dve', 'scalar_tensor_tensor',
                        out=tmp[:], in0=X[:, k, sl], scalar=AMOD[:, l, k, j:j + 1], in1=rs[:], op0=ALU.mult,
                        op1=ALU.mult, r=[Xb[tt], AMOD_b, rs_b], w=[tmp_b])
                    P.op('act', 'activation',
                        out=H[:, k, (tt - half * NHALF) * TT:(tt - half * NHALF + 1) * TT], in_=tmp[:], func=AF.Identity,
                        bias=MOD[:, l, k, j:j + 1], scale=1.0, r=[tmp_b, MOD_b], ww=[Hb[tt - half * NHALF]])

        def load_w(wsrc, cols):
            wt, wb = wrot.next()
            off = 0
            for c0, n in cols:
                P.dma('pool', 'dma_start',
                    out=wt[:, :, off:off + n], in_=wsrc[:, c0:c0 + n].rearrange("(k p) n -> p k n", p=128), w=[wb])
                off += n
            return wt, wb

        def residual_update(l, pt, pb, m, tt, bias_ap, n0=0, n=256):
            j = tile_cond(tt)
            sl = slice(tt * TT + n0, tt * TT + n0 + n)
            tmp, tmp_b = t512.next()
            if bias_ap is not None:
                P.op('dve', 'tensor_scalar', out=tmp[:, 0:n], in0=pt[:, 0:n], scalar1=bias_ap,
                                                      scalar2=MOD[:, l, 16 + m, j:j + 1], op0=ALU.add, op1=ALU.mult,
                     r=[pb, PV_b, MOD_b], w=[tmp_b])
            else:
                P.op('dve', 'tensor_scalar', out=tmp[:, 0:n], in0=pt[:, 0:n],
                                                      scalar1=MOD[:, l, 16 + m, j:j + 1], scalar2=None, op0=ALU.mult,
                     r=[pb, MOD_b], w=[tmp_b])
            P.op('pool', 'tensor_tensor', out=X[:, m, sl], in0=X[:, m, sl], in1=tmp[:, 0:n], op=ALU.add,
                 r=[tmp_b], ww=[Xb[tt]])

        def conv_layer(l, j):
            win = conv_w_in[j]
            r_bin = VROW['conv_b_in%d' % j]
            r_wdw = VROW['conv_w_dw%d' % j]
            if dbg == 3:
                P.op('dve', 'tensor_copy', out=X[:, 0, 0:48], in_=MOD[:, 0, :, :].rearrange("p a b -> p (a b)"), r=[MOD_b], w=[Xb[0]])
                P.op('dve', 'tensor_copy', out=X[:, 0, 48:64], in_=AMOD[:, 0, :, :].rearrange("p a b -> p (a b)"), r=[AMOD_b], w=[Xb[0]])
                rs, rs_b = stat.next()
                colsum_rstd(X[:, :, 256:512], Xb[1], RMS_EPS, rs, rs_b)
                P.op('dve', 'tensor_copy', out=X[:, 1, 0:256], in_=rs[:], r=[rs_b], w=[Xb[0]])
                P.op('dve', 'tensor_copy', out=X[:, 2, 0:128], in_=PV[:, 0, :], r=[PV_b], w=[Xb[0]])
                return
            if dbg == 2:
                modnorm(l, 0)
                for tt in range(NHALF):
                    for k in range(NCH):
                        P.op('dve', 'tensor_copy', out=X[:, k, tt * TT:(tt + 1) * TT], in_=H[:, k, tt * TT:(tt + 1) * TT],
                             r=[Hb[tt]], w=[Xb[tt]])
                return
            for half, m in [(hf, mm) for hf in range(2) for mm in range(NCH)]:
                if m == 0:
                    modnorm(l, half)
                wt, wb = load_w(win, [(m * 128, 128), (D + m * 128, 128), (2 * D + m * 128, 128)])
                bv = PV[:, m, r_bin:r_bin + 1]
                bg = PV[:, m, r_bin + 1:r_bin + 2]
                bz = PV[:, m, r_bin + 2:r_bin + 3]
                for tp in range(NHALF // 2):
                    hl = slice(tp * 512, (tp + 1) * 512)
                    hbs = [Hb[2 * tp], Hb[2 * tp + 1]]
                    pss = [K.ps() for _ in range(3)]
                    for q in range(3):
                        for k in range(NCH):
                            P.op('pe', 'matmul', pss[q][0][:], lhsT=wt[:, k, q * 128:(q + 1) * 128], rhs=H[:, k, hl],
                                 start=(k == 0), stop=(k == NCH - 1), r=[wb] + hbs, w=[pss[q][1]], inc=(k == NCH - 1))
                    for sub in range(2):
                        tt = half * NHALF + 2 * tp + sub
                        sl = slice(tt * TT, (tt + 1) * TT)
                        cs = slice(sub * TT, (sub + 1) * TT)
                        sg, sg_b = t512.next()
                        P.op('act', 'activation', out=sg[:], in_=pss[1][0][:, cs], func=AF.Sigmoid, bias=bg, scale=1.0,
                             r=[pss[1][1], PV_b], w=[sg_b])
                        ut, ut_b = b512.next()
                        P.op('dve', 'scalar_tensor_tensor', out=ut[:], in0=pss[0][0][:, cs], scalar=bv, in1=sg[:], op0=ALU.add,
                             op1=ALU.mult, r=[pss[0][1], sg_b, PV_b], w=[ut_b])
                        zb_, zb_b_ = t512.next()
                        P.op('act', 'activation', out=zb_[:], in_=pss[2][0][:, cs], func=AF.Identity, bias=bz, scale=1.0,
                             r=[pss[2][1], PV_b], w=[zb_b_])
                        sz_, sz_b_ = t512.next()
                        P.op('act', 'activation', out=sz_[:], in_=zb_[:], func=AF.Sigmoid, r=[zb_b_], w=[sz_b_])
                        zt, zt_b = b512.next()
                        P.op('dve', 'tensor_tensor', out=zt[:], in0=zb_[:], in1=sz_[:], op=ALU.mult, r=[zb_b_, sz_b_], w=[zt_b])
                        P.dma('auto', 'dma_start', out=U_d[m * 128:(m + 1) * 128, sl], in_=ut[:], r=[ut_b], w=[U_b])
                        P.dma('auto', 'dma_start', out=SZ_d[m * 128:(m + 1) * 128, sl], in_=zt[:], r=[zt_b], w=[SZ_b])
            for m in range(NCH):
                dgv, dg_b = diag, diag_b
                for k in range(31):
                    P.op('dve', 'tensor_scalar',
                        out=dgv[:, k * 128:(k + 1) * 128], in0=ident, scalar1=PV[:, m, r_wdw + k:r_wdw + k + 1],
                        scalar2=None, op0=ALU.mult, r=[cst_b, PV_b], ww=[dg_b])
                bdw = PV[:, m, VROW['conv_b_dw%d' % j]:VROW['conv_b_dw%d' % j] + 1]
                for (s0, L) in SEQS:
                    for t0 in range(s0, s0 + L, TT):
                        n = min(TT, s0 + L - t0)
                        lo = max(t0 - 15, s0)
                        hi = min(t0 + n + 15, s0 + L)
                        uu, uu_b = bigb.next()
                        uv = uu[:].rearrange("p c t -> p (c t)")
                        if lo > t0 - 15 or hi < t0 + n + 15:
                            P.op('dve', 'memset', uv[:, 0:n + 30], 0.0, w=[uu_b])
                        P.dma('sp', 'dma_start',
                            out=uv[:, lo - (t0 - 15):hi - (t0 - 15)], in_=U_d[m * 128:(m + 1) * 128, lo:hi],
                            r=[U_b], w=[uu_b])
                        pt, pb = K.ps()
                        for k in range(31):
                            P.op('pe', 'matmul',
                                pt[:, 0:n], lhsT=dgv[:, k * 128:(k + 1) * 128], rhs=uv[:, k:k + n],
                                start=(k == 0), stop=(k == 30), r=[dg_b, uu_b], w=[pb], inc=(k == 30))
                        ct, ct_b = t512.next()
                        P.op('act', 'activation', out=ct[:, 0:n], in_=pt[:, 0:n],
                                                                              func=AF.Identity, bias=bdw, scale=1.0,
                             r=[pb, PV_b], w=[ct_b])
                        P.dma('auto', 'dma_start',
                            out=C_d[m * 128:(m + 1) * 128, t0:t0 + n], in_=ct[:, 0:n], r=[ct_b], w=[C_b])
            P.barrier()
            gts3 = Rot([(arena[:, q_ * 2048:(q_ + 1) * 2048].rearrange("p (k n) -> p k n", n=TT), Buf("gt3_%d" % q_)) for q_ in range(3)])
            wo_t = wmem.rearrange("p (k n) -> p k n", k=NCH)
            for hh in range(2):
                P.dma('pool', 'dma_start',
                    out=wo_t[:, :, hh * 512:(hh + 1) * 512],
                    in_=conv_w_out[j][:, hh * 512:(hh + 1) * 512].rearrange("(k p) n -> p k n", p=128), w=[wb0, wb1])
            lnw = VROW['conv_ln_w%d' % j]
            lnb = VROW['conv_ln_b%d' % j]
            bo = VROW['conv_b_out%d' % j]
            for tt in range(NTILE):
                sl = slice(tt * TT, (tt + 1) * TT)
                ct, ct_b = big.next()
                P.dma('sp', 'dma_start',
                    out=ct[:], in_=C_d[:, sl].rearrange("(k p) n -> p k n", p=128), r=[C_b], w=[ct_b])
                zt, zt_b = bigb.next()
                P.dma('sp', 'dma_start',
                    out=zt[:], in_=SZ_d[:, sl].rearrange("(k p) n -> p k n", p=128), r=[SZ_b], w=[zt_b])
                pm, pm_b = K.ps()
                for k in range(NCH):
                    P.op('pe', 'matmul', pm[:, 0:TT], lhsT=ones, rhs=ct[:, k, :], start=(k == 0),
                                                                     stop=(k == NCH - 1), r=[ct_b, cst_b], w=[pm_b], inc=(k == NCH - 1))
                mean, mean_b = stat.next()
                P.op('dve', 'tensor_scalar', out=mean[:], in0=pm[:, 0:TT], scalar1=1.0 / D,
                                                                        scalar2=None, op0=ALU.mult, r=[pm_b], w=[mean_b])
                P.op('dve', 'tensor_tensor',
                    out=ct[:], in0=ct[:], in1=mean[:].unsqueeze(1).to_broadcast([128, NCH, TT]), op=ALU.subtract,
                    r=[ct_b, mean_b], w=[ct_b])
                rs, rs_b = stat.next()
                colsum_rstd(ct[:], ct_b, LN_EPS, rs, rs_b)
                gt, gt_b = gts3.next()
                for k in range(NCH):
                    tmp, tmp_b = t512.next()
                    P.op('dve', 'scalar_tensor_tensor',
                        out=tmp[:], in0=ct[:, k, :], scalar=PV[:, k, lnw:lnw + 1], in1=rs[:], op0=ALU.mult, op1=ALU.mult,
                        r=[ct_b, rs_b, PV_b], w=[tmp_b])
                    P.op('act', 'activation', out=tmp[:], in_=tmp[:], func=AF.Silu,
                                                                      bias=PV[:, k, lnb:lnb + 1], scale=1.0,
                         r=[tmp_b, PV_b], w=[tmp_b])
                    P.op('dve', 'tensor_tensor',
                        out=gt[:, k, :], in0=tmp[:], in1=zt[:, k, :], op=ALU.mult, r=[tmp_b, zt_b], ww=[gt_b])
                for m in range(NCH):
                    pt, pb = K.ps()
                    for k in range(NCH):
                        P.op('pe', 'matmul',
                            pt[:, 0:TT], lhsT=wo_t[:, k, m * 128:(m + 1) * 128], rhs=gt[:, k, :], start=(k == 0),
                            stop=(k == NCH - 1), r=[wb0, wb1, gt_b], w=[pb], inc=(k == NCH - 1))
                    residual_update(l, pt, pb, m, tt, PV[:, m, bo:bo + 1])
            P.barrier()


        def rs_row(qr):
            return min(max(qr - 4, 0), 24)

        def na_blocks():
            blocks = []
            for sq in range(4):
                for hq in range(2):
                    blocks.append((sq * 256 + hq * 128, 128, [2 * sq, 2 * sq + 1], None, 0))
            for bi in range(16):
                r = 2 * bi
                lo = rs_row(r)
                hi = rs_row(r + 1) + 8
                tiles = list(range(lo // 2, (hi - 1) // 2 + 1))
                cls = {0: 0, 2: 1, 28: 3, 30: 4}.get(r, 2)
                blocks.append((1024 + r * 64, 128, [8 + t for t in tiles], cls, len(tiles)))
            return blocks

        def na_layer(l):
            kc_t, kc_b = bigb.next()
            for kt in range(2):
                lt, lb = big.next()
                lt2 = lt[:].rearrange("p c t -> p (c t)")[:, 0:D]
                P.dma('sp', 'dma_start', out=lt2, in_=ck[kt * 128:(kt + 1) * 128, :], w=[lb])
                for half in range(2):
                    pt, pb = K.ps()
                    for q in range(4):
                        c = half * 4 + q
                        P.op('pe', 'transpose', pt[:, q * 128:(q + 1) * 128], lt2[:, c * 128:(c + 1) * 128], ident,
                             r=[lb, cst_b], w=[pb])
                    P.op('dve', 'tensor_copy', out=kc_t[:, half * 4:half * 4 + 4, kt * 128:(kt + 1) * 128],
                         in_=pt[:].rearrange("p (q t) -> p q t", t=128), r=[pb], w=[kc_b])
            P.dma('auto', 'dma_start', out=K_d[:, T:T + 256].rearrange("(c p) n -> p c n", p=128), in_=kc_t[:],
                  r=[kc_b], w=[K_b])
            for kt in range(2):
                vb, vb_b = bigb.next()
                vb2 = vb[:].rearrange("p c t -> p (c t)")[:, 0:D]
                P.dma('pool', 'dma_start', out=vb2, in_=cv[kt * 128:(kt + 1) * 128, :], w=[vb_b])
                P.dma('auto', 'dma_start', out=V_d[T + kt * 128:T + (kt + 1) * 128, :], in_=vb2, r=[vb_b], w=[V_b])
            for half in range(2):
                modnorm(l, half)
                for sec, dstd, dst_b in ((0, Q_d, Q_b), (1, K_d, K_b), (3, SZ_d, SZ_b)):
                    for hb in range(2):
                        wt, wb = load_w(na_w_in, [(sec * D + hb * 512, 512)])
                        for tp in range(NHALF // 2):
                            hl = slice(tp * 512, (tp + 1) * 512)
                            hbs = [Hb[2 * tp], Hb[2 * tp + 1]]
                            for mm in range(4):
                                pt, pb = K.ps()
                                for k in range(NCH):
                                    P.op('pe', 'matmul', pt[:], lhsT=wt[:, k, mm * 128:(mm + 1) * 128], rhs=H[:, k, hl],
                                         start=(k == 0), stop=(k == NCH - 1), r=[wb] + hbs, w=[pb], inc=(k == NCH - 1))
                                ch = hb * 4 + mm
                                for sub in range(2):
                                    tt = half * NHALF + 2 * tp + sub
                                    sl = slice(tt * TT, (tt + 1) * TT)
                                    cs = slice(sub * TT, (sub + 1) * TT)
                                    ot, ot_b = b512.next()
                                    if sec == 3:
                                        P.op('act', 'activation', out=ot[:], in_=pt[:, cs], func=AF.Silu, r=[pb], w=[ot_b])
                                    elif mm % 2 == 0:
                                        P.op('act', 'activation', out=ot[:], in_=pt[:, cs], func=AF.Copy, r=[pb], w=[ot_b])
                                    else:
                                        P.op('dve', 'tensor_copy', out=ot[:], in_=pt[:, cs], r=[pb], w=[ot_b])
                                    P.dma('auto', 'dma_start', out=dstd[ch * 128:(ch + 1) * 128, sl], in_=ot[:], r=[ot_b], w=[dst_b])
                for sec in (1, 2):
                    if sec == 1 and half == 1:
                        continue
                    for hb in range(2):
                        wt, wb = load_w(na_w_in, [(sec * D + hb * 512, 512)])
                        for tk in range(half * 12, (half + 1) * 12):
                            if sec == 1 and tk >= 8:
                                continue
                            lk = tk - half * 12
                            pt, pb = K.ps()
                            for k in range(NCH):
                                P.op('pe', 'matmul', pt[:], lhsT=H[:, k, lk * 128:(lk + 1) * 128], rhs=wt[:, k, :],
                                     start=(k == 0), stop=(k == NCH - 1), r=[wb, Hb[lk * 128 // TT]], w=[pb], inc=(k == NCH - 1))
                            src_ap, src_b = pt[:], pb
                            if tk < 8:
                                ot, ot_b = big.next()
                                ot2 = ot[:].rearrange("p c t -> p (c t)")[:, 0:512]
                                P.op('act', 'activation', out=ot2, in_=pt[:], func=AF.Copy, r=[pb], w=[ot_b])
                                dst = nk if sec == 1 else nv
                                P.dma('auto', 'dma_start', out=dst[tk * 128:(tk + 1) * 128, hb * 512:(hb + 1) * 512], in_=ot2, r=[ot_b])
                                src_ap, src_b = ot2, ot_b
                            if sec == 2:
                                vb, vb_b = bigb.next()
                                vb2 = vb[:].rearrange("p c t -> p (c t)")[:, 0:512]
                                P.op('dve', 'tensor_copy', out=vb2, in_=src_ap, r=[src_b], w=[vb_b])
                                P.dma('auto', 'dma_start', out=V_d[tk * 128:(tk + 1) * 128, hb * 512:(hb + 1) * 512], in_=vb2,
                                      r=[vb_b], w=[V_b])
            wo_t = wmem.rearrange("p (k n) -> p k n", k=NCH)
            for hh in range(2):
                P.dma('pool', 'dma_start', out=wo_t[:, :, hh * 512:(hh + 1) * 512],
                      in_=na_w_out[:, hh * 512:(hh + 1) * 512].rearrange("(k p) n -> p k n", p=128), w=[wb0, wb1])
            P.barrier()
            K.set_acc(True)
            qT = arena[:, 0:T]
            kT = arena[:, T:2 * T + 256]
            vT = arena[:, 2 * T + 256:3 * T + 512].rearrange("p (t n) -> p t n", n=128)
            for c in range(NCH):
                qT_b, kT_b, vT_b = Buf("qT%d" % c), Buf("kT%d" % c), Buf("vT%d" % c)
                if c > 0:
                    P.barrier()
                P.dma('sp', 'dma_start', out=qT, in_=Q_d[c * 128:(c + 1) * 128, :], r=[Q_b], w=[qT_b])
                P.dma('sp', 'dma_start', out=kT, in_=K_d[c * 128:(c + 1) * 128, :], r=[K_b], w=[kT_b])
                P.dma('sp', 'dma_start', out=vT, in_=V_d[:, c * 128:(c + 1) * 128].rearrange("(t p) n -> p t n", p=128),
                      r=[V_b], w=[vT_b])
                for (q0, N, tiles, cls, nloc) in na_blocks():
                    szt, szt_b = bq.next()
                    P.dma('sp', 'dma_start', out=szt[:, 0:N], in_=SZ_d[c * 128:(c + 1) * 128, q0:q0 + N], r=[SZ_b], w=[szt_b])
                    gt, gt_b = bq.next()
                    alltiles = list(tiles) + ([24, 25] if cls is not None else [])
                    pts = {}
                    bts = {}
                    for hh in range(2):
                        R = slice(hh * 64, hh * 64 + 64)
                        hd = 2 * c + hh
                        if cls is not None:
                            bt, bt_b = big.next()
                            bt3 = bt[:].rearrange("p c t -> p (c t)")[:, 0:nloc * 128].rearrange("p (t q) -> p t q", q=128)
                            P.dma('sp', 'dma_start', out=bt3, in_=btab[cls, hd, 0:nloc].rearrange("t k q -> k t q"), w=[bt_b])
                        for ti, tile in enumerate(alltiles):
                            ps_, ps_b = K.ps()
                            P.op('pe', 'matmul', ps_[:, 0:N], lhsT=kT[R, tile * 128:(tile + 1) * 128], rhs=qT[R, q0:q0 + N],
                                 start=True, stop=True, r=[kT_b, qT_b], w=[ps_b])
                            pT, pT_b = ptp.next()
                            pts[(hh, ti)] = (pT, pT_b)
                            if cls is not None and ti < nloc:
                                ssb, ssb_b = t512.next()
                                P.op('dve', 'scalar_tensor_tensor', out=ssb[:, 0:N], in0=ps_[:, 0:N], scalar=0.125,
                                     in1=bt3[:, ti, :], op0=ALU.mult, op1=ALU.add, r=[ps_b, bt_b], w=[ssb_b])
                                P.op('act', 'activation', out=pT[:, 0:N], in_=ssb[:, 0:N], func=AF.Exp, r=[ssb_b], w=[pT_b])
                            else:
                                P.op('act', 'activation', out=pT[:, 0:N], in_=ps_[:, 0:N], func=AF.Exp, scale=0.125,
                                     r=[ps_b], w=[pT_b])
                    for hh in range(2):
                        R = slice(hh * 64, hh * 64 + 64)
                        po, po_b = K.psacc()
                        pd, pd_b = K.psacc()
                        for ti, tile in enumerate(alltiles):
                            pT, pT_b = pts[(hh, ti)]
                            first, last = (ti == 0), (ti == len(alltiles) - 1)
                            P.op('pe', 'matmul', po[:, 0:N], lhsT=vT[:, tile, :], rhs=pT[:, 0:N], start=first, stop=last,
                                 r=[vT_b, pT_b], w=[po_b], inc=last)
                        for ti, tile in enumerate(alltiles):
                            pT, pT_b = pts[(hh, ti)]
                            first, last = (ti == 0), (ti == len(alltiles) - 1)
                            P.op('pe', 'matmul', pd[:, 0:N], lhsT=onesb[:], rhs=pT[:, 0:N], start=first, stop=last,
                                 r=[onesb_b, pT_b], w=[pd_b], inc=last)
                        rd, rd_b = t512.next()
                        P.op('dve', 'reciprocal', out=rd[R, 0:N], in_=pd[R, 0:N], r=[pd_b], w=[rd_b])
                        oo, oo_b = t512.next()
                        P.op('dve', 'tensor_tensor', out=oo[R, 0:N], in0=po[R, 0:N], in1=rd[R, 0:N], op=ALU.mult,
                             r=[po_b, rd_b], w=[oo_b])
                        P.op('dve', 'tensor_tensor', out=gt[R, 0:N], in0=oo[R, 0:N], in1=szt[R, 0:N], op=ALU.mult,
                             r=[oo_b, szt_b], ww=[gt_b])
                    P.dma('pool', 'dma_start', out=U_d[c * 128:(c + 1) * 128, q0:q0 + N], in_=gt[:, 0:N], r=[gt_b], w=[U_b])
            K.set_acc(False)
            P.barrier()
            for tt in range(NTILE):
                sl = slice(tt * TT, (tt + 1) * TT)
                gt, gt_b = bigb.next()
                P.dma('sp', 'dma_start', out=gt[:], in_=U_d[:, sl].rearrange("(k p) n -> p k n", p=128), r=[U_b], w=[gt_b])
                for m in range(NCH):
                    pt, pb = K.ps()
                    for k in range(NCH):
                        P.op('pe', 'matmul', pt[:, 0:TT], lhsT=wo_t[:, k, m * 128:(m + 1) * 128], rhs=gt[:, k, :],
                             start=(k == 0), stop=(k == NCH - 1), r=[wb0, wb1, gt_b], w=[pb], inc=(k == NCH - 1))
                    residual_update(l, pt, pb, m, tt, None)

        gTall_b = Buf('gTall')
        def ssd_layer(l):
            W = ssd_w_in
            r_wc = VROW['ssd_w_conv']
            r_bc = VROW['ssd_b_conv']
            P.dma('sp', 'dma_start', out=ssct[:], in_=ssc[:, :], w=[ssc_b])
            P.op('act', 'activation', out=ssct[:, 64:128], in_=ssct[:, 64:128], func=AF.Exp, r=[ssc_b], w=[ssc_b])
            P.op('dve', 'tensor_scalar', out=ssct[:, 64:128], in0=ssct[:, 64:128], scalar1=-1.0, scalar2=None, op0=ALU.mult,
                 r=[ssc_b], w=[ssc_b])
            dtb, Abc, Dbc = ssct[:, 0:64], ssct[:, 64:128], ssct[:, 128:160]
            for half in range(2):
                modnorm(l, half)
                for blk in range(4):
                    wt, wb = load_w(W, [(blk * 512, 512)])
                    for lk in range(12):
                        tk = half * 12 + lk
                        pt, pb = K.ps()
                        for k in range(NCH):
                            P.op('pe', 'matmul', pt[:], lhsT=H[:, k, lk * 128:(lk + 1) * 128], rhs=wt[:, k, :],
                                 start=(k == 0), stop=(k == NCH - 1), r=[wb, Hb[lk * 128 // TT]], w=[pb], inc=(k == NCH - 1))
                        zb, zb_b = bigb.next()
                        zb2 = zb[:].rearrange("p c t -> p (c t)")[:, 0:512]
                        P.op('act', 'activation', out=zb2, in_=pt[:], func=AF.Silu, r=[pb], w=[zb_b])
                        P.dma('auto', 'dma_start', out=ZT_d[tk * 128:(tk + 1) * 128, blk * 512:(blk + 1) * 512], in_=zb2,
                              r=[zb_b], w=[ZT_b])
                for blk in range(6):
                    wt, wb = load_w(W, [(2048 + blk * 512, 512)])
                    for tp in range(NHALF // 2):
                        hl = slice(tp * 512, (tp + 1) * 512)
                        hbs = [Hb[2 * tp], Hb[2 * tp + 1]]
                        for mm in range(4):
                            pt, pb = K.ps()
                            for k in range(NCH):
                                P.op('pe', 'matmul', pt[:], lhsT=wt[:, k, mm * 128:(mm + 1) * 128], rhs=H[:, k, hl],
                                     start=(k == 0), stop=(k == NCH - 1), r=[wb] + hbs, w=[pb], inc=(k == NCH - 1))
                            ch = blk * 4 + mm
                            for sub in range(2):
                                tt = half * NHALF + 2 * tp + sub
                                sl = slice(tt * TT, (tt + 1) * TT)
                                cs = slice(sub * TT, (sub + 1) * TT)
                                ot, ot_b = b512.next()
                                if mm % 2 == 0:
                                    P.op('act', 'activation', out=ot[:], in_=pt[:, cs], func=AF.Copy, r=[pb], w=[ot_b])
                                else:
                                    P.op('dve', 'tensor_copy', out=ot[:], in_=pt[:, cs], r=[pb], w=[ot_b])
                                P.dma('auto', 'dma_start', out=XBC_d[ch * 128:(ch + 1) * 128, sl], in_=ot[:], r=[ot_b], w=[XBC_b])
                wt, wb = load_w(W, [(5120, 64)])
                for lk in range(12):
                    tk = half * 12 + lk
                    pt, pb = K.ps()
                    for k in range(NCH):
                        P.op('pe', 'matmul', pt[:, 0:64], lhsT=H[:, k, lk * 128:(lk + 1) * 128], rhs=wt[:, k, 0:64],
                             start=(k == 0), stop=(k == NCH - 1), r=[wb, Hb[lk * 128 // TT]], w=[pb], inc=(k == NCH - 1))
                    dta, dta_b = t512.next()
                    P.op('dve', 'tensor_tensor', out=dta[:, 0:64], in0=pt[:, 0:64], in1=dtb, op=ALU.add, r=[pb, ssc_b], w=[dta_b])
                    P.op('act', 'activation', out=dta[:, 0:64], in_=dta[:, 0:64], func=AF.Exp, r=[dta_b], w=[dta_b])
                    P.op('act', 'activation', out=dta[:, 0:64], in_=dta[:, 0:64], func=AF.Ln, bias=1.0, scale=1.0,
                         r=[dta_b], w=[dta_b])
                    P.op('dve', 'tensor_tensor', out=dta[:, 64:128], in0=dta[:, 0:64], in1=Abc, op=ALU.mult,
                         r=[dta_b, ssc_b], w=[dta_b])
                    P.dma('auto', 'dma_start', out=DT_d[tk * 128:(tk + 1) * 128, :], in_=dta[:, 0:128], r=[dta_b], w=[DT_b])
            for m in range(24):
                for k in range(7):
                    P.op('dve', 'tensor_scalar', out=diag[:, k * 128:(k + 1) * 128], in0=ident,
                         scalar1=PV[:, m % 8, r_wc + 3 * k + m // 8:r_wc + 3 * k + m // 8 + 1], scalar2=None, op0=ALU.mult,
                         r=[cst_b, PV_b], ww=[diag_b])
                bcv = PV[:, m % 8, r_bc + m // 8:r_bc + m // 8 + 1]
                P.op('dve', 'tensor_scalar', out=diag[:, 7 * 128:8 * 128], in0=ident, scalar1=bcv, scalar2=None, op0=ALU.mult,
                     r=[cst_b, PV_b], ww=[diag_b])
                for (s0, L) in SEQS:
                    for t0 in range(s0, s0 + L, TT):
                        n = TT
                        lo = max(t0 - 3, s0)
                        hi = min(t0 + n + 3, s0 + L)
                        uu, uu_b = bigb.next()
                        uv = uu[:].rearrange("p c t -> p (c t)")
                        if lo > t0 - 3 or hi < t0 + n + 3:
                            P.op('dve', 'memset', uv[:, 0:n + 6], 0.0, w=[uu_b])
                        P.dma('sp', 'dma_start', out=uv[:, lo - (t0 - 3):hi - (t0 - 3)], in_=XBC_d[m * 128:(m + 1) * 128, lo:hi],
                              r=[XBC_b], w=[uu_b])
                        if m < 20:
                            for sub in range(n // 128):
                                pt, pb = K.ps()
                                for k in range(7):
                                    P.op('pe', 'matmul', pt[:, 0:128], lhsT=uv[:, k + sub * 128:k + sub * 128 + 128],
                                         rhs=diag[:, k * 128:(k + 1) * 128], start=(k == 0), stop=False, r=[uu_b, diag_b], w=[pb], inc=False)
                                P.op('pe', 'matmul', pt[:, 0:128], lhsT=onesb[:], rhs=diag[:, 7 * 128:8 * 128], start=False, stop=True,
                                     r=[onesb_b, diag_b], w=[pb])
                                ot, ot_b = b512.next()
                                P.op('act', 'activation', out=ot[:, 0:128], in_=pt[:, 0:128], func=AF.Silu, r=[pb], w=[ot_b])
                                tok = t0 + sub * 128
                                if m < 16:
                                    P.dma('auto', 'dma_start', out=XT_d[tok:tok + 128, m * 128:(m + 1) * 128], in_=ot[:, 0:128],
                                          r=[ot_b], w=[XT_b])
                                else:
                                    P.dma('auto', 'dma_start', out=BT_d[tok:tok + 128, (m - 16) * 128:(m - 15) * 128], in_=ot[:, 0:128],
                                          r=[ot_b], w=[BT_b])
                        if m >= 16:
                            pt, pb = K.ps()
                            for k in range(7):
                                P.op('pe', 'matmul', pt[:, 0:n], lhsT=diag[:, k * 128:(k + 1) * 128], rhs=uv[:, k:k + n],
                                     start=(k == 0), stop=(k == 6), r=[uu_b, diag_b], w=[pb], inc=(k == 6))
                            ot, ot_b = b512.next()
                            P.op('act', 'activation', out=ot[:, 0:n], in_=pt[:, 0:n], func=AF.Silu, bias=bcv, scale=1.0,
                                 r=[pb, PV_b], w=[ot_b])
                            P.dma('auto', 'dma_start', out=BC_d[(m - 16) * 128:(m - 15) * 128, t0:t0 + n], in_=ot[:, 0:n],
                                  r=[ot_b], w=[BC_b])
            P.barrier()
            hst = arena[:, 0:4096].bitcast(F32)
            hb16 = arena[:, 4096:6144]
            xdt = arena[:, 6144:8192]
            xdte = arena[:, 8192:10240]
            btms = Rot([(arena[:, 10240:10752], Buf("btm0")), (arena[:, 20736:21248], Buf("btm1"))])
            bcTs = Rot([(arena[:, 10752:11776].rearrange("p (c n) -> p c n", n=128), Buf("bcT0")),
                        (arena[:, 21248:22272].rearrange("p (c n) -> p c n", n=128), Buf("bcT1"))])
            sms = Rot([(arena[:, 15360:15616].bitcast(F32), Buf("sm0")), (arena[:, 22272:22528].bitcast(F32), Buf("sm1"))])
            Gm = arena[:, 13312:14336].bitcast(F32).rearrange("p (g i) -> p g i", i=128)
            Mts = Rot([(arena[:, 11776 + 0:11776 + 512].rearrange("p (h i) -> p h i", i=128), Buf("Mt0"))] +
                      [(arena[:, 15616 + q * 512:15616 + (q + 1) * 512].rearrange("p (h i) -> p h i", i=128), Buf("Mt%d" % (q + 1))) for q in range(2)])
            Rfs = Rot([(arena[:, 12288:13312].bitcast(F32).rearrange("p (h i) -> p h i", i=128), Buf("Rf0"))] +
                      [(arena[:, 16640 + q * 1024:16640 + (q + 1) * 1024].bitcast(F32).rearrange("p (h i) -> p h i", i=128), Buf("Rf%d" % (q + 1))) for q in range(2)])
            Lfs = Rot([(arena[:, 14336:15360].bitcast(F32).rearrange("p (h i) -> p h i", i=128), Buf("Lf0"))] +
                      [(arena[:, 18688 + q * 1024:18688 + (q + 1) * 1024].bitcast(F32).rearrange("p (h i) -> p h i", i=128), Buf("Lf%d" % (q + 1))) for q in range(2)])
            xdt_b, xdte_b, Gm_b = [Buf("ssd%d" % i) for i in range(3)]
            hst_bs = [Buf("hst%d" % g) for g in range(4)]
            hb16_bs = [Buf("hb16_%d" % g) for g in range(4)]
            K.set_acc(True)

            pend_yf = []

            def flush_yf():
                while pend_yf:
                    t0_, ys2_, ysb_b_ = pend_yf.pop(0)
                    P.dma('sp', 'dma_start', out=YF_d[t0_:t0_ + 128, :], in_=ys2_, r=ysb_b_, w=[YF_b])

            def chunk(t0, d, final):
                lhs_seg = Mgt if d == 0 else Mlt
                rhs_msk = Mle if d == 0 else Mge
                xt, xt_b = bigb.next()
                xt2 = xt[:].rearrange("p c t -> p (c t)")
                btm, btm_b = btms.next()
                bcT, bcT_b = bcTs.next()
                sm, sm_b = sms.next()
                P.dma('sp', 'dma_start', out=xt2, in_=XT_d[t0:t0 + 128, :], r=[XT_b], w=[xt_b])
                P.dma('sp', 'dma_start', out=btm, in_=BT_d[t0:t0 + 128, :], r=[BT_b], w=[btm_b])
                P.dma('sp', 'dma_start', out=bcT, in_=BC_d[:, t0:t0 + 128].rearrange("(c p) n -> p c n", p=128), r=[BC_b], w=[bcT_b])
                dta, dta_b = t512.next()
                P.dma('sp', 'dma_start', out=dta[:, 0:128], in_=DT_d[t0:t0 + 128, :], r=[DT_b], w=[dta_b])
                flush_yf()
                dt_d = dta[:, d * 32:(d + 1) * 32]
                a_d = dta[:, 64 + d * 32:64 + (d + 1) * 32]
                pt, pb = K.ps()
                P.op('pe', 'matmul', pt[:, 0:32], lhsT=ones, rhs=a_d, start=True, stop=True, r=[cst_b, dta_b], w=[pb])
                P.op('pe', 'matmul', pt[:, 32:64], lhsT=rhs_msk, rhs=a_d, start=True, stop=True, r=[cst_b, dta_b], w=[pb])
                P.op('pe', 'matmul', pt[:, 64:96], lhsT=lhs_seg, rhs=a_d, start=True, stop=True, r=[cst_b, dta_b], w=[pb])
                P.op('act', 'activation', out=sm[:, 0:96], in_=pt[:, 0:96], func=AF.Exp, r=[pb], w=[sm_b])
                P.op('dve', 'tensor_tensor', out=sm[:, 96:128], in0=sm[:, 64:96], in1=dt_d, op=ALU.mult, r=[sm_b, dta_b], w=[sm_b])
                x3 = xt2.rearrange("p (h q) -> p h q", q=64)
                P.op('dve', 'tensor_tensor', out=xdt.rearrange("p (h q) -> p h q", q=64), in0=x3,
                     in1=dt_d.unsqueeze(2).to_broadcast([128, 32, 64]), op=ALU.mult, r=[xt_b, dta_b], w=[xdt_b])
                pg, pg_b = K.ps()
                for g in range(4):
                    P.op('pe', 'matmul', pg[:, g * 128:(g + 1) * 128], lhsT=bcT[:, g, :], rhs=bcT[:, 4 + g, :], start=True, stop=True,
                         r=[bcT_b], w=[pg_b])
                P.op('dve', 'tensor_tensor', out=Gm, in0=pg[:].rearrange("p (g i) -> p g i", i=128),
                     in1=rhs_msk.unsqueeze(1).to_broadcast([128, 4, 128]), op=ALU.mult, r=[pg_b, cst_b], w=[Gm_b])
                ysb, ysb_b = big.next()
                ys2 = ysb[:].rearrange("p c t -> p (c t)")
                ysb_bs = [Buf("ysg%d" % g_) for g_ in range(4)]
                ysb_all = [ysb_b] + ysb_bs
                def seg_part(g, hq):
                    h0 = g * 8 + hq * 4
                    Rf, Rf_b = Rfs.next()
                    Lf, Lf_b = Lfs.next()
                    Mt, Mt_b = Mts.next()
                    P.op('pool', 'tensor_tensor', out=Rf, in0=a_d[:, h0:h0 + 4].unsqueeze(2).to_broadcast([128, 4, 128]),
                         in1=rhs_msk.unsqueeze(1).to_broadcast([128, 4, 128]), op=ALU.mult, r=[dta_b, cst_b], w=[Rf_b])
                    psg, psg_b = K.ps()
                    P.op('pe', 'matmul', psg[:], lhsT=lhs_seg, rhs=Rf.rearrange("p h i -> p (h i)"), start=True, stop=True,
                         r=[cst_b, Rf_b], w=[psg_b])
                    P.op('act', 'activation', out=Lf, in_=psg[:].rearrange("p (h i) -> p h i", i=128), func=AF.Exp,
                         r=[psg_b], w=[Lf_b])
                    P.op('dve', 'tensor_tensor', out=Mt, in0=Lf, in1=Gm[:, g, :].unsqueeze(1).to_broadcast([128, 4, 128]),
                         op=ALU.mult, r=[Lf_b, Gm_b], w=[Mt_b])
                    return Mt, Mt_b

                order = [(g, hq) for g in range(4) for hq in range(2)]
                pend = seg_part(*order[0])
                yd = yd_b = None
                for idx, (g, hq) in enumerate(order):
                    Mt, Mt_b = pend
                    if idx + 1 < len(order):
                        pend = seg_part(*order[idx + 1])
                    if hq == 0:
                        yd, yd_b = K.psacc()
                    h0 = g * 8 + hq * 4
                    for hh in range(4):
                        h = h0 + hh
                        P.op('pe', 'matmul', yd[:, (h % 8) * 64:(h % 8 + 1) * 64], lhsT=Mt[:, hh, :], rhs=xdt[:, h * 64:(h + 1) * 64],
                             start=True, stop=True, r=[Mt_b, xdt_b], w=[yd_b])
                    if hq == 0:
                        continue
                    yo, yo_b = K.psacc()
                    P.op('pe', 'matmul', yo[:], lhsT=bcT[:, 4 + g, :], rhs=hb16[:, g * 512:(g + 1) * 512], start=True, stop=True,
                         r=[bcT_b, hb16_bs[g]], w=[yo_b])
                    ysg = ys2[:, g * 512:(g + 1) * 512]
                    P.op('dve', 'tensor_tensor', out=ysg.rearrange("p (h q) -> p h q", q=64),
                         in0=yo[:].rearrange("p (h q) -> p h q", q=64),
                         in1=sm[:, 32 + g * 8:32 + (g + 1) * 8].unsqueeze(2).to_broadcast([128, 8, 64]), op=ALU.mult,
                         r=[yo_b, sm_b], w=[ysb_bs[g]], ww=[ysb_b])
                    P.op('dve', 'tensor_tensor', out=ysg, in0=ysg, in1=yd[:], op=ALU.add, r=[yd_b], w=[ysb_bs[g]], ww=[ysb_b])
                P.op('pool', 'tensor_tensor', out=xdte.rearrange("p (h q) -> p h q", q=64), in0=x3,
                     in1=sm[:, 96:128].unsqueeze(2).to_broadcast([128, 32, 64]), op=ALU.mult, r=[xt_b, sm_b], w=[xdte_b])
                for g in range(4):
                    pst, pst_b = K.psacc()
                    P.op('pe', 'matmul', pst[:], lhsT=btm[:, g * 128:(g + 1) * 128], rhs=xdte[:, g * 512:(g + 1) * 512], start=True,
                         stop=True, r=[btm_b, xdte_b], w=[pst_b])
                    hg = hst[:, g * 512:(g + 1) * 512]
                    P.op('pool', 'tensor_tensor', out=hg.rearrange("p (h q) -> p h q", q=64), in0=hg.rearrange("p (h q) -> p h q", q=64),
                         in1=sm[:, g * 8:(g + 1) * 8].unsqueeze(2).to_broadcast([128, 8, 64]), op=ALU.mult, r=[sm_b], w=[hst_bs[g]])
                    P.op('dve', 'tensor_tensor', out=hg, in0=hg, in1=pst[:], op=ALU.add, r=[pst_b], w=[hst_bs[g]])
                    P.op('act', 'activation', out=hb16[:, g * 512:(g + 1) * 512], in_=hg, func=AF.Copy, r=[hst_bs[g]], w=[hb16_bs[g]])
                if not final:
                    pend_yf.append((t0, ys2, ysb_all))
                    return
                yf, yf_b = big.next()
                yf2 = yf[:].rearrange("p c t -> p (c t)")
                P.dma('sp', 'dma_start', out=yf2, in_=YF_d[t0:t0 + 128, :], r=[YF_b], w=[yf_b])
                zt, zt_b = bigb.next()
                zt2 = zt[:].rearrange("p c t -> p (c t)")
                P.dma('sp', 'dma_start', out=zt2, in_=ZT_d[t0:t0 + 128, :], r=[ZT_b], w=[zt_b])
                P.op('pool', 'tensor_tensor', out=ys2, in0=ys2, in1=yf2, op=ALU.add, r=[yf_b], w=ysb_all)
                P.op('dve', 'tensor_tensor', out=yf2.rearrange("p (h q) -> p h q", q=64), in0=x3,
                     in1=Dbc.unsqueeze(2).to_broadcast([128, 32, 64]), op=ALU.mult, r=[xt_b, ssc_b], w=[yf_b])
                P.op('dve', 'tensor_tensor', out=ys2, in0=ys2, in1=yf2, op=ALU.add, r=[yf_b], w=ysb_all)
                P.op('dve', 'tensor_tensor', out=ys2, in0=ys2, in1=zt2, op=ALU.mult, r=[zt_b], w=ysb_all)
                ssq, ssq_b = stat.next()
                P.op('act', 'activation', out=yf2, in_=ys2, func=AF.Square, accum_out=ssq[:, 0:1], r=ysb_all, w=[yf_b, ssq_b])
                P.op('act', 'activation', out=ssq[:, 1:2], in_=ssq[:, 0:1], func=AF.Sqrt, scale=1.0 / 2048, bias=RMS_EPS,
                     r=[ssq_b], w=[ssq_b])
                P.op('dve', 'reciprocal', out=ssq[:, 2:3], in_=ssq[:, 1:2], r=[ssq_b], w=[ssq_b])
                P.op('act', 'activation', out=ys2, in_=ys2, func=AF.Copy, scale=ssq[:, 2:3], r=[ssq_b], w=ysb_all)
                P.dma('auto', 'dma_start', out=GT_d[t0:t0 + 128, :], in_=ys2, r=ysb_all, w=[GT_b])

            def init_state(seq_is_lat, d):
                if not seq_is_lat:
                    P.op('pool', 'memset', hst, 0.0, w=hst_bs)
                    P.op('dve', 'memset', hb16, 0.0, w=hb16_bs)
                    return
                for q4 in range(4):
                    lt, lb = big.next()
                    lt3 = lt[:].rearrange("p c t -> p (c t)")[:, 0:512].rearrange("p (a n) -> p a n", n=128)
                    r0 = d * 2048 + q4 * 512
                    P.dma('sp', 'dma_start', out=lt3, in_=st_in[r0:r0 + 512, :].rearrange("(a p) n -> p a n", p=128), w=[lb])
                    pt, pb = K.ps()
                    for a in range(4):
                        P.op('pe', 'transpose', pt[:, a * 128:(a + 1) * 128], lt3[:, a, :], ident, r=[lb, cst_b], w=[pb])
                    P.op('dve', 'tensor_copy', out=hst[:, q4 * 512:(q4 + 1) * 512], in_=pt[:], r=[pb], w=[hst_bs[q4]])
                    P.op('act', 'activation', out=hb16[:, q4 * 512:(q4 + 1) * 512], in_=hst[:, q4 * 512:(q4 + 1) * 512], func=AF.Copy,
                         r=[hst_bs[q4]], w=[hb16_bs[q4]])

            def out_state(si, d):
                for q4 in range(4):
                    pt, pb = K.ps()
                    for a in range(4):
                        c0 = q4 * 512 + a * 128
                        P.op('pe', 'transpose', pt[:, a * 128:(a + 1) * 128], hst[:, c0:c0 + 128], ident, r=[hst_bs[q4], cst_b], w=[pb])
                    ot, ot_b = big.next()
                    ot3 = ot[:].rearrange("p c t -> p (c t)")[:, 0:512]
                    P.op('act', 'activation', out=ot3, in_=pt[:], func=AF.Copy, r=[pb], w=[ot_b])
                    r0 = (si * 2 + d) * 2048 + q4 * 512
                    P.dma('auto', 'dma_start', out=ns[r0:r0 + 512, :].rearrange("(a p) n -> p a n", p=128),
                          in_=ot3.rearrange("p (a n) -> p a n", n=128), r=[ot_b])

            for si, (s0, L) in enumerate(SEQS):
                lat = (si == 4)
                nchk = L // 128
                init_state(lat, 0)
                for c in range(nchk):
                    chunk(s0 + c * 128, 0, False)
                flush_yf()
                if not lat:
                    out_state(si, 0)
                init_state(lat, 1)
                for c in reversed(range(nchk)):
                    chunk(s0 + c * 128, 1, True)
                if not lat:
                    out_state(si, 1)
            P.barrier()
            K.set_acc(False)
            wo_t = arena[:, 0:16384].rearrange("p (k n) -> p k n", n=D)
            wo_b = Buf("ssd_wo")
            for k4 in range(4):
                P.dma('pool', 'dma_start', out=wo_t[:, k4 * 4:(k4 + 1) * 4, :],
                      in_=ssd_w_out[k4 * 512:(k4 + 1) * 512, :].rearrange("(k p) n -> p k n", p=128), w=[wo_b])
            gTs = Rot([(arena[:, 16384 + q_ * 4096:16384 + (q_ + 1) * 4096].rearrange("p (k n) -> p k n", n=TT), Buf("gT%d" % q_)) for q_ in range(2)])
            r_nw = VROW['ssd_norm_w']
            for tt in range(NTILE):
                gT, gTall_b = gTs.next()
                if tt > 0:
                    pass
                for sub in range(TT // 128):
                    tok = tt * TT + sub * 128
                    gl, gl_b = big.next()
                    gl2 = gl[:].rearrange("p c t -> p (c t)")
                    P.dma('sp', 'dma_start', out=gl2, in_=GT_d[tok:tok + 128, :], r=[GT_b], w=[gl_b])
                    for q4 in range(4):
                        pt, pb = K.ps()
                        for a in range(4):
                            k = q4 * 4 + a
                            P.op('pe', 'transpose', pt[:, a * 128:(a + 1) * 128], gl2[:, k * 128:(k + 1) * 128], ident,
                                 r=[gl_b, cst_b], w=[pb])
                        for a in range(4):
                            k = q4 * 4 + a
                            nwv = PV[:, k % 8, r_nw + k // 8:r_nw + k // 8 + 1]
                            P.op('act', 'activation', out=gT[:, k, sub * 128:(sub + 1) * 128], in_=pt[:, a * 128:(a + 1) * 128],
                                 func=AF.Copy, scale=nwv, r=[pb, PV_b], ww=[gTall_b])
                for m in range(NCH):
                    pt, pb = K.ps()
                    for k in range(16):
                        P.op('pe', 'matmul', pt[:, 0:TT], lhsT=wo_t[:, k, m * 128:(m + 1) * 128], rhs=gT[:, k, :],
                             start=(k == 0), stop=(k == 15), r=[wo_b, gTall_b], w=[pb], inc=(k == 15))
                    residual_update(l, pt, pb, m, tt, None)
            P.barrier()

        for l in range(nlayers):
            kind = l % 3
            if kind == 0:
                conv_layer(l, l // 3)
            elif kind == 1:
                na_layer(l)
            else:
                ssd_layer(l)

        fnw = VROW['final_norm_w']
        for tt in range(NTILE):
            sl = slice(tt * TT, (tt + 1) * TT)
            rs, rs_b = stat.next()
            if dbg:
                P.op('dve', 'memset', rs[:], 1.0, w=[rs_b])
            else:
                colsum_rstd(X[:, :, sl], Xb[tt], RMS_EPS, rs, rs_b)
            yt, yt_b = big.next()
            for k in range(NCH):
                if dbg:
                    P.op('dve', 'tensor_copy', out=yt[:, k, :], in_=X[:, k, sl],
                         r=[Xb[tt]], w=[yt_b])
                else:
                    P.op('dve', 'scalar_tensor_tensor',
                        out=yt[:, k, :], in0=X[:, k, sl], scalar=PV[:, k, fnw:fnw + 1], in1=rs[:], op0=ALU.mult,
                        op1=ALU.mult, r=[Xb[tt], rs_b, PV_b], ww=[yt_b])
            for q in range(TT // 128):
                ot, ot_b = big.next()
                ot2 = ot[:].rearrange("p c t -> p (c t)")[:, 0:D]
                for half in range(2):
                    pt, pb = K.ps()
                    for c4 in range(4):
                        c = half * 4 + c4
                        P.op('pe', 'transpose',
                            pt[:, c4 * 128:(c4 + 1) * 128], yt[:, c, q * 128:(q + 1) * 128], ident,
                            r=[yt_b, cst_b], w=[pb])
                    if half == 0:
                        P.op('act', 'activation', out=ot2[:, 0:512], in_=pt[:], func=AF.Copy,
                             r=[pb], w=[ot_b])
                    else:
                        P.op('dve', 'tensor_copy', out=ot2[:, 512:1024], in_=pt[:],
                             r=[pb], w=[ot_b])
                tok = tt * TT + q * 128
                dst = y_ctx[tok:tok + 128, :] if tok < 1024 else y_lat[tok - 1024:tok - 1024 + 128, :]
                P.dma('auto', 'dma_start', out=dst, in_=ot2, r=[ot_b])

        P.finish()
        P.emit()
        print("instructions:", P.ninst, {e: P.seq[e] for e in ENGS}, {e: P.dcount[e] for e in ENGS})
    return nc


def make_btab(rpb):
    ext = np.concatenate([rpb.reshape(16, -1), np.full((16, 1), -30000.0, np.float32)], axis=1)
    rs_row = lambda qr: min(max(qr - 4, 0), 24)
    idx = np.full((5, 5, 128, 128), 465, np.int64)
    kc = np.arange(64)[:, None]
    qc = np.arange(64)[None, :]
    cs = np.clip(qc - 8, 0, 48)
    colok = (kc >= cs) & (kc < cs + 16)
    dcol = np.clip(kc - qc, -15, 15) + 15
    for cls, r in enumerate((0, 2, 4, 28, 30)):
        ft = rs_row(r) // 2
        for jt in range(5):
            for a in range(2):
                for b in range(2):
                    kr = 2 * (ft + jt) + a
                    qr = r + b
                    if kr > 31 or not (rs_row(qr) <= kr < rs_row(qr) + 8):
                        continue
                    blk = np.where(colok, (kr - qr + 7) * 31 + dcol, 465)
                    idx[cls, jt, a * 64:(a + 1) * 64, b * 64:(b + 1) * 64] = blk
    return np.ascontiguousarray(ext[:, idx].transpose(1, 0, 2, 3, 4))


def make_in_maps(inp):
    g = lambda k: np.ascontiguousarray(np.asarray(inp[k], dtype=np.float32))
    vi = np.arange(128)[:, None]
    ii = np.arange(128)[None, :]
    cst = np.concatenate([np.eye(128, dtype=np.float32), np.ones((128, 128), np.float32), (vi <= ii).astype(np.float32),
                          (vi >= ii).astype(np.float32), (vi > ii).astype(np.float32), (vi < ii).astype(np.float32)], axis=1)
    ssc = np.concatenate([g('ssd_dt_bias')[0].reshape(-1), g('ssd_a_log')[0].reshape(-1), g('ssd_d')[0].reshape(-1)])
    ssc = np.ascontiguousarray(np.broadcast_to(ssc[None, :], (128, 160)))
    btab = make_btab(g('na_rpb')[0])
    maps = []
    for core in range(8):
        s = core // 4
        vt = np.zeros((128, D), np.float32)

        def put(name, arr):
            a = np.asarray(arr, np.float32).reshape(-1, D)
            vt[VROW[name]:VROW[name] + a.shape[0]] = a
        put('c_ctx', g('c_ctx'))
        put('c', g('c')[s])
        for l in range(4):
            put('ada_b%d' % l, g('ada_b')[l])
            put('norm_w%d' % l, g('norm_w')[l])
        put('final_norm_w', g('final_norm_w'))
        for j in range(2):
            put('conv_b_in%d' % j, g('conv_b_in')[j])
            put('conv_w_dw%d' % j, g('conv_w_dw')[j])
            put('conv_b_dw%d' % j, g('conv_b_dw')[j])
            put('conv_ln_w%d' % j, g('conv_ln_w')[j])
            put('conv_ln_b%d' % j, g('conv_ln_b')[j])
            put('conv_b_out%d' % j, g('conv_b_out')[j])
        put('ssd_w_conv', g('ssd_w_conv')[0])
        put('ssd_b_conv', g('ssd_b_conv')[0])
        put('ssd_norm_w', g('ssd_norm_w')[0])
        m = {
            "xp": g('x_prompt')[core * 4:(core + 1) * 4].reshape(1024, D),
            "xs": g('x_sample')[s],
            "vtab": vt,
            "cst": cst,
            "ada_w": g('ada_w'),
            "conv_w_in": g('conv_w_in'),
            "conv_w_out": g('conv_w_out'),
            "na_w_in": g('na_w_in')[0],
            "na_w_out": g('na_w_out')[0],
            "ck": g('cache_k')[s, 0].reshape(256, D),
            "cv": g('cache_v')[s, 0].reshape(256, D),
            "btab": btab,
            "ssd_w_in": g('ssd_w_in')[0],
            "ssd_w_out": g('ssd_w_out')[0],
            "ssc": ssc,
            "st_in": g('state_ssm')[s, 0].reshape(2 * 2048, 128),
        }
        maps.append(m)
    return maps


def run(inp, nlayers=4, dbg=False):
    nc = build(nlayers, dbg)
    maps = make_in_maps(inp)
    res = run_bass_kernel_spmd(nc, maps, core_ids=list(range(8)))
    return res.results


def kernel(**inp):
    rs = run(inp)
    y_prompt = np.concatenate([r["y_ctx"].reshape(4, 256, D) for r in rs], axis=0)
    y_sample = np.stack([rs[0]["y_lat"], rs[4]["y_lat"]], axis=0)
    new_k = np.concatenate([r["nk"].reshape(4, 1, 256, 16, 64) for r in rs], axis=0)
    new_v = np.concatenate([r["nv"].reshape(4, 1, 256, 16, 64) for r in rs], axis=0)
    new_s = np.concatenate([r["ns"].reshape(4, 1, 2, 32, 64, 128) for r in rs], axis=0)
    return (y_prompt, y_sample, new_k, new_v, new_s)
```
